# Optimizing a Trainium2 kernel written in Bass

```python
import math
import jax, jax.numpy as jnp
from jax import lax
import numpy as np

D_MODEL = 1024
BATCH = 8
SEQ = 4096
DEPTH = 2

HEAD_DIM = 64
A_Q_HEADS = 6
A_KV_HEADS = 2
A_WINDOW = 128
A_BLOCK = 128
S5_GROUP = 16
S5_CHANNELS = 256
S5_GROUPS = S5_CHANNELS // S5_GROUP
S5_STATE = 64
S5_MIN_NEG = -1e-4
C_Q_HEADS = 6
C_KV_HEADS = 2
N_BRANCH = 3
CMP_BLOCK = 32
CMP_STRIDE = 16
CMP_HIDDEN = 256
SEL_BLOCK = 64
SEL_TOPK = 16
SEL_QCHUNK = 64
C_WINDOW = 512
C_BLOCK = 128
FORCE_BONUS = 1e4
A_WIDTH = A_Q_HEADS * HEAD_DIM
A_KV_WIDTH = A_KV_HEADS * HEAD_DIM
C_WIDTH = C_Q_HEADS * HEAD_DIM
C_KV_WIDTH = C_KV_HEADS * HEAD_DIM
MIX_WIDTH = A_WIDTH + S5_CHANNELS + C_WIDTH
IN_SPLITS = (A_WIDTH, A_KV_WIDTH, A_KV_WIDTH, S5_CHANNELS, C_WIDTH,
             N_BRANCH * C_KV_WIDTH, N_BRANCH * C_KV_WIDTH, C_Q_HEADS * N_BRANCH)
IN_WIDTH = sum(IN_SPLITS)
D_FF = 4 * D_MODEL
NEG_INF = -1e30

kernel_name = "hymba_swa_s5_nsa_hybrid"


def rms_norm(x, gain, eps=1e-6):
    xf = x.astype(jnp.float32)
    y = xf * lax.rsqrt(jnp.mean(xf * xf, axis=-1, keepdims=True) + eps)
    return (y * gain.astype(jnp.float32)).astype(x.dtype)


def banded_gqa_attention(q, k, v, window, block, sinks=None):
    B, S, Hkv, G, hd = q.shape
    nq = S // block
    n_prev = -(-(window - 1) // block)
    pad = n_prev * block
    kb = jnp.pad(k, ((0, 0), (pad, 0), (0, 0), (0, 0))).reshape(B, nq + n_prev, block, Hkv, hd)
    vb = jnp.pad(v, ((0, 0), (pad, 0), (0, 0), (0, 0))).reshape(B, nq + n_prev, block, Hkv, hd)
    k_band = jnp.concatenate([kb[:, j:j + nq] for j in range(n_prev + 1)], axis=2)
    v_band = jnp.concatenate([vb[:, j:j + nq] for j in range(n_prev + 1)], axis=2)
    qb = q.reshape(B, nq, block, Hkv, G, hd)
    scores = jnp.einsum('bnqhgd,bnkhd->bhgnqk', qb, k_band).astype(jnp.float32) * (hd ** -0.5)
    qpos = jnp.arange(nq)[:, None] * block + jnp.arange(block)[None, :]
    kpos = (jnp.arange(nq)[:, None] - n_prev) * block + jnp.arange((n_prev + 1) * block)[None, :]
    diff = qpos[:, :, None] - kpos[:, None, :]
    mask = (diff >= 0) & (diff < window) & (kpos[:, None, :] >= 0)
    scores = jnp.where(mask, scores, NEG_INF)
    if sinks is None:
        p = jax.nn.softmax(scores, axis=-1)
    else:
        sink = sinks.astype(jnp.float32).reshape(Hkv, G)[None, :, :, None, None, None]
        m = jnp.maximum(jnp.max(scores, axis=-1, keepdims=True), sink)
        e = jnp.exp(scores - m)
        p = e / (jnp.sum(e, axis=-1, keepdims=True) + jnp.exp(sink - m))
    out = jnp.einsum('bhgnqk,bnkhd->bnqhgd', p.astype(v.dtype), v_band)
    return out.reshape(B, S, Hkv, G, hd)


def swa_sink_mixer(q, k, v, q_gain, k_gain, sinks):
    B, S, _ = q.shape
    G = A_Q_HEADS // A_KV_HEADS
    qh = rms_norm(q.reshape(B, S, A_KV_HEADS, G, HEAD_DIM), q_gain)
    kh = rms_norm(k.reshape(B, S, A_KV_HEADS, HEAD_DIM), k_gain)
    vh = v.reshape(B, S, A_KV_HEADS, HEAD_DIM)
    o = banded_gqa_attention(qh, kh, vh, A_WINDOW, A_BLOCK, sinks)
    return o.reshape(B, S, A_WIDTH)


def s5_mixer(u, a_re, a_im, log_step, b_re, b_im, c_re, c_im, d_skip, w_glu, b_glu):
    f32 = jnp.float32
    B, S, _ = u.shape
    ug = u.reshape(B, S, S5_GROUPS, S5_GROUP).astype(f32)
    lam_re = jnp.minimum(a_re.astype(f32), S5_MIN_NEG)
    lam_im = a_im.astype(f32)
    step = jnp.exp(log_step.astype(f32))[:, None]
    mag = jnp.exp(lam_re * step)
    abar_re = mag * jnp.cos(lam_im * step)
    abar_im = mag * jnp.sin(lam_im * step)
    den = lam_re * lam_re + lam_im * lam_im
    nr = abar_re - 1.0
    coef_re = (nr * lam_re + abar_im * lam_im) / den
    coef_im = (abar_im * lam_re - nr * lam_im) / den
    b_re = b_re.astype(f32); b_im = b_im.astype(f32)
    bb_re = coef_re[..., None] * b_re - coef_im[..., None] * b_im
    bb_im = coef_re[..., None] * b_im + coef_im[..., None] * b_re
    bu_re = jnp.einsum('bsgp,gnp->bsgn', ug, bb_re)
    bu_im = jnp.einsum('bsgp,gnp->bsgn', ug, bb_im)
    ar = jnp.broadcast_to(abar_re, bu_re.shape)
    ai = jnp.broadcast_to(abar_im, bu_im.shape)

    def combine(left, right):
        ar1, ai1, br1, bi1 = left
        ar2, ai2, br2, bi2 = right
        return (ar2 * ar1 - ai2 * ai1, ar2 * ai1 + ai2 * ar1,
                ar2 * br1 - ai2 * bi1 + br2, ar2 * bi1 + ai2 * br1 + bi2)

    _, _, h_re, h_im = lax.associative_scan(combine, (ar, ai, bu_re, bu_im), axis=1)
    y = (jnp.einsum('bsgn,gpn->bsgp', h_re, c_re.astype(f32))
         - jnp.einsum('bsgn,gpn->bsgp', h_im, c_im.astype(f32)))
    y = y + d_skip.astype(f32).reshape(S5_GROUPS, S5_GROUP) * ug
    hg = jax.nn.gelu(y.reshape(B, S, S5_CHANNELS))
    out = hg * jax.nn.sigmoid(hg @ w_glu.astype(f32) + b_glu.astype(f32))
    return out.astype(u.dtype)


def compress_blocks(x, pe, w1, b1, w2, b2):
    B, S, H, hd = x.shape
    n_cmp = (S - CMP_BLOCK) // CMP_STRIDE + 1
    idx = np.arange(n_cmp)[:, None] * CMP_STRIDE + np.arange(CMP_BLOCK)[None, :]
    blk = x[:, idx] + pe[:, None, :]
    flat = jnp.transpose(blk, (0, 1, 3, 2, 4)).reshape(B, n_cmp, H, CMP_BLOCK * hd)
    return jax.nn.gelu(flat @ w1 + b1) @ w2 + b2


def cmp_to_sel_matrix(n_cmp, n_sel):
    cs = np.arange(n_cmp)[:, None] * CMP_STRIDE
    ss = np.arange(n_sel)[None, :] * SEL_BLOCK
    cover = np.clip(np.minimum(cs + CMP_BLOCK, ss + SEL_BLOCK) - np.maximum(cs, ss), 0, None)
    return (cover / CMP_BLOCK).astype(np.float32)


def nsa_mixer(q, k_all, v_all, gate_logits, q_gain, k_gain, cmp_pe, cmp_w1, cmp_b1, cmp_w2, cmp_b2):
    f32 = jnp.float32
    B, S, _ = q.shape
    G = C_Q_HEADS // C_KV_HEADS
    Hkv = C_KV_HEADS
    scale = HEAD_DIM ** -0.5
    qh = rms_norm(q.reshape(B, S, Hkv, G, HEAD_DIM), q_gain)
    k_all = k_all.reshape(B, S, N_BRANCH, Hkv, HEAD_DIM)
    v_all = v_all.reshape(B, S, N_BRANCH, Hkv, HEAD_DIM)
    t = jnp.arange(S)

    k_cmp = rms_norm(compress_blocks(k_all[:, :, 0], cmp_pe[0], cmp_w1[0], cmp_b1[0], cmp_w2[0], cmp_b2[0]), k_gain[0])
    v_cmp = compress_blocks(v_all[:, :, 0], cmp_pe[1], cmp_w1[1], cmp_b1[1], cmp_w2[1], cmp_b2[1])
    n_cmp = k_cmp.shape[1]
    s_cmp = jnp.einsum('bshgd,bchd->bhgsc', qh, k_cmp).astype(f32) * scale
    cmp_end = jnp.arange(n_cmp) * CMP_STRIDE + CMP_BLOCK - 1
    cmp_ok = cmp_end[None, :] <= t[:, None]
    p_cmp = jax.nn.softmax(jnp.where(cmp_ok, s_cmp, NEG_INF), axis=-1)
    p_cmp = p_cmp * jnp.any(cmp_ok, axis=-1)[:, None].astype(f32)
    o_cmp = jnp.einsum('bhgsc,bchd->bshgd', p_cmp.astype(v_cmp.dtype), v_cmp)

    n_sel = S // SEL_BLOCK
    topk = min(SEL_TOPK, n_sel)
    imp = jnp.einsum('bhgsc,cj->bhsj', p_cmp, jnp.asarray(cmp_to_sel_matrix(n_cmp, n_sel)))
    qblk = (t // SEL_BLOCK)[:, None]
    j = jnp.arange(n_sel)[None, :]
    forced = (j == 0) | (j == qblk) | (j == qblk - 1)
    imp = jnp.where(forced, imp + FORCE_BONUS, imp)
    imp = jnp.where(j <= qblk, imp, NEG_INF)
    _, sel_idx = lax.top_k(imp, topk)

    k_sel = rms_norm(k_all[:, :, 1], k_gain[1])
    kb = jnp.transpose(k_sel.reshape(B, n_sel, SEL_BLOCK, Hkv, HEAD_DIM), (0, 3, 1, 2, 4))
    vb = jnp.transpose(v_all[:, :, 1].reshape(B, n_sel, SEL_BLOCK, Hkv, HEAD_DIM), (0, 3, 1, 2, 4))
    n_ch = S // SEL_QCHUNK
    q_ch = jnp.moveaxis(qh.reshape(B, n_ch, SEL_QCHUNK, Hkv, G, HEAD_DIM), 1, 0)
    idx_ch = jnp.moveaxis(sel_idx.reshape(B, Hkv, n_ch, SEL_QCHUNK, topk), 2, 0)
    pos_ch = t.reshape(n_ch, SEL_QCHUNK)
    bi = jnp.arange(B)[:, None, None, None]
    hi = jnp.arange(Hkv)[None, :, None, None]
    offs = jnp.arange(SEL_BLOCK)

    def sel_chunk(args):
        qc, idx_c, pos_c = args
        kg = kb[bi, hi, idx_c]
        vg = vb[bi, hi, idx_c]
        s = jnp.einsum('bqhgd,bhqkld->bhgqkl', qc, kg).astype(f32) * scale
        kpos = idx_c[..., None] * SEL_BLOCK + offs
        ok = kpos <= pos_c[None, None, :, None, None]
        s = jnp.where(ok[:, :, None], s, NEG_INF)
        shp = s.shape
        p = jax.nn.softmax(s.reshape(shp[:4] + (shp[4] * shp[5],)), axis=-1).reshape(shp)
        return jnp.einsum('bhgqkl,bhqkld->bqhgd', p.astype(vg.dtype), vg)

    o_slc = lax.map(sel_chunk, (q_ch, idx_ch, pos_ch))
    o_slc = jnp.moveaxis(o_slc, 0, 1).reshape(B, S, Hkv, G, HEAD_DIM)

    o_win = banded_gqa_attention(qh, rms_norm(k_all[:, :, 2], k_gain[2]), v_all[:, :, 2], C_WINDOW, C_BLOCK)

    g = jax.nn.sigmoid(gate_logits.astype(f32)).reshape(B, S, Hkv, G, N_BRANCH)
    o = g[..., 0:1] * o_cmp + g[..., 1:2] * o_slc + g[..., 2:3] * o_win
    return o.reshape(B, S, C_WIDTH).astype(q.dtype)


def setup_inputs(seed: int = 0) -> dict:
    key = jax.random.key(seed)
    ks = jax.random.split(key, 40)
    f32 = jnp.float32

    def nrm(k, shape, s):
        return jax.random.normal(k, shape, f32) * s

    L, D, N, P, Gs = DEPTH, D_MODEL, S5_STATE, S5_GROUP, S5_GROUPS
    a_im = jnp.broadcast_to(math.pi * jnp.arange(N, dtype=f32), (L, Gs, N)) + nrm(ks[10], (L, Gs, N), 0.01)
    return {
        "x": nrm(ks[0], (BATCH, SEQ, D), 1.0),
        "c": nrm(ks[1], (BATCH, D), 1.0),
        "norm1_g": 1.0 + nrm(ks[2], (L, D), 0.02),
        "norm2_g": 1.0 + nrm(ks[3], (L, D), 0.02),
        "w_ada": nrm(ks[4], (L, D, 6 * D), D ** -0.5),
        "b_ada": nrm(ks[5], (L, 6 * D), 0.02),
        "w_in": nrm(ks[6], (L, D, IN_WIDTH), D ** -0.5),
        "a_q_gain": 1.0 + nrm(ks[7], (L, HEAD_DIM), 0.02),
        "a_k_gain": 1.0 + nrm(ks[8], (L, HEAD_DIM), 0.02),
        "a_sinks": nrm(ks[9], (L, A_Q_HEADS), 0.5),
        "s5_a_re": -0.5 + nrm(ks[11], (L, Gs, N), 0.01),
        "s5_a_im": a_im,
        "s5_log_step": jax.random.uniform(ks[12], (L, Gs), f32, math.log(1e-3), math.log(1e-1)),
        "s5_b_re": nrm(ks[13], (L, Gs, N, P), (2 * P) ** -0.5),
        "s5_b_im": nrm(ks[14], (L, Gs, N, P), (2 * P) ** -0.5),
        "s5_c_re": nrm(ks[15], (L, Gs, P, N), (2 * N) ** -0.5),
        "s5_c_im": nrm(ks[16], (L, Gs, P, N), (2 * N) ** -0.5),
        "s5_d": nrm(ks[17], (L, S5_CHANNELS), 1.0),
        "s5_w_glu": nrm(ks[18], (L, S5_CHANNELS, S5_CHANNELS), S5_CHANNELS ** -0.5),
        "s5_b_glu": nrm(ks[19], (L, S5_CHANNELS), 0.02),
        "c_q_gain": 1.0 + nrm(ks[20], (L, HEAD_DIM), 0.02),
        "c_k_gain": 1.0 + nrm(ks[21], (L, N_BRANCH, HEAD_DIM), 0.02),
        "cmp_pe": nrm(ks[22], (L, 2, CMP_BLOCK, HEAD_DIM), 0.1),
        "cmp_w1": nrm(ks[23], (L, 2, CMP_BLOCK * HEAD_DIM, CMP_HIDDEN), (CMP_BLOCK * HEAD_DIM) ** -0.5),
        "cmp_b1": nrm(ks[24], (L, 2, CMP_HIDDEN), 0.02),
        "cmp_w2": nrm(ks[25], (L, 2, CMP_HIDDEN, HEAD_DIM), CMP_HIDDEN ** -0.5),
        "cmp_b2": nrm(ks[26], (L, 2, HEAD_DIM), 0.02),
        "out_norm_g": 1.0 + nrm(ks[27], (L, MIX_WIDTH), 0.02),
        "w_out": nrm(ks[28], (L, MIX_WIDTH, D), MIX_WIDTH ** -0.5),
        "w_ff1": nrm(ks[29], (L, D, D_FF), D ** -0.5),
        "w_ff2": nrm(ks[30], (L, D_FF, D), D_FF ** -0.5),
    }


def reference(x, c, norm1_g, norm2_g, w_ada, b_ada, w_in, a_q_gain, a_k_gain, a_sinks,
              s5_a_re, s5_a_im, s5_log_step, s5_b_re, s5_b_im, s5_c_re, s5_c_im, s5_d, s5_w_glu, s5_b_glu,
              c_q_gain, c_k_gain, cmp_pe, cmp_w1, cmp_b1, cmp_w2, cmp_b2,
              out_norm_g, w_out, w_ff1, w_ff2):
    offsets = [int(o) for o in np.cumsum(IN_SPLITS)[:-1]]
    a_end = A_WIDTH
    b_end = A_WIDTH + S5_CHANNELS
    for l in range(DEPTH):
        mod = jax.nn.silu(c) @ w_ada[l] + b_ada[l]
        sh1, sc1, ga1, sh2, sc2, ga2 = [m[:, None, :] for m in jnp.split(mod, 6, axis=-1)]

        h = rms_norm(x, norm1_g[l]) * (1 + sc1) + sh1
        proj = h @ w_in[l]
        aq, ak, av, su, cq, ck, cv, cg = jnp.split(proj, offsets, axis=-1)
        o_a = swa_sink_mixer(aq, ak, av, a_q_gain[l], a_k_gain[l], a_sinks[l])
        o_b = s5_mixer(su, s5_a_re[l], s5_a_im[l], s5_log_step[l], s5_b_re[l], s5_b_im[l],
                       s5_c_re[l], s5_c_im[l], s5_d[l], s5_w_glu[l], s5_b_glu[l])
        o_c = nsa_mixer(cq, ck, cv, cg, c_q_gain[l], c_k_gain[l],
                        cmp_pe[l], cmp_w1[l], cmp_b1[l], cmp_w2[l], cmp_b2[l])
        mix = jnp.concatenate([rms_norm(o_a, out_norm_g[l, :a_end]),
                               rms_norm(o_b, out_norm_g[l, a_end:b_end]),
                               rms_norm(o_c, out_norm_g[l, b_end:])], axis=-1)
        x = x + ga1 * (mix @ w_out[l])

        h = rms_norm(x, norm2_g[l]) * (1 + sc2) + sh2
        x = x + ga2 * (jnp.square(jax.nn.relu(h @ w_ff1[l])) @ w_ff2[l])
    return x
```

```python
import numpy as np
from contextlib import ExitStack
import concourse.bass as bass
import concourse.mybir as mybir
from concourse.bass_utils import run_bass_kernel_spmd

F32 = mybir.dt.float32
BF16 = mybir.dt.bfloat16
AF = mybir.ActivationFunctionType
ALU = mybir.AluOpType

ENGS = ['pe', 'act', 'dve', 'pool', 'sp']
S = 4096
D = 1024
NEGB = -30000.0


class Res:
    __slots__ = ('name', 'w', 'r', 'excl')

    def __init__(self, name='', excl=False):
        self.name = name
        self.w = None
        self.r = {}
        self.excl = excl


class _Rec:
    def __init__(self):
        self.call = None

    def __getattr__(self, name):
        def f(*a, **k):
            self.call = (name, a, k)
            return self
        return f


class Prog:
    def __init__(self, nc, n_dma_sems=4):
        self.nc = nc
        self.ops = {e: [] for e in ENGS}
        self.cnt = {e: 0 for e in ENGS}
        self.seen = {e: {} for e in ENGS}
        self.esem = {}
        self.dsem = {}
        self.dsem_val = {}
        self.dq_rr = {e: 0 for e in ENGS}
        self.n_dma_sems = n_dma_sems
        self.stopped = False
        self.epoch = 0

    def alloc_sems(self, stack):
        nc = self.nc
        self._stack = stack
        for e in ['pe', 'act', 'dve', 'pool']:
            self.esem[e] = stack.enter_context(nc.semaphore('s_%s_0' % e))
        for q in ['sp', 'act', 'pool']:
            self.dsem[q] = [stack.enter_context(nc.semaphore('d_%s_%d' % (q, i))) for i in range(self.n_dma_sems)]
            for i in range(self.n_dma_sems):
                self.dsem_val[(q, i)] = 0

    def _need(self, e, dep, waits):
        if dep is None:
            return
        if dep[0] == 'eng':
            _, f, idx, ep = dep
            if ep < self.epoch:
                return
            if f == e and e in ('pe', 'sp'):
                return
            key = ('eng', f)
            if self.seen[e].get(key, 0) >= idx:
                return
            waits[key] = max(waits.get(key, 0), idx)
        else:
            _, q, i, val = dep
            key = ('dma', q, i)
            if self.seen[e].get(key, 0) >= val:
                return
            waits[key] = max(waits.get(key, 0), val)

    def _collect(self, e, reads, writes):
        waits = {}
        for r in reads:
            self._need(e, r.w, waits)
            if r.excl:
                for k, d in r.r.items():
                    if k != ('eng', e):
                        self._need(e, d, waits)
        for w in writes:
            self._need(e, w.w, waits)
            for d in w.r.values():
                self._need(e, d, waits)
        for k, v in waits.items():
            self.seen[e][k] = v
        return waits

    def op(self, e, fn, reads=(), writes=()):
        if self.stopped:
            return
        waits = self._collect(e, reads, writes)
        self.cnt[e] += 1
        dep = ('eng', e, self.cnt[e], self.epoch)
        for r in reads:
            r.r[('eng', e)] = dep
        for w in writes:
            w.w = dep
            w.r = {}
        rec = _Rec()
        fn(rec)
        name, a, k = rec.call
        self.ops[e].append(([(self._semof(kk), v) for kk, v in waits.items()],
                            (lambda eng, name=name, a=a, k=k: getattr(eng, name)(*a, **k)), (self.esem[e], 1)))

    def dma(self, q, out, in_, reads=(), writes=(), **kw):
        if self.stopped:
            return
        waits = self._collect(q, reads, writes)
        i = self.dq_rr[q]
        self.dq_rr[q] = (i + 1) % self.n_dma_sems
        prev = self.dsem_val[(q, i)]
        key = ('dma', q, i)
        if prev > 0 and self.seen[q].get(key, 0) < prev:
            waits[key] = prev
            self.seen[q][key] = prev
        val = prev + 16
        self.dsem_val[(q, i)] = val
        dep = ('dma', q, i, val)
        for r in reads:
            r.r[('dma', q, i)] = dep
        for w in writes:
            w.w = dep
            w.r = {}
        self.ops[q].append(([(self._semof(kk), v) for kk, v in waits.items()],
                            lambda eng: eng.dma_start(out=out, in_=in_, **kw), (self.dsem[q][i], 16)))

    def barrier(self, engines=ENGS, final=False):
        if self.stopped:
            return
        for e in engines:
            waits = {}
            for f in ['pe', 'act', 'dve', 'pool']:
                if self.cnt[f] > 0 and f != e and self.seen[e].get(('eng', f), 0) < self.cnt[f]:
                    waits[('eng', f)] = self.cnt[f]
            for (q, i), v in self.dsem_val.items():
                if v > 0 and self.seen[e].get(('dma', q, i), 0) < v:
                    waits[('dma', q, i)] = v
            for k, v in waits.items():
                self.seen[e][k] = v
            self.ops[e].append(([(self._semof(kk), v) for kk, v in waits.items()], None, None))
        if final:
            return
        self.epoch += 1
        for e in ['pe', 'act', 'dve', 'pool']:
            self.esem[e] = self._stack.enter_context(self.nc.semaphore('s_%s_%d' % (e, self.epoch)))
            self.cnt[e] = 0
        for e in ENGS:
            for k in [k for k in self.seen[e] if k[0] == 'eng']:
                del self.seen[e][k]

    def _semof(self, key):
        if key[0] == 'eng':
            return self.esem[key[1]]
        return self.dsem[key[1]][key[2]]

    def replay(self, block):
        engmap = {'pe': 'tensor', 'act': 'scalar', 'dve': 'vector', 'pool': 'gpsimd', 'sp': 'sync'}

        def mk(e):
            def body(eng):
                for waits, fn, inc in self.ops[e]:
                    for sem, v in waits:
                        eng.wait_ge(sem, v)
                    if fn is None:
                        continue
                    inst = fn(eng)
                    if inc is not None:
                        inst.then_inc(inc[0], inc[1])
            return body
        for e in ENGS:
            getattr(block, engmap[e])(mk(e))


def _col(v):
    return np.ascontiguousarray(v.reshape(-1, 128).T)


def _win_perm():
    idx = []
    for base in (0,):
        for g in range(3):
            for hk in range(2):
                idx += list(range(base + hk * 192 + g * 64, base + hk * 192 + g * 64 + 64))
    idx += list(range(384, 512))
    for g in range(3):
        for hk in range(2):
            idx += list(range(896 + hk * 192 + g * 64, 896 + hk * 192 + g * 64 + 64))
    idx += list(range(1280, 1664))
    idx += list(range(640, 896))
    idx += list(range(1664, 1792))
    idx += list(range(512, 640))
    idx += list(range(1792, 2066))
    assert len(idx) == 2066 and len(set(idx)) == 2066
    return np.array(idx)


def _consts():
    c = {}
    ko = np.arange(128)[:, None]
    qo = np.arange(128)[None, :]
    caus = np.where(ko <= qo, 0.0, NEGB).astype(np.float32)
    anti = np.where(ko > qo, 0.0, NEGB).astype(np.float32)
    rel = np.arange(128)[:, None]
    cbrel = np.where(16 * (rel - 96) + 15 <= qo, 0.0, NEGB).astype(np.float32)
    c['caus3'] = np.tile(caus, (1, 3))
    c['anti3'] = np.tile(anti, (1, 3))
    c['cbrel3'] = np.tile(cbrel, (1, 3))
    iw2 = np.zeros((128, 384), np.float32)
    iw2[np.arange(128), np.arange(128) + 128] = 1.0
    c['iw2'] = iw2
    ew = np.zeros((128, 4096), np.float32)
    ew[np.arange(4096) // 64, np.arange(4096)] = 1.0
    c['ewide'] = ew
    c['ident'] = np.eye(128, dtype=np.float32)
    blk = np.zeros((128, 128), np.float32)
    blk[:64, :64] = 1.0
    blk[64:, 64:] = 1.0
    c['blk64'] = blk
    q = np.arange(128)[:, None]
    r = np.arange(128)[None, :]
    relj = r - 62
    qb = (q >= 64).astype(np.int64)
    fb = np.zeros((128, 128), np.float32)
    fb[(relj == qb) | (relj == qb - 1)] = 1e4
    fb[relj > qb] = -1e30
    c['fbwide'] = fb
    n_cmp = 255
    cs = np.arange(n_cmp)[:, None] * 16
    ss = np.arange(64)[None, :] * 64
    cover = (np.clip(np.minimum(cs + 32, ss + 64) - np.maximum(cs, ss), 0, None) / 32).astype(np.float32)
    va = np.zeros((256, 65), np.float32)
    va[1:, 0] = 1.0
    va[1:, 1:] = cover
    c['vaugc'] = np.ascontiguousarray(va.reshape(2, 128, 65).transpose(1, 0, 2))
    return c


def _prep_shared(inp):
    f = lambda a: np.ascontiguousarray(np.asarray(a, dtype=np.float32))
    L = 2
    o = {}
    o['w_ada'] = f(inp['w_ada'])
    o['b_ada'] = f(inp['b_ada'])
    o['n1col'] = f(np.stack([_col(inp['norm1_g'][l]) for l in range(L)]))
    o['n2col'] = f(np.stack([_col(inp['norm2_g'][l]) for l in range(L)]))
    perm = _win_perm()
    o['w_in'] = f(inp['w_in'][:, :, perm])
    o['w_out'] = f(inp['w_out'])
    o['onorm'] = f(inp['out_norm_g'])
    o['onormcol'] = f(np.stack([_col(inp['out_norm_g'][l]) for l in range(L)]))
    g = []
    for l in range(L):
        cols = [inp['a_q_gain'][l], inp['a_k_gain'][l], inp['c_q_gain'][l],
                inp['c_k_gain'][l, 0], inp['c_k_gain'][l, 1], inp['c_k_gain'][l, 2]]
        g.append(np.stack([np.tile(np.asarray(v), 2) for v in cols], axis=1))
    o['gains'] = f(np.stack(g))
    o['sinks'] = f(inp['a_sinks'])

    def st(a):
        a = np.asarray(a).reshape(L, 8, 2, 64)
        return f(a.transpose(0, 2, 3, 1).reshape(L, 128, 8))
    o['s5_are'] = st(inp['s5_a_re'])
    o['s5_aim'] = st(inp['s5_a_im'])
    ls = np.broadcast_to(np.asarray(inp['s5_log_step'])[:, :, None], (L, 16, 64))
    o['s5_ls'] = st(ls)

    def stb(a):
        a = np.asarray(a).reshape(L, 8, 2, 64, 16)
        return f(a.transpose(0, 2, 3, 1, 4).reshape(L, 128, 8, 16))

    def stc(a):
        a = np.asarray(a).reshape(L, 8, 2, 16, 64)
        return f(a.transpose(0, 2, 4, 1, 3).reshape(L, 128, 8, 16))
    o['s5_bre'] = stb(inp['s5_b_re'])
    o['s5_bim'] = stb(inp['s5_b_im'])
    o['s5_cre'] = stc(inp['s5_c_re'])
    o['s5_cim'] = stc(inp['s5_c_im'])
    o['s5_dcol'] = f(np.stack([_col(inp['s5_d'][l]) for l in range(L)]))
    o['s5_wglu'] = f(inp['s5_w_glu'])
    o['s5_bglucol'] = f(np.stack([_col(inp['s5_b_glu'][l]) for l in range(L)]))
    pe = np.asarray(inp['cmp_pe'])
    o['cmp_peT'] = f(np.tile(pe.transpose(0, 1, 3, 2), (1, 1, 2, 1)))
    w1 = np.asarray(inp['cmp_w1']).reshape(L, 2, 32, 64, 256)
    o['cmp_w1'] = f(np.tile(w1.transpose(0, 1, 3, 2, 4), (1, 1, 2, 1, 1)))
    o['cmp_b1col'] = f(np.asarray(inp['cmp_b1']).reshape(L, 2, 2, 128).transpose(0, 3, 1, 2))
    o['cmp_w2'] = f(np.asarray(inp['cmp_w2']).reshape(L, 2, 2, 128, 64).transpose(0, 3, 1, 2, 4))
    o['cmp_b2kcol'] = f(np.tile(np.asarray(inp['cmp_b2'])[:, 0, :], (1, 2))[:, :, None])
    o['cmp_b2v'] = f(np.tile(np.asarray(inp['cmp_b2'])[:, 1, :], (1, 2)))
    o['w_ff1'] = f(inp['w_ff1'])
    o['w_ff2'] = f(inp['w_ff2'])
    o.update(_consts())
    return o


class KB:
    def __init__(self, nc, P, st):
        self.nc, self.P, self.st = nc, P, st
        self.bank_rr = 0
        self.uid = 0

    def sb(self, st, name, shape, dt):
        self.uid += 1
        nb = int(np.prod(shape[1:])) * (2 if dt == BF16 else 4)
        if not hasattr(self, 'log'):
            self.log = []
        self.log.append((name, nb))
        try:
            t = st.enter_context(self.nc.sbuf_tensor('%s_%d' % (name, self.uid), shape, dt))
        except AssertionError:
            for n_, b_ in self.log:
                print(n_, b_)
            raise
        return t, Res(name)


class StopBuild(Exception):
    pass


def build(shared_shapes, depth=2, dbg=False, n_macro=8, do_ffn=True, stage=99):
    nc = bass.Bass("TRN2", target_bir_lowering=False)
    dram = {}
    for k, shp in shared_shapes.items():
        dram[k] = nc.dram_tensor(k, list(shp), F32, kind="ExternalInput").ap()
    x_d = nc.dram_tensor("x", [S, D], F32, kind="ExternalInput").ap()
    ccol_d = nc.dram_tensor("ccol", [128, 8], F32, kind="ExternalInput").ap()
    out_d = nc.dram_tensor("out", [S, D], F32, kind="ExternalOutput").ap()
    mod_d = nc.dram_tensor("mod_scr", [2, 6144], F32, kind="Internal").ap()
    dbg_d = {}
    if dbg:
        for name, shp in [('mod', [2, 6144]), ('hT', [128, 8 * 512]), ('oa', [S, 384]), ('oc', [S, 384]),
                          ('obT', [256, S]), ('mixT', [128, 8 * 128]), ('proj', [128, 13 * 512]),
                          ('imp', [S, 128]), ('kcmp', [128, 256]), ('vcmp', [128, 2 * 2 * 129]),
                          ('s5y', [256, S]), ('t1', [128, 2 * 8 * 128]), ('avd', [128, 32 * 130])]:
            dbg_d[name] = nc.dram_tensor("dbg_" + name, shp, F32, kind="ExternalOutput").ap()

    with ExitStack() as st:
        P = Prog(nc)
        P.alloc_sems(st)
        K = KB(nc, P, st)
        nc_ = nc
        banks = []
        for b in range(8):
            t = st.enter_context(nc.psum_tensor('psb%d' % b, [128, 512], F32))
            banks.append((t, Res('bank%d' % b, excl=True)))
        pool_banks = [0, 1, 2, 7]

        def bank():
            b = pool_banks[K.bank_rr % len(pool_banks)]
            K.bank_rr += 1
            return banks[b]
        ACC_A, ACC_CMP, ACC_SEL, ACC_WIN = banks[3], banks[4], banks[5], banks[6]

        dres = {k: Res('d_' + k) for k in list(dram.keys()) + ['x', 'ccol', 'mod']}
        out_res = [Res('out%d' % t) for t in range(32)]
        dbg_res = Res('dbg')

        cst = ExitStack()
        st.enter_context(cst)
        ident_f, r_ident_f = K.sb(cst, 'ident_f', [128, 128], F32)
        ident_b, r_ident_b = K.sb(cst, 'ident_b', [128, 128], BF16)
        blk64, r_blk64 = K.sb(cst, 'blk64', [128, 128], BF16)
        ones_b, r_ones_b = K.sb(cst, 'ones_b', [128, 128], BF16)
        caus3, r_caus3 = K.sb(cst, 'caus3', [128, 384], BF16)
        anti3, r_anti3 = K.sb(cst, 'anti3', [128, 384], BF16)
        cbrel3, r_cbrel3 = K.sb(cst, 'cbrel3', [128, 384], BF16)
        iw2, r_iw2 = K.sb(cst, 'iw2', [128, 384], BF16)
        ewide, r_ewide = K.sb(cst, 'ewide', [128, 4096], BF16)
        fbwide, r_fbwide = K.sb(cst, 'fbwide', [128, 128], F32)
        eps_t, r_eps = K.sb(cst, 'eps_t', [128, 1], F32)
        mhalf, r_mhalf = K.sb(cst, 'mhalf', [128, 8], F32)
        P.dma('sp', ident_f[:], dram['ident'], writes=[r_ident_f])
        P.dma('sp', fbwide[:], dram['fbwide'], writes=[r_fbwide])
        for (t, r, nm) in [(ident_b, r_ident_b, 'ident'), (blk64, r_blk64, 'blk64'), (caus3, r_caus3, 'caus3'),
                           (anti3, r_anti3, 'anti3'), (cbrel3, r_cbrel3, 'cbrel3'), (iw2, r_iw2, 'iw2'),
                           (ewide, r_ewide, 'ewide')]:
            if nm == 'ewide':
                for cc in range(2):
                    P.dma('pool', t[:, cc * 2048:(cc + 1) * 2048], dram[nm][:, cc * 2048:(cc + 1) * 2048], writes=[r])
            else:
                P.dma('pool', t[:], dram[nm], writes=[r])
        P.op('dve', lambda e: e.memset(ones_b[:], 1.0), writes=[r_ones_b])
        P.op('dve', lambda e: e.memset(eps_t[:], 1e-6), writes=[r_eps])
        P.op('dve', lambda e: e.memset(mhalf[:], -0.5), writes=[r_mhalf])

        def rstd_small(st_, src_ap, src_res, n, inv_n, name):
            ms, r_ms = K.sb(st_, name + '_ms', [128, n], F32)
            rs, r_rs = K.sb(st_, name + '_rs', [128, n], F32)
            return ms, r_ms, rs, r_rs

        def emit_rstd(src_ap, src_res, ms, r_ms, rs, r_rs, n, inv_n):
            P.op('dve', lambda e: e.tensor_scalar(out=ms[:, 0:n], in0=src_ap, scalar1=inv_n, scalar2=1e-6,
                                                  op0=ALU.mult, op1=ALU.add), reads=[src_res], writes=[r_ms])
            P.op('pool', lambda e: e.tensor_tensor(out=rs[:, 0:n], in0=ms[:, 0:n], in1=mhalf[:, 0:n], op=ALU.pow),
                 reads=[r_ms, r_mhalf], writes=[r_rs])

        def stop_at(k):
            if stage <= k:
                P.stopped = True

        def body():
            with ExitStack() as ph:
                ccol, r_ccol = K.sb(ph, 'ccol', [128, 8], F32)
                cs, r_cs = K.sb(ph, 'cs', [128, 8], F32)
                wb = [K.sb(ph, 'wada%d' % i, [128, 3072], F32) for i in range(2)]
                row, r_row = K.sb(ph, 'row', [1, 3072], F32)
                brow, r_brow = K.sb(ph, 'brow', [1, 3072], F32)
                P.dma('sp', ccol[:], ccol_d, writes=[r_ccol])
                P.op('act', lambda e: e.activation(out=cs[:], in_=ccol[:], func=AF.Silu), reads=[r_ccol], writes=[r_cs])
                it = 0
                for l in range(depth):
                    for half in range(2):
                        for kc in range(8):
                            wt, r_wt = wb[it % 2]
                            P.dma('sp' if it % 2 == 0 else 'act', wt[:],
                                  dram['w_ada'][l, kc * 128:(kc + 1) * 128, half * 3072:(half + 1) * 3072], writes=[r_wt])
                            for n in range(6):
                                pst, r_ps = banks[n]
                                P.op('pe', lambda e, pst=pst, wt=wt, kc=kc, n=n: e.matmul(
                                    pst[0:1, :], lhsT=cs[:, kc:kc + 1], rhs=wt[:, n * 512:(n + 1) * 512],
                                    start=(kc == 0), stop=(kc == 7)), reads=[r_cs, r_wt], writes=[r_ps])
                            it += 1
                        P.dma('sp', brow[:], dram['b_ada'][l:l + 1, half * 3072:(half + 1) * 3072], writes=[r_brow])
                        for n in range(6):
                            pst, r_ps = banks[n]
                            P.op('dve', lambda e, pst=pst, n=n: e.tensor_tensor(
                                out=row[0:1, n * 512:(n + 1) * 512], in0=pst[0:1, :], in1=brow[0:1, n * 512:(n + 1) * 512],
                                op=ALU.add), reads=[r_ps, r_brow], writes=[r_row])
                        P.dma('sp', mod_d[l:l + 1, half * 3072:(half + 1) * 3072], row[:], reads=[r_row], writes=[dres['mod']])
                if dbg:
                    P.barrier()
                    mt, r_mt = K.sb(ph, 'modt', [2, 6144], F32)
                    P.dma('sp', mt[0:depth, :], mod_d[0:depth, :], reads=[dres['mod']], writes=[r_mt])
                    P.dma('sp', dbg_d['mod'][0:depth, :], mt[0:depth, :], reads=[r_mt], writes=[dbg_res])
            P.barrier()

            stop_at(0)
            for l in range(depth):
                src_d = x_d if l == 0 else out_d
                with ExitStack() as ph:
                    sbp = lambda name, shape, dt: K.sb(ph, name, shape, dt)
                    s5 = s5_setup(nc, P, K, ph, dram, l, banks, bank, ident_f, r_ident_f, dbg_d if (dbg and l == 0) else None, dbg_res)
                    stop_at(1)
                    modcol, r_modcol = sbp('modcol', [128, 48], F32)
                    n1col, r_n1col = sbp('n1col', [128, 8], F32)
                    s1col, r_s1col = sbp('s1col', [128, 8], F32)
                    ga_bc, r_ga = sbp('ga_bc', [128, 1024], F32)
                    oncol, r_oncol = sbp('oncol', [128, 8], F32)
                    gains, r_gains = sbp('gains', [128, 6], F32)
                    esink, r_esink = sbp('esink', [128, 6], F32)
                    P.dma('sp', modcol[:], mod_d[l].rearrange("(k p) -> p k", p=128), reads=[dres['mod']], writes=[r_modcol], allow_slow_non_contiguous=True)
                    P.dma('sp', n1col[:], dram['n1col'][l], writes=[r_n1col])
                    P.dma('sp', ga_bc[:], mod_d[l, 2048:3072].partition_broadcast(128), reads=[dres['mod']], writes=[r_ga])
                    P.dma('sp', oncol[:], dram['onormcol'][l], writes=[r_oncol])
                    P.dma('sp', gains[:], dram['gains'][l], writes=[r_gains])
                    P.dma('sp', esink[:], dram['sinks'][l].partition_broadcast(128), writes=[r_esink])
                    P.op('act', lambda e: e.activation(out=esink[:], in_=esink[:], func=AF.Exp), reads=[r_esink], writes=[r_esink])
                    P.op('dve', lambda e: e.scalar_tensor_tensor(out=s1col[:], in0=modcol[:, 8:16], scalar=1.0, in1=n1col[:],
                                                                 op0=ALU.add, op1=ALU.mult), reads=[r_modcol, r_n1col], writes=[r_s1col])
                    win, r_win = sbp('win', [128, 8, 2066], BF16)
                    wout, r_wout = sbp('wout', [128, 8, 1024], BF16)
                    for kc in range(8):
                        for (a, b) in ((0, 1033), (1033, 2066)):
                            P.dma('pool', win[:, kc, a:b], dram['w_in'][l, kc * 128:(kc + 1) * 128, a:b], writes=[r_win])
                        P.dma('pool', wout[:, kc, :], dram['w_out'][l, kc * 128:(kc + 1) * 128, :], writes=[r_wout])

                    akT, r_akT = sbp('akT', [128, S], BF16)
                    ck1T, r_ck1T = sbp('ck1T', [128, S], BF16)
                    ck2T, r_ck2T = sbp('ck2T', [128, 8, 128], BF16)
                    av, r_av = sbp('av', [128, 32, 2, 66], BF16)
                    cv1, r_cv1 = sbp('cv1', [128, 32, 2, 66], BF16)
                    cv2, r_cv2 = sbp('cv2', [128, 8, 2, 66], BF16)
                    kcmpT, r_kcmpT = sbp('kcmpT', [128, 256], BF16)
                    vcmp, r_vcmp = sbp('vcmp', [128, 2, 2, 130], BF16)
                    k0T, r_k0T = sbp('k0T', [128, 528], BF16)
                    v0T, r_v0T = sbp('v0T', [128, 528], BF16)
                    P.op('pool', lambda e: e.memset(av[:, :, :, 64:65], 1.0), writes=[r_av])
                    P.op('pool', lambda e: e.memset(cv1[:, :, :, 64:65], 1.0), writes=[r_cv1])
                    P.op('pool', lambda e: e.memset(cv2[:, :, :, 64:65], 1.0), writes=[r_cv2])
                    P.op('pool', lambda e: e.memset(k0T[:, 0:16], 0.0), writes=[r_k0T])
                    P.op('pool', lambda e: e.memset(v0T[:, 0:16], 0.0), writes=[r_v0T])
                    for h in range(2):
                        P.dma('pool', vcmp[:, :, h, 64:129], dram['vaugc'], writes=[r_vcmp])

                    w1buf, r_w1buf = sbp('w1buf', [128, 16, 256], BF16)
                    peT, r_peT = sbp('peT', [128, 2, 34], BF16)
                    b1col, r_b1col = sbp('b1col', [128, 2, 2], F32)
                    bias1, r_bias1 = sbp('bias1', [128, 2, 2], F32)
                    w2c, r_w2c = sbp('w2c', [128, 2, 2, 64], BF16)
                    b2kcol, r_b2k = sbp('b2kcol', [128, 1], F32)
                    b2v_bc, r_b2v = sbp('b2v_bc', [128, 128], F32)
                    P.op('pool', lambda e: e.memset(peT[:], 0.0), writes=[r_peT])
                    P.dma('pool', peT[:, :, 0:32], dram['cmp_peT'][l].rearrange("k p l -> p k l"), writes=[r_peT])
                    P.dma('sp', b1col[:], dram['cmp_b1col'][l], writes=[r_b1col])
                    P.dma('pool', w2c[:], dram['cmp_w2'][l], writes=[r_w2c])
                    P.dma('sp', b2kcol[:], dram['cmp_b2kcol'][l], writes=[r_b2k])
                    P.dma('sp', b2v_bc[:], dram['cmp_b2v'][l].partition_broadcast(128), writes=[r_b2v])

                    xin = [sbp('xin%d' % i, [128, 1024], F32) for i in range(2)]
                    xh, r_xh = sbp('xh', [128, 1024], BF16)
                    ss1, r_ss1 = sbp('ss1', [128, 1], F32)
                    ms1, r_ms1 = sbp('ms1', [128, 1], F32)
                    rs1, r_rs1 = sbp('rs1', [128, 1], F32)
                    hT, r_hT = sbp('hT', [128, 8, 512], BF16)
                    aqT, r_aqT = sbp('aqT', [128, 3, 512], BF16)
                    cqT, r_cqT = sbp('cqT', [128, 3, 512], BF16)
                    suT, r_suT = sbp('suT', [128, 2, 512], BF16)
                    sg, r_sg = sbp('sg', [128, 4, 18], F32)
                    nsq, r_nsq = sbp('nsq', [128, 512], BF16)
                    nrs, r_nrs = sbp('nrs', [128, 512], F32)
                    kraw, r_kraw = sbp('kraw', [128, 32], F32)
                    hidT, r_hidT = sbp('hidT', [128, 2, 2, 32], BF16)
                    mixBT, r_mixBT = sbp('mixBT', [128, 2, 512], BF16)

                    def qknorm(src_ap, src_res, n, gcol, dst_ap, dst_res, three_d=False):
                        v = (lambda a: a.rearrange("p (a b) -> p a b", a=4)) if three_d else (lambda a: a)
                        P.op('act', lambda e: e.activation(out=v(nsq[:, 0:n]), in_=src_ap, func=AF.Square), reads=[src_res], writes=[r_nsq])
                        pst, r_ps = bank()
                        P.op('pe', lambda e: e.matmul(pst[:, 0:n], lhsT=blk64[:], rhs=nsq[:, 0:n], start=True, stop=True),
                             reads=[r_blk64, r_nsq], writes=[r_ps])
                        P.op('act', lambda e: e.activation(out=nrs[:, 0:n], in_=pst[:, 0:n], func=AF.Sqrt, bias=eps_t[:], scale=1.0 / 64),
                             reads=[r_ps, r_eps], writes=[r_nrs])
                        P.op('dve', lambda e: e.reciprocal(out=nrs[:, 0:n], in_=nrs[:, 0:n]), reads=[r_nrs], writes=[r_nrs])
                        P.op('dve', lambda e: e.scalar_tensor_tensor(out=dst_ap, in0=src_ap, scalar=gcol, in1=v(nrs[:, 0:n]),
                                                                     op0=ALU.mult, op1=ALU.mult),
                             reads=[src_res, r_gains, r_nrs], writes=[dst_res])

                    act_dve = [0]

                    def norm_transpose(t, scol, shcol, r_cols, dstT, r_dstT, col0):
                        xt, r_xt = xin[t % 2]
                        P.dma('sp', xt[:], src_d[t * 128:(t + 1) * 128, :], reads=[out_res[t]], writes=[r_xt])
                        P.op('act', lambda e: e.activation(out=xh[:], in_=xt[:], func=AF.Square, accum_out=ss1[:]),
                             reads=[r_xt], writes=[r_xh, r_ss1])
                        emit_rstd(ss1[:, 0:1], r_ss1, ms1, r_ms1, rs1, r_rs1, 1, 1.0 / 1024)
                        P.op('dve', lambda e: e.tensor_scalar(out=xh[:], in0=xt[:], scalar1=rs1[:, 0:1], scalar2=None, op0=ALU.mult),
                             reads=[r_xt, r_rs1], writes=[r_xh])
                        pst, r_ps = bank()
                        psb = pst[:].bitcast(BF16)
                        for kc in range(8):
                            P.op('pe', lambda e, kc=kc: e.transpose(out=psb[:, kc * 128:(kc + 1) * 128], in_=xh[:, kc * 128:(kc + 1) * 128],
                                                                    identity=ident_b[:]), reads=[r_xh, r_ident_b], writes=[r_ps])
                        for kc in range(8):
                            eng = 'act' if (kc % 2 == 0) else 'dve'
                            if eng == 'act':
                                P.op('act', lambda e, kc=kc: e.activation(out=dstT[:, kc, col0:col0 + 128], in_=psb[:, kc * 128:(kc + 1) * 128],
                                                                          func=AF.Identity, bias=shcol[:, kc:kc + 1], scale=scol[:, kc:kc + 1]),
                                     reads=[r_ps] + r_cols, writes=[r_dstT])
                            else:
                                P.op('dve', lambda e, kc=kc: e.tensor_scalar(out=dstT[:, kc, col0:col0 + 128], in0=psb[:, kc * 128:(kc + 1) * 128],
                                                                             scalar1=scol[:, kc:kc + 1], scalar2=shcol[:, kc:kc + 1],
                                                                             op0=ALU.mult, op1=ALU.add),
                                     reads=[r_ps] + r_cols, writes=[r_dstT])

                    pT = [sbp('pT%d' % i, [128, 384], BF16) for i in range(4)]
                    pT_rr = [0]
                    o_a, r_oa = sbp('o_a', [128, 6, 64], F32)
                    o_c, r_oc = sbp('o_c', [128, 6, 64], F32)
                    tmp3, r_tmp3 = sbp('tmp3', [128, 3, 64], F32)
                    den3, r_den3 = sbp('den3', [128, 3], F32)
                    rc3, r_rc3 = sbp('rc3', [128, 3], F32)
                    w3, r_w3 = sbp('w3', [128, 3], F32)
                    impf, r_impf = sbp('impf', [128, 64], F32)
                    imp2, r_imp2 = sbp('imp2', [128, 64], F32)
                    m8a, r_m8a = sbp('m8a', [128, 8], F32)
                    m8b, r_m8b = sbp('m8b', [128, 8], F32)
                    selb, r_selb = sbp('selb', [128, 64], BF16)
                    selbT, r_selbT = sbp('selbT', [128, 3, 128], BF16)
                    P.op('pool', lambda e: e.memset(selbT[:], 0.0), writes=[r_selbT])
                    ssn, r_ssn = sbp('ssn', [128, 1], F32)
                    msn, r_msn = sbp('msn', [128, 1], F32)
                    rsn, r_rsn = sbp('rsn', [128, 1], F32)
                    mixn, r_mixn = sbp('mixn', [128, 384], BF16)
                    mixT, r_mixT = sbp('mixT', [128, 8, 128], BF16)
                    xres = xin

                    def attn(h, q_ap, r_q, keys, vfn, r_v, acc, ncols):
                        acct, r_acc = acc
                        n = len(keys)
                        staged = []

                        def score(j):
                            kap, kres, nk, bias = keys[j]
                            pst, r_ps = bank()
                            P.op('pe', lambda e: e.matmul(pst[0:nk, 0:384], lhsT=kap, rhs=q_ap, start=True, stop=(bias is None)),
                                 reads=[kres, r_q], writes=[r_ps])
                            if bias is not None:
                                if bias[0] == 'id':
                                    P.op('pe', lambda e: e.matmul(pst[0:nk, 0:384], lhsT=ident_b[0:nk, 0:nk], rhs=bias[1][0:nk, :], start=False, stop=True),
                                         reads=[r_ident_b, bias[2]], writes=[r_ps])
                                elif bias[0] == 'cmp':
                                    sh = bias[1]
                                    P.op('pe', lambda e: e.matmul(pst[0:nk, 0:384], lhsT=iw2[:, sh:sh + nk], rhs=cbrel3[:], start=False, stop=True),
                                         reads=[r_iw2, r_cbrel3], writes=[r_ps])
                                else:
                                    kt = bias[1]
                                    P.op('pe', lambda e: e.matmul(pst[0:nk, 0:384], lhsT=ewide[:, kt * 128:(kt + 1) * 128],
                                                                  rhs=selbT[:].rearrange("p g q -> p (g q)"), start=False, stop=True),
                                         reads=[r_ewide, r_selbT], writes=[r_ps])
                            pt, r_pt = pT[pT_rr[0] % 4]
                            pT_rr[0] += 1
                            P.op('act', lambda e: e.activation(out=pt[0:nk, :], in_=pst[0:nk, 0:384], func=AF.Exp, scale=0.125),
                                 reads=[r_ps], writes=[r_pt])
                            staged.append((pt, r_pt, nk))

                        def pv(j):
                            pt, r_pt, nk = staged[j]
                            vap = vfn(j)
                            for g in range(3):
                                P.op('pe', lambda e, g=g: e.matmul(acct[:, g * ncols:(g + 1) * ncols], lhsT=pt[0:nk, g * 128:(g + 1) * 128], rhs=vap,
                                                                   start=(j == 0 and g == 0), stop=(j == n - 1), skip_group_check=True),
                                     reads=[r_pt, r_v], writes=[r_acc])
                        for j in range(n):
                            score(j)
                            if j >= 1:
                                pv(j - 1)
                        pv(n - 1)

                    stop_at(2)
                    for m in range(n_macro):
                        for r in range(4):
                            norm_transpose(4 * m + r, s1col, modcol[:, 0:8], [r_s1col, r_modcol], hT, r_hT, r * 128)
                        stop_at(3)
                        for ci in range(13):
                            pst, r_ps = bank()
                            for kc in range(8):
                                P.op('pe', lambda e, kc=kc, ci=ci, pst=pst: e.matmul(pst[:, :], lhsT=win[:, kc, ci * 128:(ci + 1) * 128], rhs=hT[:, kc, :],
                                                                                     start=(kc == 0), stop=(kc == 7)), reads=[r_win, r_hT], writes=[r_ps])
                            cols = slice(m * 512, (m + 1) * 512)
                            if ci < 3:
                                qknorm(pst[:, :], r_ps, 512, gains[:, 0:1], aqT[:, ci, :], r_aqT)
                            elif ci == 3:
                                qknorm(pst[:, :], r_ps, 512, gains[:, 1:2], akT[:, cols], r_akT)
                            elif ci < 7:
                                qknorm(pst[:, :], r_ps, 512, gains[:, 2:3], cqT[:, ci - 4, :], r_cqT)
                            elif ci == 7:
                                P.op('act', lambda e, pst=pst: e.copy(out=k0T[:, 16:528], in_=pst[:, :]), reads=[r_ps], writes=[r_k0T])
                            elif ci == 8:
                                qknorm(pst[:, :], r_ps, 512, gains[:, 4:5], ck1T[:, cols], r_ck1T)
                            elif ci == 9:
                                qknorm(pst[:, :], r_ps, 512, gains[:, 5:6], ck2T[:, (4 * m) % 8:(4 * m) % 8 + 4, :].rearrange("p a b -> p (a b)"), r_ck2T)
                            elif ci < 12:
                                P.op('act', lambda e, pst=pst, ci=ci: e.copy(out=suT[:, ci - 10, :].rearrange("p (t k) -> p k t", t=8), in_=pst[:, :].rearrange("p (k t) -> p k t", t=8)), reads=[r_ps], writes=[r_suT])
                            else:
                                P.op('dve', lambda e, pst=pst: e.tensor_copy(out=v0T[:, 16:528], in_=pst[:, :]), reads=[r_ps], writes=[r_v0T])
                        stop_at(4)
                        import os as _os
                        for r in range(int(_os.environ.get('RSKIP', 0)), int(_os.environ.get('RLIM', 4))):
                            t = 4 * m + r
                            psa, r_psa = bank()
                            psb_, r_psb = bank()
                            for kc in range(8):
                                P.op('pe', lambda e, kc=kc, psa=psa, r=r: e.matmul(psa[:, 0:128], lhsT=hT[:, kc, r * 128:(r + 1) * 128], rhs=win[:, kc, 1664:1792],
                                                                                   start=(kc == 0), stop=(kc == 7)), reads=[r_win, r_hT], writes=[r_psa])
                            for kc in range(8):
                                P.op('pe', lambda e, kc=kc, psb_=psb_, r=r: e.matmul(psb_[:, 0:274], lhsT=hT[:, kc, r * 128:(r + 1) * 128], rhs=win[:, kc, 1792:2066],
                                                                                     start=(kc == 0), stop=(kc == 7)), reads=[r_win, r_hT], writes=[r_psb])
                            stop_at(4.2)
                            if not _os.environ.get('NOAV'):
                              P.op('act', lambda e, psa=psa, t=t: e.copy(out=av[:, t, :, 0:64], in_=psa[:, 0:128].rearrange("p (h d) -> p h d", d=64)),
                                 reads=[r_psa], writes=[r_av])
                            stop_at(4.4)
                            if not _os.environ.get('NOCV1'):
                              P.op('dve', lambda e, psb_=psb_, t=t: e.tensor_copy(out=cv1[:, t, :, 0:64], in_=psb_[:, 0:128].rearrange("p (h d) -> p h d", h=2)),
                                 reads=[r_psb], writes=[r_cv1])
                            if not _os.environ.get('NOCV2'):
                              P.op('dve', lambda e, psb_=psb_, t=t: e.tensor_copy(out=cv2[:, t % 8, :, 0:64], in_=psb_[:, 128:256].rearrange("p (h d) -> p h d", h=2)),
                                 reads=[r_psb], writes=[r_cv2])
                            stop_at(4.6)
                            if _os.environ.get('NOGATE'):
                                continue
                            P.op('act', lambda e, psb_=psb_, r=r: e.activation(out=sg[:, r, :], in_=psb_[:, 256:274], func=AF.Exp, scale=-1.0),
                                 reads=[r_psb], writes=[r_sg])
                            P.op('dve', lambda e, r=r: e.tensor_scalar(out=sg[:, r, :], in0=sg[:, r, :], scalar1=1.0, scalar2=None, op0=ALU.add),
                                 reads=[r_sg], writes=[r_sg])
                            P.op('dve', lambda e, r=r: e.reciprocal(out=sg[:, r, :], in_=sg[:, r, :]), reads=[r_sg], writes=[r_sg])
                            stop_at(4.9)
                        stop_at(5)
                        pb = 32 * (m % 4)
                        tl = m // 4
                        for kv in range(2):
                            srcT, r_src = (k0T, r_k0T) if kv == 0 else (v0T, r_v0T)
                            psHs = [bank(), bank()]
                            if m == 0:
                                psB, r_psB = bank()
                            for half in range(2):
                                for lq in range(4):
                                    l0 = half * 16 + lq * 4
                                    P.dma('pool', w1buf[:, lq * 4:(lq + 1) * 4, :], dram['cmp_w1'][l, kv, :, l0:l0 + 4, :], writes=[r_w1buf])
                                stop_at(5.1)
                                if m == 0:
                                    for ht in range(2):
                                        for lq in range(16):
                                            L_ = half * 16 + lq
                                            P.op('pe', lambda e, psB=psB, lq=lq, ht=ht, kv=kv, L_=L_, half=half: e.matmul(
                                                psB[:, ht * 2:ht * 2 + 2], lhsT=w1buf[0:64, lq, ht * 128:(ht + 1) * 128], rhs=peT[0:64, kv, L_:L_ + 2],
                                                start=(half == 0 and ht == 0 and lq == 0), stop=(half == 1 and lq == 15), skip_group_check=True),
                                                reads=[r_w1buf, r_peT], writes=[r_psB])
                                stop_at(5.2)
                                for hh in range(2):
                                    for ht in range(2):
                                        grp = hh * 2 + ht
                                        for lq in range(16):
                                            L_ = half * 16 + lq
                                            psH, r_psH = psHs[hh]
                                            P.op('pe', lambda e, psH=psH, lq=lq, ht=ht, hh=hh, srcT=srcT, grp=grp, L_=L_, half=half: e.matmul(
                                                psH[:, ht * 32:(ht + 1) * 32], lhsT=w1buf[hh * 64:(hh + 1) * 64, lq, ht * 128:(ht + 1) * 128],
                                                rhs=srcT[hh * 64:(hh + 1) * 64, L_:L_ + 497:16],
                                                start=(half == 0 and ht == 0 and lq == 0), stop=(half == 1 and lq == 15), skip_group_check=True),
                                                reads=[r_w1buf, r_src], writes=[r_psH])
                            stop_at(5.3)
                            if m == 0:
                                P.op('dve', lambda e, psB=psB, kv=kv: e.tensor_tensor(out=bias1[:, kv, :], in0=psB[:, 0:4:2], in1=b1col[:, kv, :], op=ALU.add),
                                     reads=[r_psB, r_b1col], writes=[r_bias1])
                            stop_at(5.4)
                            for hh in range(2):
                                for ht in range(2):
                                    grp = hh * 2 + ht
                                    psH, r_psH = psHs[hh]
                                    P.op('act', lambda e, psH=psH, hh=hh, ht=ht, kv=kv, grp=grp: e.activation(
                                        out=hidT[:, hh, ht, :], in_=psH[:, ht * 32:(ht + 1) * 32], func=AF.Gelu_apprx_tanh,
                                        bias=bias1[:, kv, ht:ht + 1], scale=1.0), reads=[r_psH, r_bias1], writes=[r_hidT])
                            stop_at(5.5)
                            if kv == 0:
                                pst, r_ps = bank()
                                for hh in range(2):
                                    for ht in range(2):
                                        P.op('pe', lambda e, pst=pst, hh=hh, ht=ht: e.matmul(pst[hh * 64:(hh + 1) * 64, 0:32], lhsT=w2c[:, 0, ht, :], rhs=hidT[:, hh, ht, :],
                                                                                            start=(ht == 0), stop=(ht == 1)), reads=[r_w2c, r_hidT], writes=[r_ps])
                                P.op('dve', lambda e, pst=pst: e.tensor_scalar(out=kraw[:], in0=pst[:, 0:32], scalar1=b2kcol[:, 0:1], scalar2=None, op0=ALU.add),
                                     reads=[r_ps, r_b2k], writes=[r_kraw])
                                stop_at(5.6)
                                qknorm(kraw[:], r_kraw, 32, gains[:, 3:4], kcmpT[:, 32 * m:32 * m + 32], r_kcmpT)
                                stop_at(5.7)
                            else:
                                pst, r_ps = bank()
                                for hh in range(2):
                                    for ht in range(2):
                                        P.op('pe', lambda e, pst=pst, hh=hh, ht=ht: e.matmul(pst[pb:pb + 32, hh * 64:(hh + 1) * 64], lhsT=hidT[:, hh, ht, :], rhs=w2c[:, 1, ht, :],
                                                                                            start=(ht == 0), stop=(ht == 1), tile_position=(0, pb)), reads=[r_w2c, r_hidT], writes=[r_ps])
                                P.op('dve', lambda e, pst=pst: e.tensor_tensor(out=vcmp[pb:pb + 32, tl, :, 0:64],
                                                                               in0=pst[pb:pb + 32, 0:128].rearrange("p (h d) -> p h d", d=64),
                                                                               in1=b2v_bc[pb:pb + 32, :].rearrange("p (h d) -> p h d", d=64), op=ALU.add),
                                     reads=[r_ps, r_b2v], writes=[r_vcmp])
                                if m == 0:
                                    P.op('dve', lambda e: e.memset(vcmp[0:1, 0, :, 0:64], 0.0), writes=[r_vcmp])
                        P.op('pool', lambda e: e.tensor_copy(out=k0T[:, 0:16], in_=k0T[:, 512:528]), reads=[r_k0T], writes=[r_k0T])
                        P.op('pool', lambda e: e.tensor_copy(out=v0T[:, 0:16], in_=v0T[:, 512:528]), reads=[r_v0T], writes=[r_v0T])

                        stop_at(6)
                        s5_macro(nc, P, K, s5, suT, r_suT, mixBT, r_mixBT, oncol, r_oncol, banks, bank, ones_b, r_ones_b, eps_t, r_eps, m,
                                 dbg_d if (dbg and l == 0) else None, dbg_res)

                        stop_at(7)
                        for r in range(4):
                            i = 4 * m + r
                            qs = slice(r * 128, (r + 1) * 128)
                            for h in range(2):
                                hs = slice(h * 64, (h + 1) * 64)
                                aq_ap = aqT[hs, :, qs]
                                cq_ap = cqT[hs, :, qs]
                                nsl = 32 * ((8 * i + 8 + 31) // 32)
                                Tf = (8 * i + 7) // 128
                                keys = []
                                for T in range(Tf + 1):
                                    nk = min(128, nsl - 128 * T)
                                    bias = ('cmp', 224 - 8 * i + 128 * T) if T == Tf else None
                                    keys.append((kcmpT[hs, T * 128:T * 128 + nk], r_kcmpT, nk, bias))
                                attn(h, cq_ap, r_cqT, keys, lambda j, keys=keys, h=h: vcmp[0:keys[j][2], j, h, 0:129], r_vcmp, ACC_CMP, 129)
                                acc, r_acc = ACC_CMP
                                a3 = acc[:, 0:387].rearrange("p (g c) -> p g c", g=3)
                                P.op('dve', lambda e, a3=a3: e.tensor_scalar(out=den3[:], in0=a3[:, :, 64], scalar1=1e-30, scalar2=None, op0=ALU.add),
                                     reads=[r_acc], writes=[r_den3])
                                P.op('dve', lambda e: e.reciprocal(out=rc3[:], in_=den3[:]), reads=[r_den3], writes=[r_rc3])
                                for g in range(3):
                                    if g == 0:
                                        P.op('dve', lambda e, a3=a3: e.tensor_scalar(out=impf[:], in0=a3[:, 0, 65:129], scalar1=rc3[:, 0:1], scalar2=None, op0=ALU.mult),
                                             reads=[r_acc, r_rc3], writes=[r_impf])
                                    else:
                                        P.op('dve', lambda e, a3=a3, g=g: e.scalar_tensor_tensor(out=impf[:], in0=a3[:, g, 65:129], scalar=rc3[:, g:g + 1], in1=impf[:],
                                                                                                 op0=ALU.mult, op1=ALU.add), reads=[r_acc, r_rc3, r_impf], writes=[r_impf])
                                gi = h * 9
                                P.op('dve', lambda e, r=r, gi=gi: e.tensor_tensor(out=w3[:], in0=sg[:, r, gi:gi + 9:3], in1=rc3[:], op=ALU.mult),
                                     reads=[r_sg, r_rc3], writes=[r_w3])
                                P.op('dve', lambda e, a3=a3, h=h: e.tensor_tensor(out=o_c[:, h * 3:(h + 1) * 3, :], in0=a3[:, :, 0:64],
                                                                                  in1=w3[:].unsqueeze(2).to_broadcast([128, 3, 64]), op=ALU.mult),
                                     reads=[r_acc, r_w3], writes=[r_oc])
                                P.op('dve', lambda e, i=i: e.tensor_tensor(out=impf[:], in0=impf[:], in1=fbwide[:, 62 - 2 * i:126 - 2 * i], op=ALU.add),
                                     reads=[r_impf, r_fbwide], writes=[r_impf])
                                P.op('dve', lambda e: e.tensor_scalar(out=impf[:, 0:1], in0=impf[:, 0:1], scalar1=1e4, scalar2=None, op0=ALU.add),
                                     reads=[r_impf], writes=[r_impf])
                                if dbg and l == 0:
                                    P.dma('sp', dbg_d['imp'][i * 128:(i + 1) * 128, h * 64:(h + 1) * 64], impf[:], reads=[r_impf], writes=[dbg_res])
                                P.op('dve', lambda e: e.max(out=m8a[:], in_=impf[:]), reads=[r_impf], writes=[r_m8a])
                                P.op('dve', lambda e: e.match_replace(out=imp2[:], in_to_replace=m8a[:], in_values=impf[:], imm_value=-3e38),
                                     reads=[r_m8a, r_impf], writes=[r_imp2])
                                P.op('dve', lambda e: e.max(out=m8b[:], in_=imp2[:]), reads=[r_imp2], writes=[r_m8b])
                                P.op('dve', lambda e: e.tensor_scalar(out=imp2[:], in0=impf[:], scalar1=m8b[:, 7:8], scalar2=None, op0=ALU.is_ge),
                                     reads=[r_impf, r_m8b], writes=[r_imp2])
                                P.op('dve', lambda e: e.tensor_scalar(out=selb[:], in0=imp2[:], scalar1=-NEGB, scalar2=NEGB, op0=ALU.mult, op1=ALU.add),
                                     reads=[r_imp2], writes=[r_selb])
                                pst, r_ps = bank()
                                psb = pst[:].bitcast(BF16)
                                P.op('pe', lambda e, psb=psb: e.transpose(out=psb[0:64, 0:128], in_=selb[:], identity=ident_b[:]),
                                     reads=[r_selb, r_ident_b], writes=[r_ps])
                                P.op('act', lambda e, psb=psb: e.copy(out=selbT[0:64, :, :], in_=psb[0:64, 0:128].unsqueeze(1).to_broadcast([64, 3, 128])),
                                     reads=[r_ps], writes=[r_selbT])
                                keys = []
                                if i > 0:
                                    keys.append((akT[hs, (i - 1) * 128:i * 128], r_akT, 128, ('id', anti3, r_anti3)))
                                keys.append((akT[hs, i * 128:(i + 1) * 128], r_akT, 128, ('id', caus3, r_caus3)))
                                base = i - (len(keys) - 1)
                                attn(h, aq_ap, r_aqT, keys, lambda j, base=base, h=h: av[:, base + j, h, 0:65], r_av, ACC_A, 65)
                                acc, r_acc = ACC_A
                                a3 = acc[:, 0:195].rearrange("p (g c) -> p g c", g=3)
                                P.op('dve', lambda e, a3=a3, h=h: e.tensor_tensor(out=den3[:], in0=a3[:, :, 64], in1=esink[:, h * 3:(h + 1) * 3], op=ALU.add),
                                     reads=[r_acc, r_esink], writes=[r_den3])
                                P.op('dve', lambda e: e.reciprocal(out=rc3[:], in_=den3[:]), reads=[r_den3], writes=[r_rc3])
                                P.op('dve', lambda e, a3=a3, h=h: e.tensor_tensor(out=o_a[:, h * 3:(h + 1) * 3, :], in0=a3[:, :, 0:64],
                                                                                  in1=rc3[:].unsqueeze(2).to_broadcast([128, 3, 64]), op=ALU.mult),
                                     reads=[r_acc, r_rc3], writes=[r_oa])
                                keys = []
                                lo = max(0, i - 4)
                                for kt in range(lo, i + 1):
                                    if kt == i:
                                        bias = ('id', caus3, r_caus3)
                                    elif kt == i - 4:
                                        bias = ('id', anti3, r_anti3)
                                    else:
                                        bias = None
                                    keys.append((ck2T[hs, kt % 8, :], r_ck2T, 128, bias))
                                attn(h, cq_ap, r_cqT, keys, lambda j, lo=lo, h=h: cv2[:, (lo + j) % 8, h, 0:65], r_cv2, ACC_WIN, 65)
                                for (accb, br) in ((ACC_WIN, 2),):
                                    acc, r_acc = accb
                                    a3 = acc[:, 0:195].rearrange("p (g c) -> p g c", g=3)
                                    P.op('dve', lambda e, a3=a3: e.reciprocal(out=rc3[:], in_=a3[:, :, 64]), reads=[r_acc], writes=[r_rc3])
                                    P.op('dve', lambda e, r=r, gi=gi, br=br: e.tensor_tensor(out=w3[:], in0=sg[:, r, gi + br:gi + 9:3], in1=rc3[:], op=ALU.mult),
                                         reads=[r_sg, r_rc3], writes=[r_w3])
                                    P.op('dve', lambda e, a3=a3: e.tensor_tensor(out=tmp3[:], in0=a3[:, :, 0:64], in1=w3[:].unsqueeze(2).to_broadcast([128, 3, 64]), op=ALU.mult),
                                         reads=[r_acc, r_w3], writes=[r_tmp3])
                                    P.op('pool', lambda e, h=h: e.tensor_tensor(out=o_c[:, h * 3:(h + 1) * 3, :], in0=o_c[:, h * 3:(h + 1) * 3, :], in1=tmp3[:], op=ALU.add),
                                         reads=[r_tmp3, r_oc], writes=[r_oc])
                                keys = []
                                for kt in range(0, i + 1):
                                    bias = ('id', caus3, r_caus3) if kt == i else ('sel', kt)
                                    keys.append((ck1T[hs, kt * 128:(kt + 1) * 128], r_ck1T, 128, bias))
                                attn(h, cq_ap, r_cqT, keys, lambda j, h=h: cv1[:, j, h, 0:65], r_cv1, ACC_SEL, 65)
                                acc, r_acc = ACC_SEL
                                a3 = acc[:, 0:195].rearrange("p (g c) -> p g c", g=3)
                                P.op('dve', lambda e, a3=a3: e.reciprocal(out=rc3[:], in_=a3[:, :, 64]), reads=[r_acc], writes=[r_rc3])
                                P.op('dve', lambda e, r=r, gi=gi: e.tensor_tensor(out=w3[:], in0=sg[:, r, gi + 1:gi + 9:3], in1=rc3[:], op=ALU.mult),
                                     reads=[r_sg, r_rc3], writes=[r_w3])
                                P.op('dve', lambda e, a3=a3: e.tensor_tensor(out=tmp3[:], in0=a3[:, :, 0:64], in1=w3[:].unsqueeze(2).to_broadcast([128, 3, 64]), op=ALU.mult),
                                     reads=[r_acc, r_w3], writes=[r_tmp3])
                                P.op('pool', lambda e, h=h: e.tensor_tensor(out=o_c[:, h * 3:(h + 1) * 3, :], in0=o_c[:, h * 3:(h + 1) * 3, :], in1=tmp3[:], op=ALU.add),
                                     reads=[r_tmp3, r_oc], writes=[r_oc])
                            if dbg and l == 0:
                                P.dma('sp', dbg_d['oa'][i * 128:(i + 1) * 128, :], o_a[:].rearrange("p a b -> p (a b)"), reads=[r_oa], writes=[dbg_res])
                                P.dma('sp', dbg_d['oc'][i * 128:(i + 1) * 128, :], o_c[:].rearrange("p a b -> p (a b)"), reads=[r_oc], writes=[dbg_res])
                            for (ot, r_ot, c0, kc0) in ((o_a, r_oa, 0, 0), (o_c, r_oc, 640, 5)):
                                of = ot[:].rearrange("p a b -> p (a b)")
                                P.op('act', lambda e, of=of: e.activation(out=mixn[:], in_=of, func=AF.Square, accum_out=ssn[:]),
                                     reads=[r_ot], writes=[r_mixn, r_ssn])
                                emit_rstd(ssn[:, 0:1], r_ssn, msn, r_msn, rsn, r_rsn, 1, 1.0 / 384)
                                P.op('dve', lambda e, of=of: e.tensor_scalar(out=mixn[:], in0=of, scalar1=rsn[:, 0:1], scalar2=None, op0=ALU.mult),
                                     reads=[r_ot, r_rsn], writes=[r_mixn])
                                pst, r_ps = bank()
                                psb = pst[:].bitcast(BF16)
                                for k3 in range(3):
                                    P.op('pe', lambda e, psb=psb, k3=k3: e.transpose(out=psb[:, k3 * 128:(k3 + 1) * 128], in_=mixn[:, k3 * 128:(k3 + 1) * 128], identity=ident_b[:]),
                                         reads=[r_mixn, r_ident_b], writes=[r_ps])
                                for k3 in range(3):
                                    P.op('act', lambda e, psb=psb, kc0=kc0, k3=k3: e.activation(out=mixT[:, kc0 + k3, :], in_=psb[:, k3 * 128:(k3 + 1) * 128], func=AF.Identity,
                                                                                               scale=oncol[:, kc0 + k3:kc0 + k3 + 1]),
                                         reads=[r_ps, r_oncol], writes=[r_mixT])
                            po = [bank(), bank()]
                            for n in range(2):
                                pst, r_ps = po[n]
                                for kc in range(8):
                                    if kc in (3, 4):
                                        lhs = mixBT[:, kc - 3, qs]
                                        rl = r_mixBT
                                    else:
                                        lhs = mixT[:, kc, :]
                                        rl = r_mixT
                                    P.op('pe', lambda e, pst=pst, lhs=lhs, kc=kc, n=n: e.matmul(pst[:, :], lhsT=lhs, rhs=wout[:, kc, n * 512:(n + 1) * 512],
                                                                                              start=(kc == 0), stop=(kc == 7)), reads=[rl, r_wout], writes=[r_ps])
                            xr, r_xr = xres[i % 2]
                            P.dma('sp', xr[:], src_d[i * 128:(i + 1) * 128, :], reads=[out_res[i]], writes=[r_xr])
                            for n in range(2):
                                pst, r_ps = po[n]
                                P.op('dve', lambda e, pst=pst, n=n: e.tensor_tensor(out=pst[:, :], in0=pst[:, :], in1=ga_bc[:, n * 512:(n + 1) * 512], op=ALU.mult),
                                     reads=[r_ps, r_ga], writes=[r_ps])
                                P.op('dve', lambda e, pst=pst, n=n, xr=xr: e.tensor_tensor(out=xr[:, n * 512:(n + 1) * 512], in0=pst[:, :], in1=xr[:, n * 512:(n + 1) * 512], op=ALU.add),
                                     reads=[r_ps, r_xr], writes=[r_xr])
                            P.dma('sp', out_d[i * 128:(i + 1) * 128, :], xr[:], reads=[r_xr], writes=[out_res[i]])
                        if m % 2 == 1 or m == n_macro - 1:
                            P.barrier()

                if do_ffn:
                    with ExitStack() as ph:
                        sbp = lambda name, shape, dt: K.sb(ph, name, shape, dt)
                        w1s, r_w1s = sbp('w1s', [128, 8, 4096], BF16)
                        w2s, r_w2s = sbp('w2s', [128, 32, 1024], BF16)
                        for kc in range(8):
                            for cc in range(2):
                                P.dma('pool', w1s[:, kc, cc * 2048:(cc + 1) * 2048], dram['w_ff1'][l, kc * 128:(kc + 1) * 128, cc * 2048:(cc + 1) * 2048], writes=[r_w1s])
                        for kc in range(32):
                            P.dma('pool', w2s[:, kc, :], dram['w_ff2'][l, kc * 128:(kc + 1) * 128, :], writes=[r_w2s])
                        modcol, r_modcol = sbp('modcol', [128, 48], F32)
                        n2col, r_n2col = sbp('n2col', [128, 8], F32)
                        s2col, r_s2col = sbp('s2col', [128, 8], F32)
                        ga_bc, r_ga = sbp('ga_bc', [128, 1024], F32)
                        P.dma('sp', modcol[:], mod_d[l].rearrange("(k p) -> p k", p=128), reads=[dres['mod']], writes=[r_modcol], allow_slow_non_contiguous=True)
                        P.dma('sp', n2col[:], dram['n2col'][l], writes=[r_n2col])
                        P.dma('sp', ga_bc[:], mod_d[l, 5120:6144].partition_broadcast(128), reads=[dres['mod']], writes=[r_ga])
                        P.op('dve', lambda e: e.scalar_tensor_tensor(out=s2col[:], in0=modcol[:, 32:40], scalar=1.0, in1=n2col[:],
                                                                     op0=ALU.add, op1=ALU.mult), reads=[r_modcol, r_n2col], writes=[r_s2col])
                        xin = [sbp('fxin%d' % i, [128, 1024], F32) for i in range(4)]
                        junk, r_junk = sbp('fjunk', [128, 1024], BF16)
                        xh, r_xh = sbp('fxh', [128, 1024], BF16)
                        ss1, r_ss1 = sbp('fss1', [128, 1], F32)
                        ms1, r_ms1 = sbp('fms1', [128, 1], F32)
                        rs1, r_rs1 = sbp('frs1', [128, 1], F32)
                        h2T, r_h2T = sbp('h2T', [128, 8, 256], BF16)
                        rl = [sbp('rl%d' % i, [128, 512], F32) for i in range(2)]
                        hid, r_hid = sbp('hid', [128, 32, 256], BF16)
                        xnew = [sbp('fxnew%d' % i, [128, 1024], F32) for i in range(2)]
                        for tb in range(16):
                            for r in range(2):
                                t = 2 * tb + r
                                xt, r_xt = xin[t % 4]
                                P.dma('sp', xt[:], out_d[t * 128:(t + 1) * 128, :], reads=[out_res[t]], writes=[r_xt])
                                P.op('act', lambda e, xt=xt: e.activation(out=junk[:], in_=xt[:], func=AF.Square, accum_out=ss1[:]),
                                     reads=[r_xt], writes=[r_junk, r_ss1])
                                emit_rstd(ss1[:, 0:1], r_ss1, ms1, r_ms1, rs1, r_rs1, 1, 1.0 / 1024)
                                P.op('dve', lambda e, xt=xt: e.tensor_scalar(out=xh[:], in0=xt[:], scalar1=rs1[:, 0:1], scalar2=None, op0=ALU.mult),
                                     reads=[r_xt, r_rs1], writes=[r_xh])
                                pst, r_ps = bank()
                                psb = pst[:].bitcast(BF16)
                                for kc in range(8):
                                    P.op('pe', lambda e, kc=kc, psb=psb: e.transpose(out=psb[:, kc * 128:(kc + 1) * 128], in_=xh[:, kc * 128:(kc + 1) * 128],
                                                                                    identity=ident_b[:]), reads=[r_xh, r_ident_b], writes=[r_ps])
                                for kc in range(8):
                                    if kc % 2 == 0:
                                        P.op('act', lambda e, kc=kc, psb=psb, r=r: e.activation(out=h2T[:, kc, r * 128:(r + 1) * 128], in_=psb[:, kc * 128:(kc + 1) * 128],
                                                                                               func=AF.Identity, bias=modcol[:, 24 + kc:25 + kc], scale=s2col[:, kc:kc + 1]),
                                             reads=[r_ps, r_modcol, r_s2col], writes=[r_h2T])
                                    else:
                                        P.op('dve', lambda e, kc=kc, psb=psb, r=r: e.tensor_scalar(out=h2T[:, kc, r * 128:(r + 1) * 128], in0=psb[:, kc * 128:(kc + 1) * 128],
                                                                                                  scalar1=s2col[:, kc:kc + 1], scalar2=modcol[:, 24 + kc:25 + kc],
                                                                                                  op0=ALU.mult, op1=ALU.add),
                                             reads=[r_ps, r_modcol, r_s2col], writes=[r_h2T])
                            for fp in range(16):
                                pst, r_ps = bank()
                                for j in range(2):
                                    f = 2 * fp + j
                                    for kc in range(8):
                                        P.op('pe', lambda e, pst=pst, j=j, f=f, kc=kc: e.matmul(pst[:, j * 256:(j + 1) * 256], lhsT=w1s[:, kc, f * 128:(f + 1) * 128], rhs=h2T[:, kc, :],
                                                                                              start=(kc == 0), stop=(kc == 7)), reads=[r_w1s, r_h2T], writes=[r_ps])
                                rt, r_rt = rl[fp % 2]
                                P.op('act', lambda e, pst=pst, rt=rt: e.activation(out=rt[:], in_=pst[:, :], func=AF.Relu), reads=[r_ps], writes=[r_rt])
                                P.op('dve', lambda e, rt=rt, fp=fp: e.tensor_tensor(out=hid[:, 2 * fp:2 * fp + 2, :], in0=rt[:].rearrange("p (j t) -> p j t", j=2),
                                                                                    in1=rt[:].rearrange("p (j t) -> p j t", j=2), op=ALU.mult),
                                     reads=[r_rt], writes=[r_hid])
                            for r in range(2):
                                t = 2 * tb + r
                                xt, r_xt = xin[t % 4]
                                xn, r_xn = xnew[t % 2]
                                po = [banks[4 + 2 * r], banks[5 + 2 * r]]
                                for n in range(2):
                                    pst, r_ps = po[n]
                                    for f in range(32):
                                        P.op('pe', lambda e, pst=pst, f=f, n=n, r=r: e.matmul(pst[:, :], lhsT=hid[:, f, r * 128:(r + 1) * 128], rhs=w2s[:, f, n * 512:(n + 1) * 512],
                                                                                            start=(f == 0), stop=(f == 31)), reads=[r_hid, r_w2s], writes=[r_ps])
                                for n in range(2):
                                    pst, r_ps = po[n]
                                    P.op('dve', lambda e, pst=pst, n=n, xn=xn: e.tensor_tensor(out=xn[:, n * 512:(n + 1) * 512], in0=pst[:, :], in1=ga_bc[:, n * 512:(n + 1) * 512], op=ALU.mult),
                                         reads=[r_ps, r_ga], writes=[r_xn])
                                P.op('pool', lambda e, xn=xn, xt=xt: e.tensor_tensor(out=xn[:], in0=xn[:], in1=xt[:], op=ALU.add), reads=[r_xn, r_xt], writes=[r_xn])
                                P.dma('sp', out_d[t * 128:(t + 1) * 128, :], xn[:], reads=[r_xn], writes=[out_res[t]])
                    P.barrier()

        body()
        P.stopped = False
        P.barrier(final=True)
        with nc.Block() as block:
            P.replay(block)
    return nc


TWO_PI = 6.283185307179586


def s5_setup(nc, P, K, ph, dram, l, banks, bank, ident_f, r_ident_f, dbg_d, dbg_res):
    sbp = lambda name, shape, dt: K.sb(ph, name, shape, dt)
    s5 = {}
    T1, r_T1 = sbp('T1', [128, 2, 8, 128], BF16)
    T2re, r_T2re = sbp('T2re', [128, 2, 8, 128], BF16)
    T2im, r_T2im = sbp('T2im', [128, 2, 8, 128], BF16)
    T3, r_T3 = sbp('T3', [128, 8, 8, 2, 32], BF16)
    UC, r_UC = sbp('UC', [128, 8, 64], F32)
    US, r_US = sbp('US', [128, 8, 64], F32)
    Rt, r_Rt = sbp('Rt', [128, 8], F32)
    G0re, r_G0re = sbp('G0re', [128, 8], F32)
    G0im, r_G0im = sbp('G0im', [128, 8], F32)
    wglu, r_wglu = sbp('wglu', [128, 2, 256], BF16)
    bglu, r_bglu = sbp('bglu', [128, 2], F32)
    s5.update(T1=T1, r_T1=r_T1, T2re=T2re, r_T2re=r_T2re, T2im=T2im, r_T2im=r_T2im, T3=T3, r_T3=r_T3, UC=UC, r_UC=r_UC,
              US=US, r_US=r_US, Rt=Rt, r_Rt=r_Rt, G0re=G0re, r_G0re=r_G0re, G0im=G0im, r_G0im=r_G0im,
              wglu=wglu, r_wglu=r_wglu, bglu=bglu, r_bglu=r_bglu)
    for nm, shp, dt in [('Wre', [128, 8, 64], F32), ('Wim', [128, 8, 64], F32), ('Gre', [128, 8, 64], F32), ('Gim', [128, 8, 64], F32),
                        ('tA', [128, 8, 64], F32), ('tB', [128, 8, 64], F32),
                        ('Hbre', [128, 8, 65], BF16), ('Hbim', [128, 8, 65], BF16),
                        ('hg', [128, 2, 512], F32), ('hgb', [128, 2, 512], BF16), ('sig', [128, 512], F32),
                        ('sq', [128, 2, 512], BF16)]:
        t, r = sbp(nm, shp, dt)
        s5[nm] = t
        s5['r_' + nm] = r
    for a, b in (('Hre', 'Wre'), ('Him', 'Wim'), ('ob', 'hg'), ('rsb', 'sig')):
        s5[a] = s5[b]
        s5['r_' + a] = s5['r_' + b]
    P.dma('pool', wglu[:], dram['s5_wglu'][l].rearrange("(c p) n -> p c n", p=128), writes=[r_wglu])
    P.dma('sp', bglu[:], dram['s5_bglucol'][l], writes=[r_bglu])
    P.op('dve', lambda e: e.memset(G0re[:], 0.0), writes=[r_G0re])
    P.op('dve', lambda e: e.memset(G0im[:], 0.0), writes=[r_G0im])

    with ExitStack() as tmp:
        tb = lambda name, shape, dt=F32: K.sb(tmp, name, shape, dt)
        are, r_are = tb('are', [128, 8])
        aim, r_aim = tb('aim', [128, 8])
        ls, r_ls = tb('ls', [128, 8])
        bre, r_bre = tb('bre', [128, 8, 16])
        bim, r_bim = tb('bim', [128, 8, 16])
        cre, r_cre = tb('cre', [128, 8, 16])
        cim, r_cim = tb('cim', [128, 8, 16])
        dcol, r_dcol = tb('dcol', [128, 2])
        for (t, r, nm) in [(are, r_are, 's5_are'), (aim, r_aim, 's5_aim'), (ls, r_ls, 's5_ls'), (bre, r_bre, 's5_bre'), (bim, r_bim, 's5_bim'),
                           (cre, r_cre, 's5_cre'), (cim, r_cim, 's5_cim'), (dcol, r_dcol, 's5_dcol')]:
            P.dma('sp', t[:], dram[nm][l], writes=[r])
        cnt = [0]

        def T(shape=[128, 8]):
            cnt[0] += 1
            return tb('t%d' % cnt[0], shape)

        def tt(out, o_r, a, a_r, b, b_r, op, eng='dve'):
            P.op(eng, lambda e: e.tensor_tensor(out=out, in0=a, in1=b, op=op), reads=[a_r, b_r], writes=[o_r])

        def ts(out, o_r, a, a_r, s1, s2, op0, op1=None):
            if op1 is None:
                P.op('dve', lambda e: e.tensor_scalar(out=out, in0=a, scalar1=s1, scalar2=None, op0=op0), reads=[a_r], writes=[o_r])
            else:
                P.op('dve', lambda e: e.tensor_scalar(out=out, in0=a, scalar1=s1, scalar2=s2, op0=op0, op1=op1), reads=[a_r], writes=[o_r])

        step, r_step = T()
        P.op('act', lambda e: e.activation(out=step[:], in_=ls[:], func=AF.Exp), reads=[r_ls], writes=[r_step])
        lre, r_lre = T()
        ts(lre[:], r_lre, are[:], r_are, -1e-4, None, ALU.min)
        lrs, r_lrs = T()
        tt(lrs[:], r_lrs, lre[:], r_lre, step[:], r_step, ALU.mult)
        mag, r_mag = T()
        P.op('act', lambda e: e.activation(out=mag[:], in_=lrs[:], func=AF.Exp), reads=[r_lrs], writes=[r_mag])
        P.op('act', lambda e: e.activation(out=s5['Rt'][:], in_=lrs[:], func=AF.Exp, scale=8.0), reads=[r_lrs], writes=[s5['r_Rt']])
        th, r_th = T()
        tt(th[:], r_th, aim[:], r_aim, step[:], r_step, ALU.mult)

        def sin_of(src, r_src, shift, name):
            a, r_a = T()
            ts(a[:], r_a, src[:], r_src, shift, 1.0 / TWO_PI, ALU.add, ALU.mult)
            ki, r_ki = tb(name + '_ki', [128, 8], mybir.dt.int32)
            P.op('dve', lambda e: e.tensor_copy(out=ki[:], in_=a[:]), reads=[r_a], writes=[r_ki])
            kf, r_kf = T()
            P.op('dve', lambda e: e.tensor_copy(out=kf[:], in_=ki[:]), reads=[r_ki], writes=[r_kf])
            fr, r_fr = T()
            tt(fr[:], r_fr, a[:], r_a, kf[:], r_kf, ALU.subtract)
            c1, r_c1 = T()
            ts(c1[:], r_c1, fr[:], r_fr, 0.5, None, ALU.is_gt)
            tt(fr[:], r_fr, fr[:], r_fr, c1[:], r_c1, ALU.subtract)
            ts(c1[:], r_c1, fr[:], r_fr, -0.5, None, ALU.is_lt)
            tt(fr[:], r_fr, fr[:], r_fr, c1[:], r_c1, ALU.add)
            ts(fr[:], r_fr, fr[:], r_fr, 0.5, -0.5, ALU.min, ALU.max)
            o, r_o = T()
            P.op('act', lambda e: e.activation(out=o[:], in_=fr[:], func=AF.Sin, scale=TWO_PI), reads=[r_fr], writes=[r_o])
            return o, r_o
        sn, r_sn = sin_of(th, r_th, 0.0, 'sn')
        cs_, r_cs = sin_of(th, r_th, TWO_PI / 4, 'cs')
        Ar, r_Ar = T()
        Ai, r_Ai = T()
        tt(Ar[:], r_Ar, mag[:], r_mag, cs_[:], r_cs, ALU.mult)
        tt(Ai[:], r_Ai, mag[:], r_mag, sn[:], r_sn, ALU.mult)
        den, r_den = T()
        t1, r_t1 = T()
        tt(den[:], r_den, lre[:], r_lre, lre[:], r_lre, ALU.mult)
        tt(t1[:], r_t1, aim[:], r_aim, aim[:], r_aim, ALU.mult)
        tt(den[:], r_den, den[:], r_den, t1[:], r_t1, ALU.add)
        P.op('dve', lambda e: e.reciprocal(out=den[:], in_=den[:]), reads=[r_den], writes=[r_den])
        nr, r_nr = T()
        ts(nr[:], r_nr, Ar[:], r_Ar, -1.0, None, ALU.add)
        cfr, r_cfr = T()
        cfi, r_cfi = T()
        t2, r_t2 = T()
        tt(cfr[:], r_cfr, nr[:], r_nr, lre[:], r_lre, ALU.mult)
        tt(t2[:], r_t2, Ai[:], r_Ai, aim[:], r_aim, ALU.mult)
        tt(cfr[:], r_cfr, cfr[:], r_cfr, t2[:], r_t2, ALU.add)
        tt(cfr[:], r_cfr, cfr[:], r_cfr, den[:], r_den, ALU.mult)
        tt(cfi[:], r_cfi, Ai[:], r_Ai, lre[:], r_lre, ALU.mult)
        tt(t2[:], r_t2, nr[:], r_nr, aim[:], r_aim, ALU.mult)
        tt(cfi[:], r_cfi, cfi[:], r_cfi, t2[:], r_t2, ALU.subtract)
        tt(cfi[:], r_cfi, cfi[:], r_cfi, den[:], r_den, ALU.mult)
        BBr, r_BBr = T([128, 8, 16])
        BBi, r_BBi = T([128, 8, 16])
        t16, r_t16 = T([128, 8, 16])
        bc16 = lambda a: a[:].unsqueeze(2).to_broadcast([128, 8, 16])

        def cmul(outr, r_outr, outi, r_outi, pr, r_pr, pi, r_pi, xr, r_xr, xi, r_xi, shape_bc, tmpt, r_tmpt, neg_im=False):
            tt(outr, r_outr, xr, r_xr, shape_bc(pr), r_pr, ALU.mult)
            tt(tmpt, r_tmpt, xi, r_xi, shape_bc(pi), r_pi, ALU.mult)
            tt(outr, r_outr, outr, r_outr, tmpt, r_tmpt, ALU.subtract)
            tt(outi, r_outi, xi, r_xi, shape_bc(pr), r_pr, ALU.mult)
            tt(tmpt, r_tmpt, xr, r_xr, shape_bc(pi), r_pi, ALU.mult)
            if neg_im:
                tt(outi, r_outi, outi, r_outi, tmpt, r_tmpt, ALU.add)
                ts(outi, r_outi, outi, r_outi, -1.0, None, ALU.mult)
            else:
                tt(outi, r_outi, outi, r_outi, tmpt, r_tmpt, ALU.add)
        cmul(BBr[:], r_BBr, BBi[:], r_BBi, cfr, r_cfr, cfi, r_cfi, bre[:], r_bre, bim[:], r_bim, bc16, t16[:], r_t16)
        Pr, r_Pr = T([128, 8, 9])
        Pi, r_Pi = T([128, 8, 9])
        P.op('dve', lambda e: e.memset(Pr[:, :, 0:1], 1.0), writes=[r_Pr])
        P.op('dve', lambda e: e.memset(Pi[:, :, 0:1], 0.0), writes=[r_Pi])
        t8, r_t8 = T()
        for j in range(1, 9):
            tt(Pr[:, :, j], r_Pr, Pr[:, :, j - 1], r_Pr, Ar[:], r_Ar, ALU.mult)
            tt(t8[:], r_t8, Pi[:, :, j - 1], r_Pi, Ai[:], r_Ai, ALU.mult)
            tt(Pr[:, :, j], r_Pr, Pr[:, :, j], r_Pr, t8[:], r_t8, ALU.subtract)
            tt(Pi[:, :, j], r_Pi, Pi[:, :, j - 1], r_Pi, Ar[:], r_Ar, ALU.mult)
            tt(t8[:], r_t8, Pr[:, :, j - 1], r_Pr, Ai[:], r_Ai, ALU.mult)
            tt(Pi[:, :, j], r_Pi, Pi[:, :, j], r_Pi, t8[:], r_t8, ALU.add)
        XSr, r_XSr = T([128, 8, 8, 32])
        XSi, r_XSi = T([128, 8, 8, 32])
        CSr, r_CSr = T([128, 8, 32])
        CSi, r_CSi = T([128, 8, 32])
        for (t, r) in ((XSr, r_XSr), (XSi, r_XSi), (CSr, r_CSr), (CSi, r_CSi)):
            P.op('pool', lambda e, t=t: e.memset(t[:], 0.0), writes=[r])
        P.op('pool', lambda e: e.memset(T3[:], 0.0), writes=[r_T3])
        Vr, r_Vr = T([128, 8, 16])
        Vi, r_Vi = T([128, 8, 16])
        for dlt in range(8):
            pcr = lambda a, dlt=dlt: a[:, :, dlt:dlt + 1].to_broadcast([128, 8, 16])
            cmul(Vr[:], r_Vr, Vi[:], r_Vi, Pr, r_Pr, Pi, r_Pi, BBr[:], r_BBr, BBi[:], r_BBi, pcr, t16[:], r_t16)
            for (V, r_V, XS, r_XS) in ((Vr, r_Vr, XSr, r_XSr), (Vi, r_Vi, XSi, r_XSi)):
                P.op('dve', lambda e, V=V, XS=XS, dlt=dlt: e.tensor_copy(out=XS[0:64, dlt, :, 0:16], in_=V[0:64, :, :]), reads=[r_V], writes=[r_XS])
                P.op('dve', lambda e, V=V, XS=XS, dlt=dlt: e.tensor_copy(out=XS[64:128, dlt, :, 16:32], in_=V[64:128, :, :]), reads=[r_V], writes=[r_XS])
        P.op('dve', lambda e: e.tensor_copy(out=CSr[0:64, :, 0:16], in_=cre[0:64, :, :]), reads=[r_cre], writes=[r_CSr])
        P.op('dve', lambda e: e.tensor_copy(out=CSr[64:128, :, 16:32], in_=cre[64:128, :, :]), reads=[r_cre], writes=[r_CSr])
        ts(CSi[0:64, :, 0:16], r_CSi, cim[0:64, :, :], r_cim, -1.0, None, ALU.mult)
        ts(CSi[64:128, :, 16:32], r_CSi, cim[64:128, :, :], r_cim, -1.0, None, ALU.mult)
        T1f, r_T1f = T([128, 128])
        dgt, r_dgt = T([128, 128])
        for ct in range(2):
            for dlt in range(8):
                pst, r_ps = bank()
                P.op('pool', lambda e: e.memset(T1f[:], 0.0), writes=[r_T1f])
                for q in range(4):
                    gp = 4 * ct + q
                    P.op('pe', lambda e, pst=pst, q=q, gp=gp, dlt=dlt: e.matmul(pst[32 * q:32 * q + 32, 32 * q:32 * q + 32], lhsT=XSr[:, dlt, gp, :], rhs=CSr[:, gp, :],
                                                                             start=True, stop=False, skip_group_check=True, tile_position=(0, 32 * q)), reads=[r_XSr, r_CSr], writes=[r_ps])
                    P.op('pe', lambda e, pst=pst, q=q, gp=gp, dlt=dlt: e.matmul(pst[32 * q:32 * q + 32, 32 * q:32 * q + 32], lhsT=XSi[:, dlt, gp, :], rhs=CSi[:, gp, :],
                                                                             start=False, stop=True, skip_group_check=True, tile_position=(0, 32 * q)), reads=[r_XSi, r_CSi], writes=[r_ps])
                for q in range(4):
                    P.op('dve', lambda e, pst=pst, q=q: e.tensor_copy(out=T1f[32 * q:32 * q + 32, 32 * q:32 * q + 32], in_=pst[32 * q:32 * q + 32, 32 * q:32 * q + 32]),
                         reads=[r_ps], writes=[r_T1f])
                if dlt == 0:
                    P.op('dve', lambda e, ct=ct: e.tensor_scalar(out=dgt[:], in0=ident_f[:], scalar1=dcol[:, ct:ct + 1], scalar2=None, op0=ALU.mult),
                         reads=[r_ident_f, r_dcol], writes=[r_dgt])
                    tt(T1f[:], r_T1f, T1f[:], r_T1f, dgt[:], r_dgt, ALU.add)
                P.op('act', lambda e, ct=ct, dlt=dlt: e.copy(out=T1[:, ct, dlt, :], in_=T1f[:]), reads=[r_T1f], writes=[r_T1])
        for ct in range(2):
            for sg_ in range(8):
                dlt = 7 - sg_
                for (XS, r_XS, T2, r_T2) in ((XSr, r_XSr, T2re, r_T2re), (XSi, r_XSi, T2im, r_T2im)):
                    pst, r_ps = bank()
                    P.op('pe', lambda e, pst=pst, XS=XS, dlt=dlt, ct=ct: e.transpose(out=pst[:, 0:128], in_=XS[:, dlt, 4 * ct:4 * ct + 4, :].rearrange("p a b -> p (a b)"),
                                                                                     identity=ident_f[:]), reads=[r_XS, r_ident_f], writes=[r_ps])
                    P.op('act', lambda e, pst=pst, T2=T2, ct=ct, sg_=sg_: e.copy(out=T2[:, ct, sg_, :], in_=pst[:, 0:128]), reads=[r_ps], writes=[r_T2])
        Fr, r_Fr = T([128, 8, 16])
        Fi, r_Fi = T([128, 8, 16])
        for tau in range(8):
            pcr = lambda a, tau=tau: a[:, :, tau + 1:tau + 2].to_broadcast([128, 8, 16])
            cmul(Fr[:], r_Fr, Fi[:], r_Fi, Pr, r_Pr, Pi, r_Pi, cre[:], r_cre, cim[:], r_cim, pcr, t16[:], r_t16, neg_im=True)
            for (Fx, r_Fx, part) in ((Fr, r_Fr, 0), (Fi, r_Fi, 1)):
                P.op('dve', lambda e, Fx=Fx, tau=tau, part=part: e.tensor_copy(out=T3[0:64, :, tau, part, 0:16], in_=Fx[0:64, :, :]), reads=[r_Fx], writes=[r_T3])
                P.op('dve', lambda e, Fx=Fx, tau=tau, part=part: e.tensor_copy(out=T3[64:128, :, tau, part, 16:32], in_=Fx[64:128, :, :]), reads=[r_Fx], writes=[r_T3])
        rinv, r_rinv = T()
        P.op('dve', lambda e: e.reciprocal(out=rinv[:], in_=s5['Rt'][:]), reads=[s5['r_Rt']], writes=[r_rinv])
        tt(UC[:, :, 0], r_UC, Pr[:, :, 8], r_Pr, rinv[:], r_rinv, ALU.mult)
        tt(US[:, :, 0], r_US, Pi[:, :, 8], r_Pi, rinv[:], r_rinv, ALU.mult)
        tw, r_tw = T([128, 8, 32])
        n = 1
        while n < 64:
            bcn = lambda a, n=n: a[:, :, n - 1:n].to_broadcast([128, 8, n])
            cmul(UC[:, :, n:2 * n], r_UC, US[:, :, n:2 * n], r_US, UC, r_UC, US, r_US, UC[:, :, 0:n], r_UC, US[:, :, 0:n], r_US, bcn, tw[:, :, 0:n], r_tw)
            n *= 2
        if dbg_d is not None:
            t1d, r_t1d = T([128, 2 * 8 * 128])
            P.op('dve', lambda e: e.tensor_copy(out=t1d[:], in_=T1[:].rearrange("p a b c -> p (a b c)")), reads=[r_T1], writes=[r_t1d])
            P.dma('sp', dbg_d['t1'], t1d[:], reads=[r_t1d], writes=[dbg_res])
        P.barrier()
    return s5


def s5_macro(nc, P, K, s5, suT, r_suT, mixBT, r_mixBT, oncol, r_oncol, banks, bank, ones_b, r_ones_b, eps_t, r_eps, m, dbg_d, dbg_res):
    g = lambda k: (s5[k], s5['r_' + k])
    T1, r_T1 = g('T1')
    T2re, r_T2re = g('T2re')
    T2im, r_T2im = g('T2im')
    T3, r_T3 = g('T3')
    UC, r_UC = g('UC')
    US, r_US = g('US')
    Rt, r_Rt = g('Rt')
    G0re, r_G0re = g('G0re')
    G0im, r_G0im = g('G0im')
    Wre, r_Wre = g('Wre')
    Wim, r_Wim = g('Wim')
    Gre, r_Gre = g('Gre')
    Gim, r_Gim = g('Gim')
    Hre, r_Hre = g('Hre')
    Him, r_Him = g('Him')
    tA, r_tA = g('tA')
    tB, r_tB = g('tB')
    Hbre, r_Hbre = g('Hbre')
    Hbim, r_Hbim = g('Hbim')
    hg, r_hg = g('hg')
    hgb, r_hgb = g('hgb')
    sig, r_sig = g('sig')
    ob, r_ob = g('ob')
    sq, r_sq = g('sq')
    rsb, r_rsb = g('rsb')
    wglu, r_wglu = g('wglu')
    bglu, r_bglu = g('bglu')

    def tt(out, o_r, a, a_r, b, b_r, op, eng='dve'):
        P.op(eng, lambda e: e.tensor_tensor(out=out, in0=a, in1=b, op=op), reads=[a_r, b_r], writes=[o_r])
    def tt4(out, o_r, eb, other, r_other, op, q):
        pst, r_ps = eb[q]
        P.op('dve', lambda e: e.tensor_tensor(out=out[:, q:8:4, :], in0=pst[:, 0:128].rearrange("p (c k) -> p c k", c=2),
                                              in1=other[:, q:8:4, :], op=op), reads=[r_ps, r_other], writes=[o_r])
    for part, (T2, r_T2) in enumerate(((T2re, r_T2re), (T2im, r_T2im))):
        eb = [banks[3 + q] for q in range(4)]
        for q in range(4):
            pst, r_ps = eb[q]
            for ct in range(2):
                for sg_ in range(8):
                    P.op('pe', lambda e, pst=pst, T2=T2, ct=ct, q=q, sg_=sg_: e.matmul(
                        pst[:, ct * 64:(ct + 1) * 64], lhsT=T2[32 * q:32 * q + 32, ct, sg_, :], rhs=suT[32 * q:32 * q + 32, ct, sg_ * 64:(sg_ + 1) * 64],
                        start=(sg_ == 0), stop=(sg_ == 7), skip_group_check=True, tile_position=(32 * q, 0)), reads=[r_T2, r_suT], writes=[r_ps])
        for q in range(4):
            if part == 0:
                tt4(Wre, r_Wre, eb, UC, r_UC, ALU.mult, q)
                tt4(tB, r_tB, eb, US, r_US, ALU.mult, q)
            else:
                tt4(tA, r_tA, eb, US, r_US, ALU.mult, q)
                tt4(Wim, r_Wim, eb, UC, r_UC, ALU.mult, q)
    tt(Wre[:], r_Wre, Wre[:], r_Wre, tA[:], r_tA, ALU.add)
    tt(Wim[:], r_Wim, Wim[:], r_Wim, tB[:], r_tB, ALU.subtract)
    P.op('act', lambda e: e.copy(out=Hbre[:, :, 0], in_=G0re[:]), reads=[r_G0re], writes=[r_Hbre])
    P.op('act', lambda e: e.copy(out=Hbim[:, :, 0], in_=G0im[:]), reads=[r_G0im], writes=[r_Hbim])
    for gp in range(8):
        P.op('dve', lambda e, gp=gp: e.tensor_tensor_scan(out=Gre[:, gp, :], data0=Rt[:, gp:gp + 1].to_broadcast([128, 64]), data1=Wre[:, gp, :],
                                                          initial=G0re[:, gp:gp + 1], op0=ALU.mult, op1=ALU.add),
             reads=[r_Rt, r_Wre, r_G0re], writes=[r_Gre])
        P.op('dve', lambda e, gp=gp: e.tensor_tensor_scan(out=Gim[:, gp, :], data0=Rt[:, gp:gp + 1].to_broadcast([128, 64]), data1=Wim[:, gp, :],
                                                          initial=G0im[:, gp:gp + 1], op0=ALU.mult, op1=ALU.add),
             reads=[r_Rt, r_Wim, r_G0im], writes=[r_Gim])
    tt(Hre[:], r_Hre, Gre[:], r_Gre, UC[:], r_UC, ALU.mult)
    tt(tA[:], r_tA, Gim[:], r_Gim, US[:], r_US, ALU.mult)
    tt(Hre[:], r_Hre, Hre[:], r_Hre, tA[:], r_tA, ALU.subtract)
    tt(Him[:], r_Him, Gim[:], r_Gim, UC[:], r_UC, ALU.mult)
    tt(tB[:], r_tB, Gre[:], r_Gre, US[:], r_US, ALU.mult)
    tt(Him[:], r_Him, Him[:], r_Him, tB[:], r_tB, ALU.add)
    P.op('act', lambda e: e.copy(out=Hbre[:, :, 1:65], in_=Hre[:]), reads=[r_Hre], writes=[r_Hbre])
    P.op('act', lambda e: e.copy(out=Hbim[:, :, 1:65], in_=Him[:]), reads=[r_Him], writes=[r_Hbim])
    P.op('dve', lambda e: e.tensor_copy(out=G0re[:], in_=Hre[:, :, 63]), reads=[r_Hre], writes=[r_G0re])
    P.op('dve', lambda e: e.tensor_copy(out=G0im[:], in_=Him[:, :, 63]), reads=[r_Him], writes=[r_G0im])
    pY = [bank(), bank()]
    for ct in range(2):
        pst, r_ps = pY[ct]
        for dlt in range(8):
            P.op('pe', lambda e, pst=pst, dlt=dlt, ct=ct: e.matmul(
                pst[:, dlt * 64:512], lhsT=T1[:, ct, dlt, :], rhs=suT[:, ct, 0:(8 - dlt) * 64],
                start=(dlt == 0), stop=False, skip_group_check=True), reads=[r_T1, r_suT], writes=[r_ps])
        for tau in range(8):
            for q in range(4):
                gp = 4 * ct + q
                for part, (Hb, r_Hb) in enumerate(((Hbre, r_Hbre), (Hbim, r_Hbim))):
                    last = (tau == 7 and q == 3 and part == 1)
                    P.op('pe', lambda e, pst=pst, tau=tau, q=q, gp=gp, part=part, Hb=Hb, last=last: e.matmul(
                        pst[32 * q:32 * q + 32, tau * 64:(tau + 1) * 64], lhsT=T3[:, gp, tau, part, :], rhs=Hb[:, gp, 0:64],
                        start=False, stop=last, skip_group_check=True, tile_position=(0, 32 * q)), reads=[r_T3, r_Hb], writes=[r_ps])
    if dbg_d is not None:
        for ct in range(2):
            P.op('act', lambda e, ct=ct: e.copy(out=hg[:, ct, :], in_=pY[ct][0][:, :]), reads=[pY[ct][1]], writes=[r_hg])
            P.dma('sp', dbg_d['s5y'][ct * 128:(ct + 1) * 128, m * 512:(m + 1) * 512], hg[:, ct, :], reads=[r_hg], writes=[dbg_res])
    for ct in range(2):
        P.op('act', lambda e, ct=ct: e.activation(out=hg[:, ct, :], in_=pY[ct][0][:, :], func=AF.Gelu_apprx_tanh), reads=[pY[ct][1]], writes=[r_hg])
        P.op('dve', lambda e, ct=ct: e.tensor_copy(out=hgb[:, ct, :], in_=hg[:, ct, :]), reads=[r_hg], writes=[r_hgb])
    for c2 in range(2):
        pst, r_ps = bank()
        for ct in range(2):
            P.op('pe', lambda e, pst=pst, ct=ct, c2=c2: e.matmul(pst[:, :], lhsT=wglu[:, ct, c2 * 128:(c2 + 1) * 128], rhs=hgb[:, ct, :],
                                                               start=(ct == 0), stop=(ct == 1)), reads=[r_wglu, r_hgb], writes=[r_ps])
        P.op('act', lambda e, pst=pst, c2=c2: e.activation(out=sig[:], in_=pst[:, :], func=AF.Sigmoid, bias=bglu[:, c2:c2 + 1], scale=1.0),
             reads=[r_ps, r_bglu], writes=[r_sig])
        tt(ob[:, c2, :], r_ob, hg[:, c2, :], r_hg, sig[:], r_sig, ALU.mult)
        P.op('act', lambda e, c2=c2: e.activation(out=sq[:, c2, :], in_=ob[:, c2, :], func=AF.Square), reads=[r_ob], writes=[r_sq])
    if dbg_d is not None:
        for ct in range(2):
            P.dma('sp', dbg_d['obT'][ct * 128:(ct + 1) * 128, m * 512:(m + 1) * 512], ob[:, ct, :], reads=[r_ob], writes=[dbg_res])
    pst, r_ps = bank()
    for ct in range(2):
        P.op('pe', lambda e, pst=pst, ct=ct: e.matmul(pst[:, :], lhsT=ones_b[:], rhs=sq[:, ct, :], start=(ct == 0), stop=(ct == 1)),
             reads=[r_ones_b, r_sq], writes=[r_ps])
    P.op('act', lambda e, pst=pst: e.activation(out=rsb[:], in_=pst[:, :], func=AF.Sqrt, bias=eps_t[:], scale=1.0 / 256), reads=[r_ps, r_eps], writes=[r_rsb])
    P.op('dve', lambda e: e.reciprocal(out=rsb[:], in_=rsb[:]), reads=[r_rsb], writes=[r_rsb])
    for ct in range(2):
        P.op('dve', lambda e, ct=ct: e.scalar_tensor_tensor(out=mixBT[:, ct, :].rearrange("p (k t) -> p t k", t=8),
                                                            in0=ob[:, ct, :].rearrange("p (t k) -> p t k", t=8), scalar=oncol[:, 3 + ct:4 + ct],
                                                            in1=rsb[:].rearrange("p (t k) -> p t k", t=8),
                                                            op0=ALU.mult, op1=ALU.mult), reads=[r_ob, r_oncol, r_rsb], writes=[r_mixBT])


_CACHE = {}


def kernel(**inputs):
    shared = _prep_shared(inputs)
    x = np.ascontiguousarray(np.asarray(inputs['x'], dtype=np.float32))
    c = np.asarray(inputs['c'], dtype=np.float32)
    key = 'nc'
    if key not in _CACHE:
        _CACHE[key] = build({k: v.shape for k, v in shared.items()})
    nc = _CACHE[key]
    in_maps = []
    for b in range(8):
        d = dict(shared)
        d['x'] = x[b]
        d['ccol'] = _col(c[b])
        in_maps.append(d)
    res = run_bass_kernel_spmd(nc, in_maps, core_ids=list(range(8)))
    return np.stack([np.asarray(r['out'], dtype=np.float32) for r in res.results], axis=0)
```

```python
import numpy as np
from contextlib import ExitStack
import concourse.bass as bass
import concourse.mybir as mybir
from concourse.bass_utils import run_bass_kernel_spmd

F32 = mybir.dt.float32
BF16 = mybir.dt.bfloat16
AF = mybir.ActivationFunctionType
ALU = mybir.AluOpType

ENGS = ['pe', 'act', 'dve', 'pool', 'sp']
S = 4096
D = 1024
NEGB = -30000.0


class Res:
    __slots__ = ('name', 'w', 'r', 'excl')

    def __init__(self, name='', excl=False):
        self.name = name
        self.w = None
        self.r = {}
        self.excl = excl


class _Rec:
    def __init__(self):
        self.call = None

    def __getattr__(self, name):
        def f(*a, **k):
            self.call = (name, a, k)
            return self
        return f


class Prog:
    def __init__(self, nc, n_dma_sems=4):
        self.nc = nc
        self.ops = {e: [] for e in ENGS}
        self.cnt = {e: 0 for e in ENGS}
        self.seen = {e: {} for e in ENGS}
        self.esem = {}
        self.dsem = {}
        self.dsem_val = {}
        self.dq_rr = {e: 0 for e in ENGS}
        self.n_dma_sems = n_dma_sems
        self.stopped = False
        self.epoch = 0

    def alloc_sems(self, stack):
        nc = self.nc
        self._stack = stack
        for e in ['pe', 'act', 'dve', 'pool']:
            self.esem[e] = stack.enter_context(nc.semaphore('s_%s_0' % e))
        for q in ['sp', 'act', 'pool']:
            self.dsem[q] = [stack.enter_context(nc.semaphore('d_%s_%d' % (q, i))) for i in range(self.n_dma_sems)]
            for i in range(self.n_dma_sems):
                self.dsem_val[(q, i)] = 0

    def _need(self, e, dep, waits):
        if dep is None:
            return
        if dep[0] == 'eng':
            _, f, idx, ep = dep
            if ep < self.epoch:
                return
            if f == e and e in ('pe', 'sp'):
                return
            key = ('eng', f)
            if self.seen[e].get(key, 0) >= idx:
                return
            waits[key] = max(waits.get(key, 0), idx)
        else:
            _, q, i, val = dep
            key = ('dma', q, i)
            if self.seen[e].get(key, 0) >= val:
                return
            waits[key] = max(waits.get(key, 0), val)

    def _collect(self, e, reads, writes):
        waits = {}
        for r in reads:
            self._need(e, r.w, waits)
            if r.excl:
                for k, d in r.r.items():
                    if k != ('eng', e):
                        self._need(e, d, waits)
        for w in writes:
            self._need(e, w.w, waits)
            for d in w.r.values():
                self._need(e, d, waits)
        for k, v in waits.items():
            self.seen[e][k] = v
        return waits

    def op(self, e, fn, reads=(), writes=()):
        if self.stopped:
            return
        waits = self._collect(e, reads, writes)
        self.cnt[e] += 1
        dep = ('eng', e, self.cnt[e], self.epoch)
        for r in reads:
            r.r[('eng', e)] = dep
        for w in writes:
            w.w = dep
            w.r = {}
        rec = _Rec()
        fn(rec)
        name, a, k = rec.call
        self.ops[e].append(([(self._semof(kk), v) for kk, v in waits.items()],
                            (lambda eng, name=name, a=a, k=k: getattr(eng, name)(*a, **k)), (self.esem[e], 1)))

    def dma(self, q, out, in_, reads=(), writes=(), **kw):
        if self.stopped:
            return
        waits = self._collect(q, reads, writes)
        i = self.dq_rr[q]
        self.dq_rr[q] = (i + 1) % self.n_dma_sems
        prev = self.dsem_val[(q, i)]
        key = ('dma', q, i)
        if prev > 0 and self.seen[q].get(key, 0) < prev:
            waits[key] = prev
            self.seen[q][key] = prev
        val = prev + 16
        self.dsem_val[(q, i)] = val
        dep = ('dma', q, i, val)
        for r in reads:
            r.r[('dma', q, i)] = dep
        for w in writes:
            w.w = dep
            w.r = {}
        self.ops[q].append(([(self._semof(kk), v) for kk, v in waits.items()],
                            lambda eng: eng.dma_start(out=out, in_=in_, **kw), (self.dsem[q][i], 16)))

    def barrier(self, engines=ENGS, final=False):
        if self.stopped:
            return
        for e in engines:
            waits = {}
            for f in ['pe', 'act', 'dve', 'pool']:
                if self.cnt[f] > 0 and f != e and self.seen[e].get(('eng', f), 0) < self.cnt[f]:
                    waits[('eng', f)] = self.cnt[f]
            for (q, i), v in self.dsem_val.items():
                if v > 0 and self.seen[e].get(('dma', q, i), 0) < v:
                    waits[('dma', q, i)] = v
            for k, v in waits.items():
                self.seen[e][k] = v
            self.ops[e].append(([(self._semof(kk), v) for kk, v in waits.items()], None, None))
        if final:
            return
        self.epoch += 1
        for e in ['pe', 'act', 'dve', 'pool']:
            self.esem[e] = self._stack.enter_context(self.nc.semaphore('s_%s_%d' % (e, self.epoch)))
            self.cnt[e] = 0
        for e in ENGS:
            for k in [k for k in self.seen[e] if k[0] == 'eng']:
                del self.seen[e][k]

    def _semof(self, key):
        if key[0] == 'eng':
            return self.esem[key[1]]
        return self.dsem[key[1]][key[2]]

    def replay(self, block):
        engmap = {'pe': 'tensor', 'act': 'scalar', 'dve': 'vector', 'pool': 'gpsimd', 'sp': 'sync'}

        def mk(e):
            def body(eng):
                for waits, fn, inc in self.ops[e]:
                    for sem, v in waits:
                        eng.wait_ge(sem, v)
                    if fn is None:
                        continue
                    inst = fn(eng)
                    if inc is not None:
                        inst.then_inc(inc[0], inc[1])
            return body
        for e in ENGS:
            getattr(block, engmap[e])(mk(e))


def _col(v):
    return np.ascontiguousarray(v.reshape(-1, 128).T)


def _win_perm():
    idx = []
    for base in (0,):
        for g in range(3):
            for hk in range(2):
                idx += list(range(base + hk * 192 + g * 64, base + hk * 192 + g * 64 + 64))
    idx += list(range(384, 512))
    for g in range(3):
        for hk in range(2):
            idx += list(range(896 + hk * 192 + g * 64, 896 + hk * 192 + g * 64 + 64))
    idx += list(range(1280, 1664))
    idx += list(range(640, 896))
    idx += list(range(1664, 1792))
    idx += list(range(512, 640))
    idx += list(range(1792, 2066))
    assert len(idx) == 2066 and len(set(idx)) == 2066
    return np.array(idx)


def _consts():
    c = {}
    ko = np.arange(128)[:, None]
    qo = np.arange(128)[None, :]
    caus = np.where(ko <= qo, 0.0, NEGB).astype(np.float32)
    anti = np.where(ko > qo, 0.0, NEGB).astype(np.float32)
    rel = np.arange(128)[:, None]
    cbrel = np.where(16 * (rel - 96) + 15 <= qo, 0.0, NEGB).astype(np.float32)
    c['caus3'] = np.tile(caus, (1, 3))
    c['anti3'] = np.tile(anti, (1, 3))
    c['cbrel3'] = np.tile(cbrel, (1, 3))
    iw2 = np.zeros((128, 384), np.float32)
    iw2[np.arange(128), np.arange(128) + 128] = 1.0
    c['iw2'] = iw2
    ew = np.zeros((128, 4096), np.float32)
    ew[np.arange(4096) // 64, np.arange(4096)] = 1.0
    c['ewide'] = ew
    c['ident'] = np.eye(128, dtype=np.float32)
    blk = np.zeros((128, 128), np.float32)
    blk[:64, :64] = 1.0
    blk[64:, 64:] = 1.0
    c['blk64'] = blk
    q = np.arange(128)[:, None]
    r = np.arange(128)[None, :]
    relj = r - 62
    qb = (q >= 64).astype(np.int64)
    fb = np.zeros((128, 128), np.float32)
    fb[(relj == qb) | (relj == qb - 1)] = 1e4
    fb[relj > qb] = -1e30
    c['fbwide'] = fb
    n_cmp = 255
    cs = np.arange(n_cmp)[:, None] * 16
    ss = np.arange(64)[None, :] * 64
    cover = (np.clip(np.minimum(cs + 32, ss + 64) - np.maximum(cs, ss), 0, None) / 32).astype(np.float32)
    va = np.zeros((256, 65), np.float32)
    va[1:, 0] = 1.0
    va[1:, 1:] = cover
    c['vaugc'] = np.ascontiguousarray(va.reshape(2, 128, 65).transpose(1, 0, 2))
    return c


def _prep_shared(inp):
    f = lambda a: np.ascontiguousarray(np.asarray(a, dtype=np.float32))
    L = 2
    o = {}
    o['w_ada'] = f(inp['w_ada'])
    o['b_ada'] = f(inp['b_ada'])
    o['n1col'] = f(np.stack([_col(inp['norm1_g'][l]) for l in range(L)]))
    o['n2col'] = f(np.stack([_col(inp['norm2_g'][l]) for l in range(L)]))
    perm = _win_perm()
    o['w_in'] = f(inp['w_in'][:, :, perm])
    o['w_out'] = f(inp['w_out'])
    o['onorm'] = f(inp['out_norm_g'])
    o['onormcol'] = f(np.stack([_col(inp['out_norm_g'][l]) for l in range(L)]))
    g = []
    for l in range(L):
        cols = [inp['a_q_gain'][l], inp['a_k_gain'][l], inp['c_q_gain'][l],
                inp['c_k_gain'][l, 0], inp['c_k_gain'][l, 1], inp['c_k_gain'][l, 2]]
        g.append(np.stack([np.tile(np.asarray(v), 2) for v in cols], axis=1))
    o['gains'] = f(np.stack(g))
    o['sinks'] = f(inp['a_sinks'])

    def st(a):
        a = np.asarray(a).reshape(L, 8, 2, 64)
        return f(a.transpose(0, 2, 3, 1).reshape(L, 128, 8))
    o['s5_are'] = st(inp['s5_a_re'])
    o['s5_aim'] = st(inp['s5_a_im'])
    ls = np.broadcast_to(np.asarray(inp['s5_log_step'])[:, :, None], (L, 16, 64))
    o['s5_ls'] = st(ls)

    def stb(a):
        a = np.asarray(a).reshape(L, 8, 2, 64, 16)
        return f(a.transpose(0, 2, 3, 1, 4).reshape(L, 128, 8, 16))

    def stc(a):
        a = np.asarray(a).reshape(L, 8, 2, 16, 64)
        return f(a.transpose(0, 2, 4, 1, 3).reshape(L, 128, 8, 16))
    o['s5_bre'] = stb(inp['s5_b_re'])
    o['s5_bim'] = stb(inp['s5_b_im'])
    o['s5_cre'] = stc(inp['s5_c_re'])
    o['s5_cim'] = stc(inp['s5_c_im'])
    o['s5_dcol'] = f(np.stack([_col(inp['s5_d'][l]) for l in range(L)]))
    o['s5_wglu'] = f(inp['s5_w_glu'])
    o['s5_bglucol'] = f(np.stack([_col(inp['s5_b_glu'][l]) for l in range(L)]))
    pe = np.asarray(inp['cmp_pe'])
    o['cmp_peT'] = f(np.tile(pe.transpose(0, 1, 3, 2), (1, 1, 2, 1)))
    w1 = np.asarray(inp['cmp_w1']).reshape(L, 2, 32, 64, 256)
    o['cmp_w1'] = f(np.tile(w1.transpose(0, 1, 3, 2, 4), (1, 1, 2, 1, 1)))
    o['cmp_b1col'] = f(np.asarray(inp['cmp_b1']).reshape(L, 2, 2, 128).transpose(0, 3, 1, 2))
    o['cmp_w2'] = f(np.asarray(inp['cmp_w2']).reshape(L, 2, 2, 128, 64).transpose(0, 3, 1, 2, 4))
    o['cmp_b2kcol'] = f(np.tile(np.asarray(inp['cmp_b2'])[:, 0, :], (1, 2))[:, :, None])
    o['cmp_b2v'] = f(np.tile(np.asarray(inp['cmp_b2'])[:, 1, :], (1, 2)))
    o['w_ff1'] = f(inp['w_ff1'])
    o['w_ff2'] = f(inp['w_ff2'])
    o.update(_consts())
    return o


class KB:
    def __init__(self, nc, P, st):
        self.nc, self.P, self.st = nc, P, st
        self.bank_rr = 0
        self.uid = 0

    def sb(self, st, name, shape, dt):
        self.uid += 1
        nb = int(np.prod(shape[1:])) * (2 if dt == BF16 else 4)
        if not hasattr(self, 'log'):
            self.log = []
        self.log.append((name, nb))
        try:
            t = st.enter_context(self.nc.sbuf_tensor('%s_%d' % (name, self.uid), shape, dt))
        except AssertionError:
            for n_, b_ in self.log:
                print(n_, b_)
            raise
        return t, Res(name)


class StopBuild(Exception):
    pass


def build(shared_shapes, depth=2, dbg=False, n_macro=8, do_ffn=True, stage=99):
    nc = bass.Bass("TRN2", target_bir_lowering=False)
    dram = {}
    for k, shp in shared_shapes.items():
        dram[k] = nc.dram_tensor(k, list(shp), F32, kind="ExternalInput").ap()
    x_d = nc.dram_tensor("x", [S, D], F32, kind="ExternalInput").ap()
    ccol_d = nc.dram_tensor("ccol", [128, 8], F32, kind="ExternalInput").ap()
    out_d = nc.dram_tensor("out", [S, D], F32, kind="ExternalOutput").ap()
    mod_d = nc.dram_tensor("mod_scr", [2, 6144], F32, kind="Internal").ap()
    dbg_d = {}
    if dbg:
        for name, shp in [('mod', [2, 6144]), ('hT', [128, 8 * 512]), ('oa', [S, 384]), ('oc', [S, 384]),
                          ('obT', [256, S]), ('mixT', [128, 8 * 128]), ('proj', [128, 13 * 512]),
                          ('imp', [S, 128]), ('kcmp', [128, 256]), ('vcmp', [128, 2 * 2 * 129]),
                          ('s5y', [256, S]), ('t1', [128, 2 * 8 * 128]), ('avd', [128, 32 * 130])]:
            dbg_d[name] = nc.dram_tensor("dbg_" + name, shp, F32, kind="ExternalOutput").ap()

    with ExitStack() as st:
        P = Prog(nc)
        P.alloc_sems(st)
        K = KB(nc, P, st)
        nc_ = nc
        banks = []
        for b in range(8):
            t = st.enter_context(nc.psum_tensor('psb%d' % b, [128, 512], F32))
            banks.append((t, Res('bank%d' % b, excl=True)))
        pool_banks = [0, 1, 2, 7]

        def bank():
            b = pool_banks[K.bank_rr % len(pool_banks)]
            K.bank_rr += 1
            return banks[b]
        ACC_A, ACC_CMP, ACC_SEL, ACC_WIN = banks[3], banks[4], banks[5], banks[6]

        dres = {k: Res('d_' + k) for k in list(dram.keys()) + ['x', 'ccol', 'mod']}
        out_res = [Res('out%d' % t) for t in range(32)]
        dbg_res = Res('dbg')

        cst = ExitStack()
        st.enter_context(cst)
        ident_f, r_ident_f = K.sb(cst, 'ident_f', [128, 128], F32)
        ident_b, r_ident_b = K.sb(cst, 'ident_b', [128, 128], BF16)
        blk64, r_blk64 = K.sb(cst, 'blk64', [128, 128], BF16)
        ones_b, r_ones_b = K.sb(cst, 'ones_b', [128, 128], BF16)
        caus3, r_caus3 = K.sb(cst, 'caus3', [128, 384], BF16)
        anti3, r_anti3 = K.sb(cst, 'anti3', [128, 384], BF16)
        cbrel3, r_cbrel3 = K.sb(cst, 'cbrel3', [128, 384], BF16)
        iw2, r_iw2 = K.sb(cst, 'iw2', [128, 384], BF16)
        ewide, r_ewide = K.sb(cst, 'ewide', [128, 4096], BF16)
        fbwide, r_fbwide = K.sb(cst, 'fbwide', [128, 128], F32)
        eps_t, r_eps = K.sb(cst, 'eps_t', [128, 1], F32)
        mhalf, r_mhalf = K.sb(cst, 'mhalf', [128, 8], F32)
        P.dma('sp', ident_f[:], dram['ident'], writes=[r_ident_f])
        P.dma('sp', fbwide[:], dram['fbwide'], writes=[r_fbwide])
        for (t, r, nm) in [(ident_b, r_ident_b, 'ident'), (blk64, r_blk64, 'blk64'), (caus3, r_caus3, 'caus3'),
                           (anti3, r_anti3, 'anti3'), (cbrel3, r_cbrel3, 'cbrel3'), (iw2, r_iw2, 'iw2'),
                           (ewide, r_ewide, 'ewide')]:
            if nm == 'ewide':
                for cc in range(2):
                    P.dma('pool', t[:, cc * 2048:(cc + 1) * 2048], dram[nm][:, cc * 2048:(cc + 1) * 2048], writes=[r])
            else:
                P.dma('pool', t[:], dram[nm], writes=[r])
        P.op('dve', lambda e: e.memset(ones_b[:], 1.0), writes=[r_ones_b])
        P.op('dve', lambda e: e.memset(eps_t[:], 1e-6), writes=[r_eps])
        P.op('dve', lambda e: e.memset(mhalf[:], -0.5), writes=[r_mhalf])

        def rstd_small(st_, src_ap, src_res, n, inv_n, name):
            ms, r_ms = K.sb(st_, name + '_ms', [128, n], F32)
            rs, r_rs = K.sb(st_, name + '_rs', [128, n], F32)
            return ms, r_ms, rs, r_rs

        def emit_rstd(src_ap, src_res, ms, r_ms, rs, r_rs, n, inv_n):
            P.op('dve', lambda e: e.tensor_scalar(out=ms[:, 0:n], in0=src_ap, scalar1=inv_n, scalar2=1e-6,
                                                  op0=ALU.mult, op1=ALU.add), reads=[src_res], writes=[r_ms])
            P.op('pool', lambda e: e.tensor_tensor(out=rs[:, 0:n], in0=ms[:, 0:n], in1=mhalf[:, 0:n], op=ALU.pow),
                 reads=[r_ms, r_mhalf], writes=[r_rs])

        def stop_at(k):
            if stage <= k:
                P.stopped = True

        def body():
            with ExitStack() as ph:
                ccol, r_ccol = K.sb(ph, 'ccol', [128, 8], F32)
                cs, r_cs = K.sb(ph, 'cs', [128, 8], F32)
                wb = [K.sb(ph, 'wada%d' % i, [128, 3072], F32) for i in range(2)]
                row, r_row = K.sb(ph, 'row', [1, 3072], F32)
                brow, r_brow = K.sb(ph, 'brow', [1, 3072], F32)
                P.dma('sp', ccol[:], ccol_d, writes=[r_ccol])
                P.op('act', lambda e: e.activation(out=cs[:], in_=ccol[:], func=AF.Silu), reads=[r_ccol], writes=[r_cs])
                it = 0
                for l in range(depth):
                    for half in range(2):
                        for kc in range(8):
                            wt, r_wt = wb[it % 2]
                            P.dma('sp' if it % 2 == 0 else 'act', wt[:],
                                  dram['w_ada'][l, kc * 128:(kc + 1) * 128, half * 3072:(half + 1) * 3072], writes=[r_wt])
                            for n in range(6):
                                pst, r_ps = banks[n]
                                P.op('pe', lambda e, pst=pst, wt=wt, kc=kc, n=n: e.matmul(
                                    pst[0:1, :], lhsT=cs[:, kc:kc + 1], rhs=wt[:, n * 512:(n + 1) * 512],
                                    start=(kc == 0), stop=(kc == 7)), reads=[r_cs, r_wt], writes=[r_ps])
                            it += 1
                        P.dma('sp', brow[:], dram['b_ada'][l:l + 1, half * 3072:(half + 1) * 3072], writes=[r_brow])
                        for n in range(6):
                            pst, r_ps = banks[n]
                            P.op('dve', lambda e, pst=pst, n=n: e.tensor_tensor(
                                out=row[0:1, n * 512:(n + 1) * 512], in0=pst[0:1, :], in1=brow[0:1, n * 512:(n + 1) * 512],
                                op=ALU.add), reads=[r_ps, r_brow], writes=[r_row])
                        P.dma('sp', mod_d[l:l + 1, half * 3072:(half + 1) * 3072], row[:], reads=[r_row], writes=[dres['mod']])
                if dbg:
                    P.barrier()
                    mt, r_mt = K.sb(ph, 'modt', [2, 6144], F32)
                    P.dma('sp', mt[0:depth, :], mod_d[0:depth, :], reads=[dres['mod']], writes=[r_mt])
                    P.dma('sp', dbg_d['mod'][0:depth, :], mt[0:depth, :], reads=[r_mt], writes=[dbg_res])
            P.barrier()

            stop_at(0)
            for l in range(depth):
                src_d = x_d if l == 0 else out_d
                with ExitStack() as ph:
                    sbp = lambda name, shape, dt: K.sb(ph, name, shape, dt)
                    s5 = s5_setup(nc, P, K, ph, dram, l, banks, bank, ident_f, r_ident_f, dbg_d if (dbg and l == 0) else None, dbg_res)
                    stop_at(1)
                    modcol, r_modcol = sbp('modcol', [128, 48], F32)
                    n1col, r_n1col = sbp('n1col', [128, 8], F32)
                    s1col, r_s1col = sbp('s1col', [128, 8], F32)
                    ga_bc, r_ga = sbp('ga_bc', [128, 1024], F32)
                    oncol, r_oncol = sbp('oncol', [128, 8], F32)
                    gains, r_gains = sbp('gains', [128, 6], F32)
                    esink, r_esink = sbp('esink', [128, 6], F32)
                    P.dma('sp', modcol[:], mod_d[l].rearrange("(k p) -> p k", p=128), reads=[dres['mod']], writes=[r_modcol], allow_slow_non_contiguous=True)
                    P.dma('sp', n1col[:], dram['n1col'][l], writes=[r_n1col])
                    P.dma('sp', ga_bc[:], mod_d[l, 2048:3072].partition_broadcast(128), reads=[dres['mod']], writes=[r_ga])
                    P.dma('sp', oncol[:], dram['onormcol'][l], writes=[r_oncol])
                    P.dma('sp', gains[:], dram['gains'][l], writes=[r_gains])
                    P.dma('sp', esink[:], dram['sinks'][l].partition_broadcast(128), writes=[r_esink])
                    P.op('act', lambda e: e.activation(out=esink[:], in_=esink[:], func=AF.Exp), reads=[r_esink], writes=[r_esink])
                    P.op('dve', lambda e: e.scalar_tensor_tensor(out=s1col[:], in0=modcol[:, 8:16], scalar=1.0, in1=n1col[:],
                                                                 op0=ALU.add, op1=ALU.mult), reads=[r_modcol, r_n1col], writes=[r_s1col])
                    win, r_win = sbp('win', [128, 8, 2066], BF16)
                    wout, r_wout = sbp('wout', [128, 8, 1024], BF16)
                    r_win_k = [Res('win%d' % kc) for kc in range(8)]
                    r_wout_k = [Res('wout%d' % kc) for kc in range(8)]
                    for kc in range(8):
                        for (a, b) in ((0, 1033), (1033, 2066)):
                            P.dma('pool', win[:, kc, a:b], dram['w_in'][l, kc * 128:(kc + 1) * 128, a:b], writes=[r_win_k[kc]])
                        P.dma('pool', wout[:, kc, :], dram['w_out'][l, kc * 128:(kc + 1) * 128, :], writes=[r_wout_k[kc]])

                    akT, r_akT = sbp('akT', [128, S], BF16)
                    ck1T, r_ck1T = sbp('ck1T', [128, S], BF16)
                    ck2T, r_ck2T = sbp('ck2T', [128, 8, 128], BF16)
                    av, r_av = sbp('av', [128, 32, 2, 66], BF16)
                    cv1, r_cv1 = sbp('cv1', [128, 32, 2, 66], BF16)
                    cv2, r_cv2 = sbp('cv2', [128, 8, 2, 66], BF16)
                    kcmpT, r_kcmpT = sbp('kcmpT', [128, 256], BF16)
                    vcmp, r_vcmp = sbp('vcmp', [128, 2, 2, 130], BF16)
                    k0T, r_k0T = sbp('k0T', [128, 528], BF16)
                    v0T, r_v0T = sbp('v0T', [128, 528], BF16)
                    P.op('pool', lambda e: e.memset(av[:, :, :, 64:65], 1.0), writes=[r_av])
                    P.op('pool', lambda e: e.memset(cv1[:, :, :, 64:65], 1.0), writes=[r_cv1])
                    P.op('pool', lambda e: e.memset(cv2[:, :, :, 64:65], 1.0), writes=[r_cv2])
                    P.op('pool', lambda e: e.memset(k0T[:, 0:16], 0.0), writes=[r_k0T])
                    P.op('pool', lambda e: e.memset(v0T[:, 0:16], 0.0), writes=[r_v0T])
                    for h in range(2):
                        P.dma('pool', vcmp[:, :, h, 64:129], dram['vaugc'], writes=[r_vcmp])

                    w1buf, r_w1buf = sbp('w1buf', [128, 16, 256], BF16)
                    peT, r_peT = sbp('peT', [128, 2, 34], BF16)
                    b1col, r_b1col = sbp('b1col', [128, 2, 2], F32)
                    bias1, r_bias1 = sbp('bias1', [128, 2, 2], F32)
                    w2c, r_w2c = sbp('w2c', [128, 2, 2, 64], BF16)
                    b2kcol, r_b2k = sbp('b2kcol', [128, 1], F32)
                    b2v_bc, r_b2v = sbp('b2v_bc', [128, 128], F32)
                    P.op('pool', lambda e: e.memset(peT[:], 0.0), writes=[r_peT])
                    P.dma('pool', peT[:, :, 0:32], dram['cmp_peT'][l].rearrange("k p l -> p k l"), writes=[r_peT])
                    P.dma('sp', b1col[:], dram['cmp_b1col'][l], writes=[r_b1col])
                    P.dma('pool', w2c[:], dram['cmp_w2'][l], writes=[r_w2c])
                    P.dma('sp', b2kcol[:], dram['cmp_b2kcol'][l], writes=[r_b2k])
                    P.dma('sp', b2v_bc[:], dram['cmp_b2v'][l].partition_broadcast(128), writes=[r_b2v])

                    xin = [sbp('xin%d' % i, [128, 1024], F32) for i in range(2)]
                    xh, r_xh = sbp('xh', [128, 1024], BF16)
                    ss1, r_ss1 = sbp('ss1', [128, 1], F32)
                    ms1, r_ms1 = sbp('ms1', [128, 1], F32)
                    rs1, r_rs1 = sbp('rs1', [128, 1], F32)
                    hT, r_hT = sbp('hT', [128, 8, 512], BF16)
                    aqT, r_aqT = sbp('aqT', [128, 3, 512], BF16)
                    cqT, r_cqT = sbp('cqT', [128, 3, 512], BF16)
                    suT, r_suT = sbp('suT', [128, 2, 512], BF16)
                    sg, r_sg = sbp('sg', [128, 4, 18], F32)
                    nsq, r_nsq = sbp('nsq', [128, 512], BF16)
                    nrs, r_nrs = sbp('nrs', [128, 512], F32)
                    kraw, r_kraw = sbp('kraw', [128, 32], F32)
                    hidT, r_hidT = sbp('hidT', [128, 2, 2, 32], BF16)
                    mixBT, r_mixBT = sbp('mixBT', [128, 2, 512], BF16)

                    def qknorm(src_ap, src_res, n, gcol, dst_ap, dst_res, three_d=False):
                        v = (lambda a: a.rearrange("p (a b) -> p a b", a=4)) if three_d else (lambda a: a)
                        P.op('act', lambda e: e.activation(out=v(nsq[:, 0:n]), in_=src_ap, func=AF.Square), reads=[src_res], writes=[r_nsq])
                        pst, r_ps = bank()
                        P.op('pe', lambda e: e.matmul(pst[:, 0:n], lhsT=blk64[:], rhs=nsq[:, 0:n], start=True, stop=True),
                             reads=[r_blk64, r_nsq], writes=[r_ps])
                        P.op('act', lambda e: e.activation(out=nrs[:, 0:n], in_=pst[:, 0:n], func=AF.Sqrt, bias=eps_t[:], scale=1.0 / 64),
                             reads=[r_ps, r_eps], writes=[r_nrs])
                        P.op('dve', lambda e: e.reciprocal(out=nrs[:, 0:n], in_=nrs[:, 0:n]), reads=[r_nrs], writes=[r_nrs])
                        P.op('dve', lambda e: e.scalar_tensor_tensor(out=dst_ap, in0=src_ap, scalar=gcol, in1=v(nrs[:, 0:n]),
                                                                     op0=ALU.mult, op1=ALU.mult),
                             reads=[src_res, r_gains, r_nrs], writes=[dst_res])

                    act_dve = [0]

                    def norm_transpose(t, scol, shcol, r_cols, dstT, r_dstT, col0):
                        xt, r_xt = xin[t % 2]
                        P.dma('sp', xt[:], src_d[t * 128:(t + 1) * 128, :], reads=[out_res[t]], writes=[r_xt])
                        P.op('act', lambda e: e.activation(out=xh[:], in_=xt[:], func=AF.Square, accum_out=ss1[:]),
                             reads=[r_xt], writes=[r_xh, r_ss1])
                        emit_rstd(ss1[:, 0:1], r_ss1, ms1, r_ms1, rs1, r_rs1, 1, 1.0 / 1024)
                        P.op('dve', lambda e: e.tensor_scalar(out=xh[:], in0=xt[:], scalar1=rs1[:, 0:1], scalar2=None, op0=ALU.mult),
                             reads=[r_xt, r_rs1], writes=[r_xh])
                        pst, r_ps = bank()
                        psb = pst[:].bitcast(BF16)
                        for kc in range(8):
                            P.op('pe', lambda e, kc=kc: e.transpose(out=psb[:, kc * 128:(kc + 1) * 128], in_=xh[:, kc * 128:(kc + 1) * 128],
                                                                    identity=ident_b[:]), reads=[r_xh, r_ident_b], writes=[r_ps])
                        for kc in range(8):
                            eng = 'act' if (kc % 2 == 0) else 'dve'
                            if eng == 'act':
                                P.op('act', lambda e, kc=kc: e.activation(out=dstT[:, kc, col0:col0 + 128], in_=psb[:, kc * 128:(kc + 1) * 128],
                                                                          func=AF.Identity, bias=shcol[:, kc:kc + 1], scale=scol[:, kc:kc + 1]),
                                     reads=[r_ps] + r_cols, writes=[r_dstT])
                            else:
                                P.op('dve', lambda e, kc=kc: e.tensor_scalar(out=dstT[:, kc, col0:col0 + 128], in0=psb[:, kc * 128:(kc + 1) * 128],
                                                                             scalar1=scol[:, kc:kc + 1], scalar2=shcol[:, kc:kc + 1],
                                                                             op0=ALU.mult, op1=ALU.add),
                                     reads=[r_ps] + r_cols, writes=[r_dstT])

                    pT = [sbp('pT%d' % i, [128, 384], BF16) for i in range(4)]
                    pT_rr = [0]
                    o_a, r_oa = sbp('o_a', [128, 6, 64], F32)
                    o_c, r_oc = sbp('o_c', [128, 6, 64], F32)
                    tmp3, r_tmp3 = sbp('tmp3', [128, 3, 64], F32)
                    den3, r_den3 = sbp('den3', [128, 3], F32)
                    rc3, r_rc3 = sbp('rc3', [128, 3], F32)
                    w3, r_w3 = sbp('w3', [128, 3], F32)
                    impf, r_impf = sbp('impf', [128, 64], F32)
                    imp2, r_imp2 = sbp('imp2', [128, 64], F32)
                    m8a, r_m8a = sbp('m8a', [128, 8], F32)
                    m8b, r_m8b = sbp('m8b', [128, 8], F32)
                    selb, r_selb = sbp('selb', [128, 64], BF16)
                    selbT, r_selbT = sbp('selbT', [128, 3, 128], BF16)
                    P.op('pool', lambda e: e.memset(selbT[:], 0.0), writes=[r_selbT])
                    ssn, r_ssn = sbp('ssn', [128, 1], F32)
                    msn, r_msn = sbp('msn', [128, 1], F32)
                    rsn, r_rsn = sbp('rsn', [128, 1], F32)
                    mixn, r_mixn = sbp('mixn', [128, 384], BF16)
                    mixT, r_mixT = sbp('mixT', [128, 8, 128], BF16)
                    xres = xin

                    def attn(h, q_ap, r_q, keys, vfn, r_v, acc, ncols):
                        acct, r_acc = acc
                        n = len(keys)
                        staged = []

                        def score(j):
                            kap, kres, nk, bias = keys[j]
                            pst, r_ps = bank()
                            P.op('pe', lambda e: e.matmul(pst[0:nk, 0:384], lhsT=kap, rhs=q_ap, start=True, stop=(bias is None)),
                                 reads=[kres, r_q], writes=[r_ps])
                            if bias is not None:
                                if bias[0] == 'id':
                                    P.op('pe', lambda e: e.matmul(pst[0:nk, 0:384], lhsT=ident_b[0:nk, 0:nk], rhs=bias[1][0:nk, :], start=False, stop=True),
                                         reads=[r_ident_b, bias[2]], writes=[r_ps])
                                elif bias[0] == 'cmp':
                                    sh = bias[1]
                                    P.op('pe', lambda e: e.matmul(pst[0:nk, 0:384], lhsT=iw2[:, sh:sh + nk], rhs=cbrel3[:], start=False, stop=True),
                                         reads=[r_iw2, r_cbrel3], writes=[r_ps])
                                else:
                                    kt = bias[1]
                                    P.op('pe', lambda e: e.matmul(pst[0:nk, 0:384], lhsT=ewide[:, kt * 128:(kt + 1) * 128],
                                                                  rhs=selbT[:].rearrange("p g q -> p (g q)"), start=False, stop=True),
                                         reads=[r_ewide, r_selbT], writes=[r_ps])
                            pt, r_pt = pT[pT_rr[0] % 4]
                            pT_rr[0] += 1
                            P.op('act', lambda e: e.activation(out=pt[0:nk, :], in_=pst[0:nk, 0:384], func=AF.Exp, scale=0.125),
                                 reads=[r_ps], writes=[r_pt])
                            staged.append((pt, r_pt, nk))

                        def pv(j):
                            pt, r_pt, nk = staged[j]
                            vap = vfn(j)
                            for g in range(3):
                                P.op('pe', lambda e, g=g: e.matmul(acct[:, g * ncols:(g + 1) * ncols], lhsT=pt[0:nk, g * 128:(g + 1) * 128], rhs=vap,
                                                                   start=(j == 0 and g == 0), stop=(j == n - 1), skip_group_check=True),
                                     reads=[r_pt, r_v], writes=[r_acc])
                        for j in range(n):
                            score(j)
                            if j >= 1:
                                pv(j - 1)
                        pv(n - 1)

                    stop_at(2)
                    for m in range(n_macro):
                        for r in range(4):
                            norm_transpose(4 * m + r, s1col, modcol[:, 0:8], [r_s1col, r_modcol], hT, r_hT, r * 128)
                        stop_at(3)
                        for ci in range(13):
                            pst, r_ps = bank()
                            for kc in range(8):
                                P.op('pe', lambda e, kc=kc, ci=ci, pst=pst: e.matmul(pst[:, :], lhsT=win[:, kc, ci * 128:(ci + 1) * 128], rhs=hT[:, kc, :],
                                                                                     start=(kc == 0), stop=(kc == 7)), reads=[r_win_k[kc], r_hT], writes=[r_ps])
                            cols = slice(m * 512, (m + 1) * 512)
                            if ci < 3:
                                qknorm(pst[:, :], r_ps, 512, gains[:, 0:1], aqT[:, ci, :], r_aqT)
                            elif ci == 3:
                                qknorm(pst[:, :], r_ps, 512, gains[:, 1:2], akT[:, cols], r_akT)
                            elif ci < 7:
                                qknorm(pst[:, :], r_ps, 512, gains[:, 2:3], cqT[:, ci - 4, :], r_cqT)
                            elif ci == 7:
                                P.op('act', lambda e, pst=pst: e.copy(out=k0T[:, 16:528], in_=pst[:, :]), reads=[r_ps], writes=[r_k0T])
                            elif ci == 8:
                                qknorm(pst[:, :], r_ps, 512, gains[:, 4:5], ck1T[:, cols], r_ck1T)
                            elif ci == 9:
                                qknorm(pst[:, :], r_ps, 512, gains[:, 5:6], ck2T[:, (4 * m) % 8:(4 * m) % 8 + 4, :].rearrange("p a b -> p (a b)"), r_ck2T)
                            elif ci < 12:
                                P.op('act', lambda e, pst=pst, ci=ci: e.copy(out=suT[:, ci - 10, :].rearrange("p (t k) -> p k t", t=8), in_=pst[:, :].rearrange("p (k t) -> p k t", t=8)), reads=[r_ps], writes=[r_suT])
                            else:
                                P.op('dve', lambda e, pst=pst: e.tensor_copy(out=v0T[:, 16:528], in_=pst[:, :]), reads=[r_ps], writes=[r_v0T])
                        stop_at(4)
                        import os as _os
                        for r in range(int(_os.environ.get('RSKIP', 0)), int(_os.environ.get('RLIM', 4))):
                            t = 4 * m + r
                            psa, r_psa = bank()
                            psb_, r_psb = bank()
                            for kc in range(8):
                                P.op('pe', lambda e, kc=kc, psa=psa, r=r: e.matmul(psa[:, 0:128], lhsT=hT[:, kc, r * 128:(r + 1) * 128], rhs=win[:, kc, 1664:1792],
                                                                                   start=(kc == 0), stop=(kc == 7)), reads=[r_win_k[kc], r_hT], writes=[r_psa])
                            for kc in range(8):
                                P.op('pe', lambda e, kc=kc, psb_=psb_, r=r: e.matmul(psb_[:, 0:274], lhsT=hT[:, kc, r * 128:(r + 1) * 128], rhs=win[:, kc, 1792:2066],
                                                                                     start=(kc == 0), stop=(kc == 7)), reads=[r_win_k[kc], r_hT], writes=[r_psb])
                            stop_at(4.2)
                            if not _os.environ.get('NOAV'):
                              P.op('act', lambda e, psa=psa, t=t: e.copy(out=av[:, t, :, 0:64], in_=psa[:, 0:128].rearrange("p (h d) -> p h d", d=64)),
                                 reads=[r_psa], writes=[r_av])
                            stop_at(4.4)
                            if not _os.environ.get('NOCV1'):
                              P.op('dve', lambda e, psb_=psb_, t=t: e.tensor_copy(out=cv1[:, t, :, 0:64], in_=psb_[:, 0:128].rearrange("p (h d) -> p h d", h=2)),
                                 reads=[r_psb], writes=[r_cv1])
                            if not _os.environ.get('NOCV2'):
                              P.op('dve', lambda e, psb_=psb_, t=t: e.tensor_copy(out=cv2[:, t % 8, :, 0:64], in_=psb_[:, 128:256].rearrange("p (h d) -> p h d", h=2)),
                                 reads=[r_psb], writes=[r_cv2])
                            stop_at(4.6)
                            if _os.environ.get('NOGATE'):
                                continue
                            P.op('act', lambda e, psb_=psb_, r=r: e.activation(out=sg[:, r, :], in_=psb_[:, 256:274], func=AF.Exp, scale=-1.0),
                                 reads=[r_psb], writes=[r_sg])
                            P.op('dve', lambda e, r=r: e.tensor_scalar(out=sg[:, r, :], in0=sg[:, r, :], scalar1=1.0, scalar2=None, op0=ALU.add),
                                 reads=[r_sg], writes=[r_sg])
                            P.op('dve', lambda e, r=r: e.reciprocal(out=sg[:, r, :], in_=sg[:, r, :]), reads=[r_sg], writes=[r_sg])
                            stop_at(4.9)
                        stop_at(5)
                        pb = 32 * (m % 4)
                        tl = m // 4
                        for kv in range(2):
                            srcT, r_src = (k0T, r_k0T) if kv == 0 else (v0T, r_v0T)
                            psHs = [bank(), bank()]
                            if m == 0:
                                psB, r_psB = bank()
                            for half in range(2):
                                for lq in range(4):
                                    l0 = half * 16 + lq * 4
                                    P.dma('pool', w1buf[:, lq * 4:(lq + 1) * 4, :], dram['cmp_w1'][l, kv, :, l0:l0 + 4, :], writes=[r_w1buf])
                                stop_at(5.1)
                                if m == 0:
                                    for ht in range(2):
                                        for lq in range(16):
                                            L_ = half * 16 + lq
                                            P.op('pe', lambda e, psB=psB, lq=lq, ht=ht, kv=kv, L_=L_, half=half: e.matmul(
                                                psB[:, ht * 2:ht * 2 + 2], lhsT=w1buf[0:64, lq, ht * 128:(ht + 1) * 128], rhs=peT[0:64, kv, L_:L_ + 2],
                                                start=(half == 0 and ht == 0 and lq == 0), stop=(half == 1 and lq == 15), skip_group_check=True),
                                                reads=[r_w1buf, r_peT], writes=[r_psB])
                                stop_at(5.2)
                                for hh in range(2):
                                    for ht in range(2):
                                        grp = hh * 2 + ht
                                        for lq in range(16):
                                            L_ = half * 16 + lq
                                            psH, r_psH = psHs[hh]
                                            P.op('pe', lambda e, psH=psH, lq=lq, ht=ht, hh=hh, srcT=srcT, grp=grp, L_=L_, half=half: e.matmul(
                                                psH[:, ht * 32:(ht + 1) * 32], lhsT=w1buf[hh * 64:(hh + 1) * 64, lq, ht * 128:(ht + 1) * 128],
                                                rhs=srcT[hh * 64:(hh + 1) * 64, L_:L_ + 497:16],
                                                start=(half == 0 and ht == 0 and lq == 0), stop=(half == 1 and lq == 15), skip_group_check=True),
                                                reads=[r_w1buf, r_src], writes=[r_psH])
                            stop_at(5.3)
                            if m == 0:
                                P.op('dve', lambda e, psB=psB, kv=kv: e.tensor_tensor(out=bias1[:, kv, :], in0=psB[:, 0:4:2], in1=b1col[:, kv, :], op=ALU.add),
                                     reads=[r_psB, r_b1col], writes=[r_bias1])
                            stop_at(5.4)
                            for hh in range(2):
                                for ht in range(2):
                                    grp = hh * 2 + ht
                                    psH, r_psH = psHs[hh]
                                    P.op('act', lambda e, psH=psH, hh=hh, ht=ht, kv=kv, grp=grp: e.activation(
                                        out=hidT[:, hh, ht, :], in_=psH[:, ht * 32:(ht + 1) * 32], func=AF.Gelu_apprx_tanh,
                                        bias=bias1[:, kv, ht:ht + 1], scale=1.0), reads=[r_psH, r_bias1], writes=[r_hidT])
                            stop_at(5.5)
                            if kv == 0:
                                pst, r_ps = bank()
                                for hh in range(2):
                                    for ht in range(2):
                                        P.op('pe', lambda e, pst=pst, hh=hh, ht=ht: e.matmul(pst[hh * 64:(hh + 1) * 64, 0:32], lhsT=w2c[:, 0, ht, :], rhs=hidT[:, hh, ht, :],
                                                                                            start=(ht == 0), stop=(ht == 1)), reads=[r_w2c, r_hidT], writes=[r_ps])
                                P.op('dve', lambda e, pst=pst: e.tensor_scalar(out=kraw[:], in0=pst[:, 0:32], scalar1=b2kcol[:, 0:1], scalar2=None, op0=ALU.add),
                                     reads=[r_ps, r_b2k], writes=[r_kraw])
                                stop_at(5.6)
                                qknorm(kraw[:], r_kraw, 32, gains[:, 3:4], kcmpT[:, 32 * m:32 * m + 32], r_kcmpT)
                                stop_at(5.7)
                            else:
                                pst, r_ps = bank()
                                for hh in range(2):
                                    for ht in range(2):
                                        P.op('pe', lambda e, pst=pst, hh=hh, ht=ht: e.matmul(pst[pb:pb + 32, hh * 64:(hh + 1) * 64], lhsT=hidT[:, hh, ht, :], rhs=w2c[:, 1, ht, :],
                                                                                            start=(ht == 0), stop=(ht == 1), tile_position=(0, pb)), reads=[r_w2c, r_hidT], writes=[r_ps])
                                P.op('dve', lambda e, pst=pst: e.tensor_tensor(out=vcmp[pb:pb + 32, tl, :, 0:64],
                                                                               in0=pst[pb:pb + 32, 0:128].rearrange("p (h d) -> p h d", d=64),
                                                                               in1=b2v_bc[pb:pb + 32, :].rearrange("p (h d) -> p h d", d=64), op=ALU.add),
                                     reads=[r_ps, r_b2v], writes=[r_vcmp])
                                if m == 0:
                                    P.op('dve', lambda e: e.memset(vcmp[0:1, 0, :, 0:64], 0.0), writes=[r_vcmp])
                        P.op('pool', lambda e: e.tensor_copy(out=k0T[:, 0:16], in_=k0T[:, 512:528]), reads=[r_k0T], writes=[r_k0T])
                        P.op('pool', lambda e: e.tensor_copy(out=v0T[:, 0:16], in_=v0T[:, 512:528]), reads=[r_v0T], writes=[r_v0T])

                        stop_at(6)
                        s5_macro(nc, P, K, s5, suT, r_suT, mixBT, r_mixBT, oncol, r_oncol, banks, bank, ones_b, r_ones_b, eps_t, r_eps, m,
                                 dbg_d if (dbg and l == 0) else None, dbg_res)

                        stop_at(7)
                        for r in range(4):
                            i = 4 * m + r
                            qs = slice(r * 128, (r + 1) * 128)
                            for h in range(2):
                                hs = slice(h * 64, (h + 1) * 64)
                                aq_ap = aqT[hs, :, qs]
                                cq_ap = cqT[hs, :, qs]
                                nsl = 32 * ((8 * i + 8 + 31) // 32)
                                Tf = (8 * i + 7) // 128
                                keys = []
                                for T in range(Tf + 1):
                                    nk = min(128, nsl - 128 * T)
                                    bias = ('cmp', 224 - 8 * i + 128 * T) if T == Tf else None
                                    keys.append((kcmpT[hs, T * 128:T * 128 + nk], r_kcmpT, nk, bias))
                                attn(h, cq_ap, r_cqT, keys, lambda j, keys=keys, h=h: vcmp[0:keys[j][2], j, h, 0:129], r_vcmp, ACC_CMP, 129)
                                acc, r_acc = ACC_CMP
                                a3 = acc[:, 0:387].rearrange("p (g c) -> p g c", g=3)
                                P.op('dve', lambda e, a3=a3: e.tensor_scalar(out=den3[:], in0=a3[:, :, 64], scalar1=1e-30, scalar2=None, op0=ALU.add),
                                     reads=[r_acc], writes=[r_den3])
                                P.op('dve', lambda e: e.reciprocal(out=rc3[:], in_=den3[:]), reads=[r_den3], writes=[r_rc3])
                                for g in range(3):
                                    if g == 0:
                                        P.op('dve', lambda e, a3=a3: e.tensor_scalar(out=impf[:], in0=a3[:, 0, 65:129], scalar1=rc3[:, 0:1], scalar2=None, op0=ALU.mult),
                                             reads=[r_acc, r_rc3], writes=[r_impf])
                                    else:
                                        P.op('dve', lambda e, a3=a3, g=g: e.scalar_tensor_tensor(out=impf[:], in0=a3[:, g, 65:129], scalar=rc3[:, g:g + 1], in1=impf[:],
                                                                                                 op0=ALU.mult, op1=ALU.add), reads=[r_acc, r_rc3, r_impf], writes=[r_impf])
                                gi = h * 9
                                P.op('dve', lambda e, r=r, gi=gi: e.tensor_tensor(out=w3[:], in0=sg[:, r, gi:gi + 9:3], in1=rc3[:], op=ALU.mult),
                                     reads=[r_sg, r_rc3], writes=[r_w3])
                                P.op('dve', lambda e, a3=a3, h=h: e.tensor_tensor(out=o_c[:, h * 3:(h + 1) * 3, :], in0=a3[:, :, 0:64],
                                                                                  in1=w3[:].unsqueeze(2).to_broadcast([128, 3, 64]), op=ALU.mult),
                                     reads=[r_acc, r_w3], writes=[r_oc])
                                P.op('dve', lambda e, i=i: e.tensor_tensor(out=impf[:], in0=impf[:], in1=fbwide[:, 62 - 2 * i:126 - 2 * i], op=ALU.add),
                                     reads=[r_impf, r_fbwide], writes=[r_impf])
                                P.op('dve', lambda e: e.tensor_scalar(out=impf[:, 0:1], in0=impf[:, 0:1], scalar1=1e4, scalar2=None, op0=ALU.add),
                                     reads=[r_impf], writes=[r_impf])
                                if dbg and l == 0:
                                    P.dma('sp', dbg_d['imp'][i * 128:(i + 1) * 128, h * 64:(h + 1) * 64], impf[:], reads=[r_impf], writes=[dbg_res])
                                P.op('dve', lambda e: e.max(out=m8a[:], in_=impf[:]), reads=[r_impf], writes=[r_m8a])
                                P.op('dve', lambda e: e.match_replace(out=imp2[:], in_to_replace=m8a[:], in_values=impf[:], imm_value=-3e38),
                                     reads=[r_m8a, r_impf], writes=[r_imp2])
                                P.op('dve', lambda e: e.max(out=m8b[:], in_=imp2[:]), reads=[r_imp2], writes=[r_m8b])
                                P.op('dve', lambda e: e.tensor_scalar(out=imp2[:], in0=impf[:], scalar1=m8b[:, 7:8], scalar2=None, op0=ALU.is_ge),
                                     reads=[r_impf, r_m8b], writes=[r_imp2])
                                P.op('dve', lambda e: e.tensor_scalar(out=selb[:], in0=imp2[:], scalar1=-NEGB, scalar2=NEGB, op0=ALU.mult, op1=ALU.add),
                                     reads=[r_imp2], writes=[r_selb])
                                pst, r_ps = bank()
                                psb = pst[:].bitcast(BF16)
                                P.op('pe', lambda e, psb=psb: e.transpose(out=psb[0:64, 0:128], in_=selb[:], identity=ident_b[:]),
                                     reads=[r_selb, r_ident_b], writes=[r_ps])
                                P.op('act', lambda e, psb=psb: e.copy(out=selbT[0:64, :, :], in_=psb[0:64, 0:128].unsqueeze(1).to_broadcast([64, 3, 128])),
                                     reads=[r_ps], writes=[r_selbT])
                                keys = []
                                if i > 0:
                                    keys.append((akT[hs, (i - 1) * 128:i * 128], r_akT, 128, ('id', anti3, r_anti3)))
                                keys.append((akT[hs, i * 128:(i + 1) * 128], r_akT, 128, ('id', caus3, r_caus3)))
                                base = i - (len(keys) - 1)
                                attn(h, aq_ap, r_aqT, keys, lambda j, base=base, h=h: av[:, base + j, h, 0:65], r_av, ACC_A, 65)
                                acc, r_acc = ACC_A
                                a3 = acc[:, 0:195].rearrange("p (g c) -> p g c", g=3)
                                P.op('dve', lambda e, a3=a3, h=h: e.tensor_tensor(out=den3[:], in0=a3[:, :, 64], in1=esink[:, h * 3:(h + 1) * 3], op=ALU.add),
                                     reads=[r_acc, r_esink], writes=[r_den3])
                                P.op('dve', lambda e: e.reciprocal(out=rc3[:], in_=den3[:]), reads=[r_den3], writes=[r_rc3])
                                P.op('dve', lambda e, a3=a3, h=h: e.tensor_tensor(out=o_a[:, h * 3:(h + 1) * 3, :], in0=a3[:, :, 0:64],
                                                                                  in1=rc3[:].unsqueeze(2).to_broadcast([128, 3, 64]), op=ALU.mult),
                                     reads=[r_acc, r_rc3], writes=[r_oa])
                                keys = []
                                lo = max(0, i - 4)
                                for kt in range(lo, i + 1):
                                    if kt == i:
                                        bias = ('id', caus3, r_caus3)
                                    elif kt == i - 4:
                                        bias = ('id', anti3, r_anti3)
                                    else:
                                        bias = None
                                    keys.append((ck2T[hs, kt % 8, :], r_ck2T, 128, bias))
                                attn(h, cq_ap, r_cqT, keys, lambda j, lo=lo, h=h: cv2[:, (lo + j) % 8, h, 0:65], r_cv2, ACC_WIN, 65)
                                for (accb, br) in ((ACC_WIN, 2),):
                                    acc, r_acc = accb
                                    a3 = acc[:, 0:195].rearrange("p (g c) -> p g c", g=3)
                                    P.op('dve', lambda e, a3=a3: e.reciprocal(out=rc3[:], in_=a3[:, :, 64]), reads=[r_acc], writes=[r_rc3])
                                    P.op('dve', lambda e, r=r, gi=gi, br=br: e.tensor_tensor(out=w3[:], in0=sg[:, r, gi + br:gi + 9:3], in1=rc3[:], op=ALU.mult),
                                         reads=[r_sg, r_rc3], writes=[r_w3])
                                    P.op('dve', lambda e, a3=a3: e.tensor_tensor(out=tmp3[:], in0=a3[:, :, 0:64], in1=w3[:].unsqueeze(2).to_broadcast([128, 3, 64]), op=ALU.mult),
                                         reads=[r_acc, r_w3], writes=[r_tmp3])
                                    P.op('pool', lambda e, h=h: e.tensor_tensor(out=o_c[:, h * 3:(h + 1) * 3, :], in0=o_c[:, h * 3:(h + 1) * 3, :], in1=tmp3[:], op=ALU.add),
                                         reads=[r_tmp3, r_oc], writes=[r_oc])
                                keys = []
                                for kt in range(0, i + 1):
                                    bias = ('id', caus3, r_caus3) if kt == i else ('sel', kt)
                                    keys.append((ck1T[hs, kt * 128:(kt + 1) * 128], r_ck1T, 128, bias))
                                attn(h, cq_ap, r_cqT, keys, lambda j, h=h: cv1[:, j, h, 0:65], r_cv1, ACC_SEL, 65)
                                acc, r_acc = ACC_SEL
                                a3 = acc[:, 0:195].rearrange("p (g c) -> p g c", g=3)
                                P.op('dve', lambda e, a3=a3: e.reciprocal(out=rc3[:], in_=a3[:, :, 64]), reads=[r_acc], writes=[r_rc3])
                                P.op('dve', lambda e, r=r, gi=gi: e.tensor_tensor(out=w3[:], in0=sg[:, r, gi + 1:gi + 9:3], in1=rc3[:], op=ALU.mult),
                                     reads=[r_sg, r_rc3], writes=[r_w3])
                                P.op('dve', lambda e, a3=a3: e.tensor_tensor(out=tmp3[:], in0=a3[:, :, 0:64], in1=w3[:].unsqueeze(2).to_broadcast([128, 3, 64]), op=ALU.mult),
                                     reads=[r_acc, r_w3], writes=[r_tmp3])
                                P.op('pool', lambda e, h=h: e.tensor_tensor(out=o_c[:, h * 3:(h + 1) * 3, :], in0=o_c[:, h * 3:(h + 1) * 3, :], in1=tmp3[:], op=ALU.add),
                                     reads=[r_tmp3, r_oc], writes=[r_oc])
                            if dbg and l == 0:
                                P.dma('sp', dbg_d['oa'][i * 128:(i + 1) * 128, :], o_a[:].rearrange("p a b -> p (a b)"), reads=[r_oa], writes=[dbg_res])
                                P.dma('sp', dbg_d['oc'][i * 128:(i + 1) * 128, :], o_c[:].rearrange("p a b -> p (a b)"), reads=[r_oc], writes=[dbg_res])
                            for (ot, r_ot, c0, kc0) in ((o_a, r_oa, 0, 0), (o_c, r_oc, 640, 5)):
                                of = ot[:].rearrange("p a b -> p (a b)")
                                P.op('act', lambda e, of=of: e.activation(out=mixn[:], in_=of, func=AF.Square, accum_out=ssn[:]),
                                     reads=[r_ot], writes=[r_mixn, r_ssn])
                                emit_rstd(ssn[:, 0:1], r_ssn, msn, r_msn, rsn, r_rsn, 1, 1.0 / 384)
                                P.op('dve', lambda e, of=of: e.tensor_scalar(out=mixn[:], in0=of, scalar1=rsn[:, 0:1], scalar2=None, op0=ALU.mult),
                                     reads=[r_ot, r_rsn], writes=[r_mixn])
                                pst, r_ps = bank()
                                psb = pst[:].bitcast(BF16)
                                for k3 in range(3):
                                    P.op('pe', lambda e, psb=psb, k3=k3: e.transpose(out=psb[:, k3 * 128:(k3 + 1) * 128], in_=mixn[:, k3 * 128:(k3 + 1) * 128], identity=ident_b[:]),
                                         reads=[r_mixn, r_ident_b], writes=[r_ps])
                                for k3 in range(3):
                                    P.op('act', lambda e, psb=psb, kc0=kc0, k3=k3: e.activation(out=mixT[:, kc0 + k3, :], in_=psb[:, k3 * 128:(k3 + 1) * 128], func=AF.Identity,
                                                                                               scale=oncol[:, kc0 + k3:kc0 + k3 + 1]),
                                         reads=[r_ps, r_oncol], writes=[r_mixT])
                            po = [bank(), bank()]
                            for n in range(2):
                                pst, r_ps = po[n]
                                for kc in range(8):
                                    if kc in (3, 4):
                                        lhs = mixBT[:, kc - 3, qs]
                                        rl = r_mixBT
                                    else:
                                        lhs = mixT[:, kc, :]
                                        rl = r_mixT
                                    P.op('pe', lambda e, pst=pst, lhs=lhs, kc=kc, n=n: e.matmul(pst[:, :], lhsT=lhs, rhs=wout[:, kc, n * 512:(n + 1) * 512],
                                                                                              start=(kc == 0), stop=(kc == 7)), reads=[rl, r_wout_k[kc]], writes=[r_ps])
                            xr, r_xr = xres[i % 2]
                            P.dma('sp', xr[:], src_d[i * 128:(i + 1) * 128, :], reads=[out_res[i]], writes=[r_xr])
                            for n in range(2):
                                pst, r_ps = po[n]
                                P.op('dve', lambda e, pst=pst, n=n: e.tensor_tensor(out=pst[:, :], in0=pst[:, :], in1=ga_bc[:, n * 512:(n + 1) * 512], op=ALU.mult),
                                     reads=[r_ps, r_ga], writes=[r_ps])
                                P.op('dve', lambda e, pst=pst, n=n, xr=xr: e.tensor_tensor(out=xr[:, n * 512:(n + 1) * 512], in0=pst[:, :], in1=xr[:, n * 512:(n + 1) * 512], op=ALU.add),
                                     reads=[r_ps, r_xr], writes=[r_xr])
                            P.dma('sp', out_d[i * 128:(i + 1) * 128, :], xr[:], reads=[r_xr], writes=[out_res[i]])
                        if m % 2 == 1 or m == n_macro - 1:
                            P.barrier()

                if do_ffn:
                    with ExitStack() as ph:
                        sbp = lambda name, shape, dt: K.sb(ph, name, shape, dt)
                        w1s, r_w1s = sbp('w1s', [128, 8, 4096], BF16)
                        w2s, r_w2s = sbp('w2s', [128, 32, 1024], BF16)
                        r_w1s_k = [Res('w1s%d' % kc) for kc in range(8)]
                        r_w2s_k = [Res('w2s%d' % kc) for kc in range(32)]
                        for kc in range(8):
                            for cc in range(2):
                                P.dma('pool', w1s[:, kc, cc * 2048:(cc + 1) * 2048], dram['w_ff1'][l, kc * 128:(kc + 1) * 128, cc * 2048:(cc + 1) * 2048], writes=[r_w1s_k[kc]])
                        for kc in range(32):
                            P.dma('pool', w2s[:, kc, :], dram['w_ff2'][l, kc * 128:(kc + 1) * 128, :], writes=[r_w2s_k[kc]])
                        modcol, r_modcol = sbp('modcol', [128, 48], F32)
                        n2col, r_n2col = sbp('n2col', [128, 8], F32)
                        s2col, r_s2col = sbp('s2col', [128, 8], F32)
                        ga_bc, r_ga = sbp('ga_bc', [128, 1024], F32)
                        P.dma('sp', modcol[:], mod_d[l].rearrange("(k p) -> p k", p=128), reads=[dres['mod']], writes=[r_modcol], allow_slow_non_contiguous=True)
                        P.dma('sp', n2col[:], dram['n2col'][l], writes=[r_n2col])
                        P.dma('sp', ga_bc[:], mod_d[l, 5120:6144].partition_broadcast(128), reads=[dres['mod']], writes=[r_ga])
                        P.op('dve', lambda e: e.scalar_tensor_tensor(out=s2col[:], in0=modcol[:, 32:40], scalar=1.0, in1=n2col[:],
                                                                     op0=ALU.add, op1=ALU.mult), reads=[r_modcol, r_n2col], writes=[r_s2col])
                        xin = [sbp('fxin%d' % i, [128, 1024], F32) for i in range(4)]
                        junk, r_junk = sbp('fjunk', [128, 1024], BF16)
                        xh, r_xh = sbp('fxh', [128, 1024], BF16)
                        ss1, r_ss1 = sbp('fss1', [128, 1], F32)
                        ms1, r_ms1 = sbp('fms1', [128, 1], F32)
                        rs1, r_rs1 = sbp('frs1', [128, 1], F32)
                        h2T, r_h2T = sbp('h2T', [128, 8, 256], BF16)
                        rl = [sbp('rl%d' % i, [128, 512], F32) for i in range(2)]
                        hid, r_hid = sbp('hid', [128, 32, 256], BF16)
                        xnew = [sbp('fxnew%d' % i, [128, 1024], F32) for i in range(2)]
                        for tb in range(16):
                            for r in range(2):
                                t = 2 * tb + r
                                xt, r_xt = xin[t % 4]
                                P.dma('sp', xt[:], out_d[t * 128:(t + 1) * 128, :], reads=[out_res[t]], writes=[r_xt])
                                P.op('act', lambda e, xt=xt: e.activation(out=junk[:], in_=xt[:], func=AF.Square, accum_out=ss1[:]),
                                     reads=[r_xt], writes=[r_junk, r_ss1])
                                emit_rstd(ss1[:, 0:1], r_ss1, ms1, r_ms1, rs1, r_rs1, 1, 1.0 / 1024)
                                P.op('dve', lambda e, xt=xt: e.tensor_scalar(out=xh[:], in0=xt[:], scalar1=rs1[:, 0:1], scalar2=None, op0=ALU.mult),
                                     reads=[r_xt, r_rs1], writes=[r_xh])
                                pst, r_ps = bank()
                                psb = pst[:].bitcast(BF16)
                                for kc in range(8):
                                    P.op('pe', lambda e, kc=kc, psb=psb: e.transpose(out=psb[:, kc * 128:(kc + 1) * 128], in_=xh[:, kc * 128:(kc + 1) * 128],
                                                                                    identity=ident_b[:]), reads=[r_xh, r_ident_b], writes=[r_ps])
                                for kc in range(8):
                                    if kc % 2 == 0:
                                        P.op('act', lambda e, kc=kc, psb=psb, r=r: e.activation(out=h2T[:, kc, r * 128:(r + 1) * 128], in_=psb[:, kc * 128:(kc + 1) * 128],
                                                                                               func=AF.Identity, bias=modcol[:, 24 + kc:25 + kc], scale=s2col[:, kc:kc + 1]),
                                             reads=[r_ps, r_modcol, r_s2col], writes=[r_h2T])
                                    else:
                                        P.op('dve', lambda e, kc=kc, psb=psb, r=r: e.tensor_scalar(out=h2T[:, kc, r * 128:(r + 1) * 128], in0=psb[:, kc * 128:(kc + 1) * 128],
                                                                                                  scalar1=s2col[:, kc:kc + 1], scalar2=modcol[:, 24 + kc:25 + kc],
                                                                                                  op0=ALU.mult, op1=ALU.add),
                                             reads=[r_ps, r_modcol, r_s2col], writes=[r_h2T])
                            for fp in range(16):
                                pst, r_ps = bank()
                                for j in range(2):
                                    f = 2 * fp + j
                                    for kc in range(8):
                                        P.op('pe', lambda e, pst=pst, j=j, f=f, kc=kc: e.matmul(pst[:, j * 256:(j + 1) * 256], lhsT=w1s[:, kc, f * 128:(f + 1) * 128], rhs=h2T[:, kc, :],
                                                                                              start=(kc == 0), stop=(kc == 7)), reads=[r_w1s_k[kc], r_h2T], writes=[r_ps])
                                rt, r_rt = rl[fp % 2]
                                P.op('act', lambda e, pst=pst, rt=rt: e.activation(out=rt[:], in_=pst[:, :], func=AF.Relu), reads=[r_ps], writes=[r_rt])
                                P.op('dve', lambda e, rt=rt, fp=fp: e.tensor_tensor(out=hid[:, 2 * fp:2 * fp + 2, :], in0=rt[:].rearrange("p (j t) -> p j t", j=2),
                                                                                    in1=rt[:].rearrange("p (j t) -> p j t", j=2), op=ALU.mult),
                                     reads=[r_rt], writes=[r_hid])
                            for r in range(2):
                                t = 2 * tb + r
                                xt, r_xt = xin[t % 4]
                                xn, r_xn = xnew[t % 2]
                                po = [banks[4 + 2 * r], banks[5 + 2 * r]]
                                for n in range(2):
                                    pst, r_ps = po[n]
                                    for f in range(32):
                                        P.op('pe', lambda e, pst=pst, f=f, n=n, r=r: e.matmul(pst[:, :], lhsT=hid[:, f, r * 128:(r + 1) * 128], rhs=w2s[:, f, n * 512:(n + 1) * 512],
                                                                                            start=(f == 0), stop=(f == 31)), reads=[r_hid, r_w2s_k[f]], writes=[r_ps])
                                for n in range(2):
                                    pst, r_ps = po[n]
                                    P.op('dve', lambda e, pst=pst, n=n, xn=xn: e.tensor_tensor(out=xn[:, n * 512:(n + 1) * 512], in0=pst[:, :], in1=ga_bc[:, n * 512:(n + 1) * 512], op=ALU.mult),
                                         reads=[r_ps, r_ga], writes=[r_xn])
                                P.op('pool', lambda e, xn=xn, xt=xt: e.tensor_tensor(out=xn[:], in0=xn[:], in1=xt[:], op=ALU.add), reads=[r_xn, r_xt], writes=[r_xn])
                                P.dma('sp', out_d[t * 128:(t + 1) * 128, :], xn[:], reads=[r_xn], writes=[out_res[t]])
                    P.barrier()

        body()
        P.stopped = False
        P.barrier(final=True)
        with nc.Block() as block:
            P.replay(block)
    return nc


TWO_PI = 6.283185307179586


def s5_setup(nc, P, K, ph, dram, l, banks, bank, ident_f, r_ident_f, dbg_d, dbg_res):
    sbp = lambda name, shape, dt: K.sb(ph, name, shape, dt)
    s5 = {}
    T1, r_T1 = sbp('T1', [128, 2, 8, 128], BF16)
    T2re, r_T2re = sbp('T2re', [128, 2, 8, 128], BF16)
    T2im, r_T2im = sbp('T2im', [128, 2, 8, 128], BF16)
    T3, r_T3 = sbp('T3', [128, 8, 8, 2, 32], BF16)
    UC, r_UC = sbp('UC', [128, 8, 64], F32)
    US, r_US = sbp('US', [128, 8, 64], F32)
    Rt, r_Rt = sbp('Rt', [128, 8], F32)
    G0re, r_G0re = sbp('G0re', [128, 8], F32)
    G0im, r_G0im = sbp('G0im', [128, 8], F32)
    wglu, r_wglu = sbp('wglu', [128, 2, 256], BF16)
    bglu, r_bglu = sbp('bglu', [128, 2], F32)
    s5.update(T1=T1, r_T1=r_T1, T2re=T2re, r_T2re=r_T2re, T2im=T2im, r_T2im=r_T2im, T3=T3, r_T3=r_T3, UC=UC, r_UC=r_UC,
              US=US, r_US=r_US, Rt=Rt, r_Rt=r_Rt, G0re=G0re, r_G0re=r_G0re, G0im=G0im, r_G0im=r_G0im,
              wglu=wglu, r_wglu=r_wglu, bglu=bglu, r_bglu=r_bglu)
    for nm, shp, dt in [('Wre', [128, 8, 64], F32), ('Wim', [128, 8, 64], F32), ('Gre', [128, 8, 64], F32), ('Gim', [128, 8, 64], F32),
                        ('tA', [128, 8, 64], F32), ('tB', [128, 8, 64], F32),
                        ('Hbre', [128, 8, 65], BF16), ('Hbim', [128, 8, 65], BF16),
                        ('hg', [128, 2, 512], F32), ('hgb', [128, 2, 512], BF16), ('sig', [128, 512], F32),
                        ('sq', [128, 2, 512], BF16)]:
        t, r = sbp(nm, shp, dt)
        s5[nm] = t
        s5['r_' + nm] = r
    for a, b in (('Hre', 'Wre'), ('Him', 'Wim'), ('ob', 'hg'), ('rsb', 'sig')):
        s5[a] = s5[b]
        s5['r_' + a] = s5['r_' + b]
    P.dma('pool', wglu[:], dram['s5_wglu'][l].rearrange("(c p) n -> p c n", p=128), writes=[r_wglu])
    P.dma('sp', bglu[:], dram['s5_bglucol'][l], writes=[r_bglu])
    P.op('dve', lambda e: e.memset(G0re[:], 0.0), writes=[r_G0re])
    P.op('dve', lambda e: e.memset(G0im[:], 0.0), writes=[r_G0im])

    with ExitStack() as tmp:
        tb = lambda name, shape, dt=F32: K.sb(tmp, name, shape, dt)
        are, r_are = tb('are', [128, 8])
        aim, r_aim = tb('aim', [128, 8])
        ls, r_ls = tb('ls', [128, 8])
        bre, r_bre = tb('bre', [128, 8, 16])
        bim, r_bim = tb('bim', [128, 8, 16])
        cre, r_cre = tb('cre', [128, 8, 16])
        cim, r_cim = tb('cim', [128, 8, 16])
        dcol, r_dcol = tb('dcol', [128, 2])
        for (t, r, nm) in [(are, r_are, 's5_are'), (aim, r_aim, 's5_aim'), (ls, r_ls, 's5_ls'), (bre, r_bre, 's5_bre'), (bim, r_bim, 's5_bim'),
                           (cre, r_cre, 's5_cre'), (cim, r_cim, 's5_cim'), (dcol, r_dcol, 's5_dcol')]:
            P.dma('sp', t[:], dram[nm][l], writes=[r])
        cnt = [0]

        def T(shape=[128, 8]):
            cnt[0] += 1
            return tb('t%d' % cnt[0], shape)

        def tt(out, o_r, a, a_r, b, b_r, op, eng='dve'):
            P.op(eng, lambda e: e.tensor_tensor(out=out, in0=a, in1=b, op=op), reads=[a_r, b_r], writes=[o_r])

        def ts(out, o_r, a, a_r, s1, s2, op0, op1=None):
            if op1 is None:
                P.op('dve', lambda e: e.tensor_scalar(out=out, in0=a, scalar1=s1, scalar2=None, op0=op0), reads=[a_r], writes=[o_r])
            else:
                P.op('dve', lambda e: e.tensor_scalar(out=out, in0=a, scalar1=s1, scalar2=s2, op0=op0, op1=op1), reads=[a_r], writes=[o_r])

        step, r_step = T()
        P.op('act', lambda e: e.activation(out=step[:], in_=ls[:], func=AF.Exp), reads=[r_ls], writes=[r_step])
        lre, r_lre = T()
        ts(lre[:], r_lre, are[:], r_are, -1e-4, None, ALU.min)
        lrs, r_lrs = T()
        tt(lrs[:], r_lrs, lre[:], r_lre, step[:], r_step, ALU.mult)
        mag, r_mag = T()
        P.op('act', lambda e: e.activation(out=mag[:], in_=lrs[:], func=AF.Exp), reads=[r_lrs], writes=[r_mag])
        P.op('act', lambda e: e.activation(out=s5['Rt'][:], in_=lrs[:], func=AF.Exp, scale=8.0), reads=[r_lrs], writes=[s5['r_Rt']])
        th, r_th = T()
        tt(th[:], r_th, aim[:], r_aim, step[:], r_step, ALU.mult)

        def sin_of(src, r_src, shift, name):
            a, r_a = T()
            ts(a[:], r_a, src[:], r_src, shift, 1.0 / TWO_PI, ALU.add, ALU.mult)
            ki, r_ki = tb(name + '_ki', [128, 8], mybir.dt.int32)
            P.op('dve', lambda e: e.tensor_copy(out=ki[:], in_=a[:]), reads=[r_a], writes=[r_ki])
            kf, r_kf = T()
            P.op('dve', lambda e: e.tensor_copy(out=kf[:], in_=ki[:]), reads=[r_ki], writes=[r_kf])
            fr, r_fr = T()
            tt(fr[:], r_fr, a[:], r_a, kf[:], r_kf, ALU.subtract)
            c1, r_c1 = T()
            ts(c1[:], r_c1, fr[:], r_fr, 0.5, None, ALU.is_gt)
            tt(fr[:], r_fr, fr[:], r_fr, c1[:], r_c1, ALU.subtract)
            ts(c1[:], r_c1, fr[:], r_fr, -0.5, None, ALU.is_lt)
            tt(fr[:], r_fr, fr[:], r_fr, c1[:], r_c1, ALU.add)
            ts(fr[:], r_fr, fr[:], r_fr, 0.5, -0.5, ALU.min, ALU.max)
            o, r_o = T()
            P.op('act', lambda e: e.activation(out=o[:], in_=fr[:], func=AF.Sin, scale=TWO_PI), reads=[r_fr], writes=[r_o])
            return o, r_o
        sn, r_sn = sin_of(th, r_th, 0.0, 'sn')
        cs_, r_cs = sin_of(th, r_th, TWO_PI / 4, 'cs')
        Ar, r_Ar = T()
        Ai, r_Ai = T()
        tt(Ar[:], r_Ar, mag[:], r_mag, cs_[:], r_cs, ALU.mult)
        tt(Ai[:], r_Ai, mag[:], r_mag, sn[:], r_sn, ALU.mult)
        den, r_den = T()
        t1, r_t1 = T()
        tt(den[:], r_den, lre[:], r_lre, lre[:], r_lre, ALU.mult)
        tt(t1[:], r_t1, aim[:], r_aim, aim[:], r_aim, ALU.mult)
        tt(den[:], r_den, den[:], r_den, t1[:], r_t1, ALU.add)
        P.op('dve', lambda e: e.reciprocal(out=den[:], in_=den[:]), reads=[r_den], writes=[r_den])
        nr, r_nr = T()
        ts(nr[:], r_nr, Ar[:], r_Ar, -1.0, None, ALU.add)
        cfr, r_cfr = T()
        cfi, r_cfi = T()
        t2, r_t2 = T()
        tt(cfr[:], r_cfr, nr[:], r_nr, lre[:], r_lre, ALU.mult)
        tt(t2[:], r_t2, Ai[:], r_Ai, aim[:], r_aim, ALU.mult)
        tt(cfr[:], r_cfr, cfr[:], r_cfr, t2[:], r_t2, ALU.add)
        tt(cfr[:], r_cfr, cfr[:], r_cfr, den[:], r_den, ALU.mult)
        tt(cfi[:], r_cfi, Ai[:], r_Ai, lre[:], r_lre, ALU.mult)
        tt(t2[:], r_t2, nr[:], r_nr, aim[:], r_aim, ALU.mult)
        tt(cfi[:], r_cfi, cfi[:], r_cfi, t2[:], r_t2, ALU.subtract)
        tt(cfi[:], r_cfi, cfi[:], r_cfi, den[:], r_den, ALU.mult)
        BBr, r_BBr = T([128, 8, 16])
        BBi, r_BBi = T([128, 8, 16])
        t16, r_t16 = T([128, 8, 16])
        bc16 = lambda a: a[:].unsqueeze(2).to_broadcast([128, 8, 16])

        def cmul(outr, r_outr, outi, r_outi, pr, r_pr, pi, r_pi, xr, r_xr, xi, r_xi, shape_bc, tmpt, r_tmpt, neg_im=False):
            tt(outr, r_outr, xr, r_xr, shape_bc(pr), r_pr, ALU.mult)
            tt(tmpt, r_tmpt, xi, r_xi, shape_bc(pi), r_pi, ALU.mult)
            tt(outr, r_outr, outr, r_outr, tmpt, r_tmpt, ALU.subtract)
            tt(outi, r_outi, xi, r_xi, shape_bc(pr), r_pr, ALU.mult)
            tt(tmpt, r_tmpt, xr, r_xr, shape_bc(pi), r_pi, ALU.mult)
            if neg_im:
                tt(outi, r_outi, outi, r_outi, tmpt, r_tmpt, ALU.add)
                ts(outi, r_outi, outi, r_outi, -1.0, None, ALU.mult)
            else:
                tt(outi, r_outi, outi, r_outi, tmpt, r_tmpt, ALU.add)
        cmul(BBr[:], r_BBr, BBi[:], r_BBi, cfr, r_cfr, cfi, r_cfi, bre[:], r_bre, bim[:], r_bim, bc16, t16[:], r_t16)
        Pr, r_Pr = T([128, 8, 9])
        Pi, r_Pi = T([128, 8, 9])
        P.op('dve', lambda e: e.memset(Pr[:, :, 0:1], 1.0), writes=[r_Pr])
        P.op('dve', lambda e: e.memset(Pi[:, :, 0:1], 0.0), writes=[r_Pi])
        t8, r_t8 = T()
        for j in range(1, 9):
            tt(Pr[:, :, j], r_Pr, Pr[:, :, j - 1], r_Pr, Ar[:], r_Ar, ALU.mult)
            tt(t8[:], r_t8, Pi[:, :, j - 1], r_Pi, Ai[:], r_Ai, ALU.mult)
            tt(Pr[:, :, j], r_Pr, Pr[:, :, j], r_Pr, t8[:], r_t8, ALU.subtract)
            tt(Pi[:, :, j], r_Pi, Pi[:, :, j - 1], r_Pi, Ar[:], r_Ar, ALU.mult)
            tt(t8[:], r_t8, Pr[:, :, j - 1], r_Pr, Ai[:], r_Ai, ALU.mult)
            tt(Pi[:, :, j], r_Pi, Pi[:, :, j], r_Pi, t8[:], r_t8, ALU.add)
        XSr, r_XSr = T([128, 8, 8, 32])
        XSi, r_XSi = T([128, 8, 8, 32])
        CSr, r_CSr = T([128, 8, 32])
        CSi, r_CSi = T([128, 8, 32])
        for (t, r) in ((XSr, r_XSr), (XSi, r_XSi), (CSr, r_CSr), (CSi, r_CSi)):
            P.op('pool', lambda e, t=t: e.memset(t[:], 0.0), writes=[r])
        P.op('pool', lambda e: e.memset(T3[:], 0.0), writes=[r_T3])
        Vr, r_Vr = T([128, 8, 16])
        Vi, r_Vi = T([128, 8, 16])
        for dlt in range(8):
            pcr = lambda a, dlt=dlt: a[:, :, dlt:dlt + 1].to_broadcast([128, 8, 16])
            cmul(Vr[:], r_Vr, Vi[:], r_Vi, Pr, r_Pr, Pi, r_Pi, BBr[:], r_BBr, BBi[:], r_BBi, pcr, t16[:], r_t16)
            for (V, r_V, XS, r_XS) in ((Vr, r_Vr, XSr, r_XSr), (Vi, r_Vi, XSi, r_XSi)):
                P.op('dve', lambda e, V=V, XS=XS, dlt=dlt: e.tensor_copy(out=XS[0:64, dlt, :, 0:16], in_=V[0:64, :, :]), reads=[r_V], writes=[r_XS])
                P.op('dve', lambda e, V=V, XS=XS, dlt=dlt: e.tensor_copy(out=XS[64:128, dlt, :, 16:32], in_=V[64:128, :, :]), reads=[r_V], writes=[r_XS])
        P.op('dve', lambda e: e.tensor_copy(out=CSr[0:64, :, 0:16], in_=cre[0:64, :, :]), reads=[r_cre], writes=[r_CSr])
        P.op('dve', lambda e: e.tensor_copy(out=CSr[64:128, :, 16:32], in_=cre[64:128, :, :]), reads=[r_cre], writes=[r_CSr])
        ts(CSi[0:64, :, 0:16], r_CSi, cim[0:64, :, :], r_cim, -1.0, None, ALU.mult)
        ts(CSi[64:128, :, 16:32], r_CSi, cim[64:128, :, :], r_cim, -1.0, None, ALU.mult)
        T1f, r_T1f = T([128, 128])
        dgt, r_dgt = T([128, 128])
        for ct in range(2):
            for dlt in range(8):
                pst, r_ps = bank()
                P.op('pool', lambda e: e.memset(T1f[:], 0.0), writes=[r_T1f])
                for q in range(4):
                    gp = 4 * ct + q
                    P.op('pe', lambda e, pst=pst, q=q, gp=gp, dlt=dlt: e.matmul(pst[32 * q:32 * q + 32, 32 * q:32 * q + 32], lhsT=XSr[:, dlt, gp, :], rhs=CSr[:, gp, :],
                                                                             start=True, stop=False, skip_group_check=True, tile_position=(0, 32 * q)), reads=[r_XSr, r_CSr], writes=[r_ps])
                    P.op('pe', lambda e, pst=pst, q=q, gp=gp, dlt=dlt: e.matmul(pst[32 * q:32 * q + 32, 32 * q:32 * q + 32], lhsT=XSi[:, dlt, gp, :], rhs=CSi[:, gp, :],
                                                                             start=False, stop=True, skip_group_check=True, tile_position=(0, 32 * q)), reads=[r_XSi, r_CSi], writes=[r_ps])
                for q in range(4):
                    P.op('dve', lambda e, pst=pst, q=q: e.tensor_copy(out=T1f[32 * q:32 * q + 32, 32 * q:32 * q + 32], in_=pst[32 * q:32 * q + 32, 32 * q:32 * q + 32]),
                         reads=[r_ps], writes=[r_T1f])
                if dlt == 0:
                    P.op('dve', lambda e, ct=ct: e.tensor_scalar(out=dgt[:], in0=ident_f[:], scalar1=dcol[:, ct:ct + 1], scalar2=None, op0=ALU.mult),
                         reads=[r_ident_f, r_dcol], writes=[r_dgt])
                    tt(T1f[:], r_T1f, T1f[:], r_T1f, dgt[:], r_dgt, ALU.add)
                P.op('act', lambda e, ct=ct, dlt=dlt: e.copy(out=T1[:, ct, dlt, :], in_=T1f[:]), reads=[r_T1f], writes=[r_T1])
        for ct in range(2):
            for sg_ in range(8):
                dlt = 7 - sg_
                for (XS, r_XS, T2, r_T2) in ((XSr, r_XSr, T2re, r_T2re), (XSi, r_XSi, T2im, r_T2im)):
                    pst, r_ps = bank()
                    P.op('pe', lambda e, pst=pst, XS=XS, dlt=dlt, ct=ct: e.transpose(out=pst[:, 0:128], in_=XS[:, dlt, 4 * ct:4 * ct + 4, :].rearrange("p a b -> p (a b)"),
                                                                                     identity=ident_f[:]), reads=[r_XS, r_ident_f], writes=[r_ps])
                    P.op('act', lambda e, pst=pst, T2=T2, ct=ct, sg_=sg_: e.copy(out=T2[:, ct, sg_, :], in_=pst[:, 0:128]), reads=[r_ps], writes=[r_T2])
        Fr, r_Fr = T([128, 8, 16])
        Fi, r_Fi = T([128, 8, 16])
        for tau in range(8):
            pcr = lambda a, tau=tau: a[:, :, tau + 1:tau + 2].to_broadcast([128, 8, 16])
            cmul(Fr[:], r_Fr, Fi[:], r_Fi, Pr, r_Pr, Pi, r_Pi, cre[:], r_cre, cim[:], r_cim, pcr, t16[:], r_t16, neg_im=True)
            for (Fx, r_Fx, part) in ((Fr, r_Fr, 0), (Fi, r_Fi, 1)):
                P.op('dve', lambda e, Fx=Fx, tau=tau, part=part: e.tensor_copy(out=T3[0:64, :, tau, part, 0:16], in_=Fx[0:64, :, :]), reads=[r_Fx], writes=[r_T3])
                P.op('dve', lambda e, Fx=Fx, tau=tau, part=part: e.tensor_copy(out=T3[64:128, :, tau, part, 16:32], in_=Fx[64:128, :, :]), reads=[r_Fx], writes=[r_T3])
        rinv, r_rinv = T()
        P.op('dve', lambda e: e.reciprocal(out=rinv[:], in_=s5['Rt'][:]), reads=[s5['r_Rt']], writes=[r_rinv])
        tt(UC[:, :, 0], r_UC, Pr[:, :, 8], r_Pr, rinv[:], r_rinv, ALU.mult)
        tt(US[:, :, 0], r_US, Pi[:, :, 8], r_Pi, rinv[:], r_rinv, ALU.mult)
        tw, r_tw = T([128, 8, 32])
        n = 1
        while n < 64:
            bcn = lambda a, n=n: a[:, :, n - 1:n].to_broadcast([128, 8, n])
            cmul(UC[:, :, n:2 * n], r_UC, US[:, :, n:2 * n], r_US, UC, r_UC, US, r_US, UC[:, :, 0:n], r_UC, US[:, :, 0:n], r_US, bcn, tw[:, :, 0:n], r_tw)
            n *= 2
        if dbg_d is not None:
            t1d, r_t1d = T([128, 2 * 8 * 128])
            P.op('dve', lambda e: e.tensor_copy(out=t1d[:], in_=T1[:].rearrange("p a b c -> p (a b c)")), reads=[r_T1], writes=[r_t1d])
            P.dma('sp', dbg_d['t1'], t1d[:], reads=[r_t1d], writes=[dbg_res])
        P.barrier()
    return s5


def s5_macro(nc, P, K, s5, suT, r_suT, mixBT, r_mixBT, oncol, r_oncol, banks, bank, ones_b, r_ones_b, eps_t, r_eps, m, dbg_d, dbg_res):
    g = lambda k: (s5[k], s5['r_' + k])
    T1, r_T1 = g('T1')
    T2re, r_T2re = g('T2re')
    T2im, r_T2im = g('T2im')
    T3, r_T3 = g('T3')
    UC, r_UC = g('UC')
    US, r_US = g('US')
    Rt, r_Rt = g('Rt')
    G0re, r_G0re = g('G0re')
    G0im, r_G0im = g('G0im')
    Wre, r_Wre = g('Wre')
    Wim, r_Wim = g('Wim')
    Gre, r_Gre = g('Gre')
    Gim, r_Gim = g('Gim')
    Hre, r_Hre = g('Hre')
    Him, r_Him = g('Him')
    tA, r_tA = g('tA')
    tB, r_tB = g('tB')
    Hbre, r_Hbre = g('Hbre')
    Hbim, r_Hbim = g('Hbim')
    hg, r_hg = g('hg')
    hgb, r_hgb = g('hgb')
    sig, r_sig = g('sig')
    ob, r_ob = g('ob')
    sq, r_sq = g('sq')
    rsb, r_rsb = g('rsb')
    wglu, r_wglu = g('wglu')
    bglu, r_bglu = g('bglu')

    def tt(out, o_r, a, a_r, b, b_r, op, eng='dve'):
        P.op(eng, lambda e: e.tensor_tensor(out=out, in0=a, in1=b, op=op), reads=[a_r, b_r], writes=[o_r])
    def tt4(out, o_r, eb, other, r_other, op, q):
        pst, r_ps = eb[q]
        P.op('dve', lambda e: e.tensor_tensor(out=out[:, q:8:4, :], in0=pst[:, 0:128].rearrange("p (c k) -> p c k", c=2),
                                              in1=other[:, q:8:4, :], op=op), reads=[r_ps, r_other], writes=[o_r])
    for part, (T2, r_T2) in enumerate(((T2re, r_T2re), (T2im, r_T2im))):
        eb = [banks[3 + q] for q in range(4)]
        for q in range(4):
            pst, r_ps = eb[q]
            for ct in range(2):
                for sg_ in range(8):
                    P.op('pe', lambda e, pst=pst, T2=T2, ct=ct, q=q, sg_=sg_: e.matmul(
                        pst[:, ct * 64:(ct + 1) * 64], lhsT=T2[32 * q:32 * q + 32, ct, sg_, :], rhs=suT[32 * q:32 * q + 32, ct, sg_ * 64:(sg_ + 1) * 64],
                        start=(sg_ == 0), stop=(sg_ == 7), skip_group_check=True, tile_position=(32 * q, 0)), reads=[r_T2, r_suT], writes=[r_ps])
        for q in range(4):
            if part == 0:
                tt4(Wre, r_Wre, eb, UC, r_UC, ALU.mult, q)
                tt4(tB, r_tB, eb, US, r_US, ALU.mult, q)
            else:
                tt4(tA, r_tA, eb, US, r_US, ALU.mult, q)
                tt4(Wim, r_Wim, eb, UC, r_UC, ALU.mult, q)
    tt(Wre[:], r_Wre, Wre[:], r_Wre, tA[:], r_tA, ALU.add)
    tt(Wim[:], r_Wim, Wim[:], r_Wim, tB[:], r_tB, ALU.subtract)
    P.op('act', lambda e: e.copy(out=Hbre[:, :, 0], in_=G0re[:]), reads=[r_G0re], writes=[r_Hbre])
    P.op('act', lambda e: e.copy(out=Hbim[:, :, 0], in_=G0im[:]), reads=[r_G0im], writes=[r_Hbim])
    for gp in range(8):
        P.op('dve', lambda e, gp=gp: e.tensor_tensor_scan(out=Gre[:, gp, :], data0=Rt[:, gp:gp + 1].to_broadcast([128, 64]), data1=Wre[:, gp, :],
                                                          initial=G0re[:, gp:gp + 1], op0=ALU.mult, op1=ALU.add),
             reads=[r_Rt, r_Wre, r_G0re], writes=[r_Gre])
        P.op('dve', lambda e, gp=gp: e.tensor_tensor_scan(out=Gim[:, gp, :], data0=Rt[:, gp:gp + 1].to_broadcast([128, 64]), data1=Wim[:, gp, :],
                                                          initial=G0im[:, gp:gp + 1], op0=ALU.mult, op1=ALU.add),
             reads=[r_Rt, r_Wim, r_G0im], writes=[r_Gim])
    tt(Hre[:], r_Hre, Gre[:], r_Gre, UC[:], r_UC, ALU.mult)
    tt(tA[:], r_tA, Gim[:], r_Gim, US[:], r_US, ALU.mult)
    tt(Hre[:], r_Hre, Hre[:], r_Hre, tA[:], r_tA, ALU.subtract)
    tt(Him[:], r_Him, Gim[:], r_Gim, UC[:], r_UC, ALU.mult)
    tt(tB[:], r_tB, Gre[:], r_Gre, US[:], r_US, ALU.mult)
    tt(Him[:], r_Him, Him[:], r_Him, tB[:], r_tB, ALU.add)
    P.op('act', lambda e: e.copy(out=Hbre[:, :, 1:65], in_=Hre[:]), reads=[r_Hre], writes=[r_Hbre])
    P.op('act', lambda e: e.copy(out=Hbim[:, :, 1:65], in_=Him[:]), reads=[r_Him], writes=[r_Hbim])
    P.op('dve', lambda e: e.tensor_copy(out=G0re[:], in_=Hre[:, :, 63]), reads=[r_Hre], writes=[r_G0re])
    P.op('dve', lambda e: e.tensor_copy(out=G0im[:], in_=Him[:, :, 63]), reads=[r_Him], writes=[r_G0im])
    pY = [bank(), bank()]
    for ct in range(2):
        pst, r_ps = pY[ct]
        for dlt in range(8):
            P.op('pe', lambda e, pst=pst, dlt=dlt, ct=ct: e.matmul(
                pst[:, dlt * 64:512], lhsT=T1[:, ct, dlt, :], rhs=suT[:, ct, 0:(8 - dlt) * 64],
                start=(dlt == 0), stop=False, skip_group_check=True), reads=[r_T1, r_suT], writes=[r_ps])
        for tau in range(8):
            for q in range(4):
                gp = 4 * ct + q
                for part, (Hb, r_Hb) in enumerate(((Hbre, r_Hbre), (Hbim, r_Hbim))):
                    last = (tau == 7 and q == 3 and part == 1)
                    P.op('pe', lambda e, pst=pst, tau=tau, q=q, gp=gp, part=part, Hb=Hb, last=last: e.matmul(
                        pst[32 * q:32 * q + 32, tau * 64:(tau + 1) * 64], lhsT=T3[:, gp, tau, part, :], rhs=Hb[:, gp, 0:64],
                        start=False, stop=last, skip_group_check=True, tile_position=(0, 32 * q)), reads=[r_T3, r_Hb], writes=[r_ps])
    if dbg_d is not None:
        for ct in range(2):
            P.op('act', lambda e, ct=ct: e.copy(out=hg[:, ct, :], in_=pY[ct][0][:, :]), reads=[pY[ct][1]], writes=[r_hg])
            P.dma('sp', dbg_d['s5y'][ct * 128:(ct + 1) * 128, m * 512:(m + 1) * 512], hg[:, ct, :], reads=[r_hg], writes=[dbg_res])
    for ct in range(2):
        P.op('act', lambda e, ct=ct: e.activation(out=hg[:, ct, :], in_=pY[ct][0][:, :], func=AF.Gelu_apprx_tanh), reads=[pY[ct][1]], writes=[r_hg])
        P.op('dve', lambda e, ct=ct: e.tensor_copy(out=hgb[:, ct, :], in_=hg[:, ct, :]), reads=[r_hg], writes=[r_hgb])
    for c2 in range(2):
        pst, r_ps = bank()
        for ct in range(2):
            P.op('pe', lambda e, pst=pst, ct=ct, c2=c2: e.matmul(pst[:, :], lhsT=wglu[:, ct, c2 * 128:(c2 + 1) * 128], rhs=hgb[:, ct, :],
                                                               start=(ct == 0), stop=(ct == 1)), reads=[r_wglu, r_hgb], writes=[r_ps])
        P.op('act', lambda e, pst=pst, c2=c2: e.activation(out=sig[:], in_=pst[:, :], func=AF.Sigmoid, bias=bglu[:, c2:c2 + 1], scale=1.0),
             reads=[r_ps, r_bglu], writes=[r_sig])
        tt(ob[:, c2, :], r_ob, hg[:, c2, :], r_hg, sig[:], r_sig, ALU.mult)
        P.op('act', lambda e, c2=c2: e.activation(out=sq[:, c2, :], in_=ob[:, c2, :], func=AF.Square), reads=[r_ob], writes=[r_sq])
    if dbg_d is not None:
        for ct in range(2):
            P.dma('sp', dbg_d['obT'][ct * 128:(ct + 1) * 128, m * 512:(m + 1) * 512], ob[:, ct, :], reads=[r_ob], writes=[dbg_res])
    pst, r_ps = bank()
    for ct in range(2):
        P.op('pe', lambda e, pst=pst, ct=ct: e.matmul(pst[:, :], lhsT=ones_b[:], rhs=sq[:, ct, :], start=(ct == 0), stop=(ct == 1)),
             reads=[r_ones_b, r_sq], writes=[r_ps])
    P.op('act', lambda e, pst=pst: e.activation(out=rsb[:], in_=pst[:, :], func=AF.Sqrt, bias=eps_t[:], scale=1.0 / 256), reads=[r_ps, r_eps], writes=[r_rsb])
    P.op('dve', lambda e: e.reciprocal(out=rsb[:], in_=rsb[:]), reads=[r_rsb], writes=[r_rsb])
    for ct in range(2):
        P.op('dve', lambda e, ct=ct: e.scalar_tensor_tensor(out=mixBT[:, ct, :].rearrange("p (k t) -> p t k", t=8),
                                                            in0=ob[:, ct, :].rearrange("p (t k) -> p t k", t=8), scalar=oncol[:, 3 + ct:4 + ct],
                                                            in1=rsb[:].rearrange("p (t k) -> p t k", t=8),
                                                            op0=ALU.mult, op1=ALU.mult), reads=[r_ob, r_oncol, r_rsb], writes=[r_mixBT])


_CACHE = {}


def kernel(**inputs):
    shared = _prep_shared(inputs)
    x = np.ascontiguousarray(np.asarray(inputs['x'], dtype=np.float32))
    c = np.asarray(inputs['c'], dtype=np.float32)
    key = 'nc'
    if key not in _CACHE:
        _CACHE[key] = build({k: v.shape for k, v in shared.items()})
    nc = _CACHE[key]
    in_maps = []
    for b in range(8):
        d = dict(shared)
        d['x'] = x[b]
        d['ccol'] = _col(c[b])
        in_maps.append(d)
    res = run_bass_kernel_spmd(nc, in_maps, core_ids=list(range(8)))
    return np.stack([np.asarray(r['out'], dtype=np.float32) for r in res.results], axis=0)
```
